# Optimizing a Trainium2 kernel written in Bass

```python
import numpy as np
import jax
import jax.numpy as jnp
from jax import lax

D_MODEL = 2048
BATCH = 4
SEQ = 4096
DEPTH = 4

GRID_W = 64
CTX_LEN = 256
HEAD_DIM = 128
N_HEADS = D_MODEL // HEAD_DIM
NA_HEADS = N_HEADS // 2
GLA_HEADS = N_HEADS - NA_HEADS
NA_ROWS_MAX = 8
NA_COLS = 16
GLA_DK = HEAD_DIM // 2
GLA_DV = HEAD_DIM
GLA_GATE_RANK = 16
GLA_TAU = 16.0
GLA_CHUNK = 64
POOL_WINDOWS = (2, 4, 8, 16)
POOL_GROUP = D_MODEL // 4
D_FF = 5632
ROPE_BASE = 10000.0
EPS = 1e-6
N_EVEN = (DEPTH + 1) // 2
N_ODD = DEPTH // 2
A_W = NA_HEADS * HEAD_DIM
BQK_W = GLA_HEADS * GLA_DK
BV_W = GLA_HEADS * GLA_DV
IN_SPLITS = (A_W, A_W, A_W, BQK_W, BQK_W, BV_W, BV_W, 2 * GLA_GATE_RANK)
D_IN = 3 * A_W + 2 * BQK_W + 2 * BV_W + 2 * GLA_GATE_RANK

kernel_name = 'hybrid_natten_gla_pool_dit'


def _rmsnorm(x, g):
    xf = x.astype(jnp.float32)
    y = xf * lax.rsqrt(jnp.mean(xf * xf, axis=-1, keepdims=True) + EPS)
    return (y * g.astype(jnp.float32)).astype(x.dtype)


def _modulate(h, shift, scale):
    return h * (1 + scale) + shift


def _split_heads(a, n):
    b, l, _ = a.shape
    return a.reshape(b, l, n, -1).transpose(0, 2, 1, 3)


def _merge_heads(a):
    b, h, l, d = a.shape
    return a.transpose(0, 2, 1, 3).reshape(b, l, h * d)


def _flip(a):
    return a[:, :, ::-1]


def _axial_rope(x, seq_len):
    t = jnp.arange(seq_len)
    row = (t // GRID_W).astype(jnp.float32)
    col = (t % GRID_W).astype(jnp.float32)
    half = x.shape[-1] // 2
    nf = half // 2
    inv = ROPE_BASE ** (-jnp.arange(nf, dtype=jnp.float32) / nf)

    def rot(xh, pos):
        ang = pos[:, None] * inv[None, :]
        cos = jnp.cos(ang).astype(xh.dtype)
        sin = jnp.sin(ang).astype(xh.dtype)
        x1, x2 = xh[..., :nf], xh[..., nf:]
        return jnp.concatenate([x1 * cos - x2 * sin, x2 * cos + x1 * sin], axis=-1)

    return jnp.concatenate([rot(x[..., :half], row), rot(x[..., half:], col)], axis=-1)


def _dense_attention(q, k, v):
    s = jnp.einsum('bhqd,bhkd->bhqk', q, k).astype(jnp.float32) * (q.shape[-1] ** -0.5)
    p = jax.nn.softmax(s, axis=-1).astype(v.dtype)
    return jnp.einsum('bhqk,bhkd->bhqd', p, v)


def _neighbourhood_attention(q, k, v, k_ctx, v_ctx, rpb):
    b, h, seq_len, dh = q.shape
    rows = seq_len // GRID_W
    wr = min(NA_ROWS_MAX, rows)
    band = wr * GRID_W
    scale = dh ** -0.5
    qcol = jnp.arange(GRID_W)
    kcol = jnp.arange(GRID_W)
    cstart = jnp.clip(qcol - NA_COLS // 2, 0, GRID_W - NA_COLS)
    col_in = (kcol[None, :] >= cstart[:, None]) & (kcol[None, :] < cstart[:, None] + NA_COLS)
    mask = jnp.broadcast_to(col_in[:, None, :], (GRID_W, wr, GRID_W)).reshape(GRID_W, band)
    dc_idx = jnp.clip(kcol[None, :] - qcol[:, None] + NA_COLS - 1, 0, 2 * NA_COLS - 2)
    col_bias = rpb[:, :, dc_idx]

    def row_block(r):
        rstart = jnp.clip(r - wr // 2, 0, rows - wr)
        qr = lax.dynamic_slice_in_dim(q, r * GRID_W, GRID_W, axis=2)
        kb = lax.dynamic_slice_in_dim(k, rstart * GRID_W, band, axis=2)
        vb = lax.dynamic_slice_in_dim(v, rstart * GRID_W, band, axis=2)
        dr_idx = rstart + jnp.arange(wr) - r + NA_ROWS_MAX - 1
        bias = col_bias[:, dr_idx].transpose(0, 2, 1, 3).reshape(h, GRID_W, band)
        s_loc = jnp.einsum('bhqd,bhkd->bhqk', qr, kb).astype(jnp.float32) * scale + bias[None].astype(jnp.float32)
        s_loc = jnp.where(mask, s_loc, -jnp.inf)
        s_ctx = jnp.einsum('bhqd,bhkd->bhqk', qr, k_ctx).astype(jnp.float32) * scale
        p = jax.nn.softmax(jnp.concatenate([s_loc, s_ctx], axis=-1), axis=-1).astype(v.dtype)
        return (jnp.einsum('bhqk,bhkd->bhqd', p[..., :band], vb)
                + jnp.einsum('bhqk,bhkd->bhqd', p[..., band:], v_ctx))

    out = lax.map(row_block, jnp.arange(rows))
    return out.transpose(1, 2, 0, 3, 4).reshape(b, h, seq_len, dh)


def _gla_chunked(q, k, v, g, s0):
    b, h, seq_len, dk = q.shape
    dv = v.shape[-1]
    n = seq_len // GLA_CHUNK
    out_dtype = v.dtype

    def to_chunks(a):
        return a.astype(jnp.float32).reshape(b, h, n, GLA_CHUNK, a.shape[-1]).transpose(2, 0, 1, 3, 4)

    lower = jnp.tril(jnp.ones((GLA_CHUNK, GLA_CHUNK), dtype=bool))

    def step(s, inp):
        qc, kc, vc, gc = inp
        cum = jnp.cumsum(gc, axis=2)
        o_inter = jnp.einsum('bhcd,bhde->bhce', qc * jnp.exp(cum), s)
        diff = cum[:, :, :, None, :] - cum[:, :, None, :, :]
        decay = jnp.where(lower[:, :, None], jnp.exp(jnp.minimum(diff, 0.0)), 0.0)
        attn = jnp.einsum('bhid,bhjd,bhijd->bhij', qc, kc, decay)
        o_intra = jnp.einsum('bhij,bhje->bhie', attn, vc)
        last = cum[:, :, -1:, :]
        s_new = jnp.exp(last[:, :, 0, :])[..., None] * s + jnp.einsum('bhcd,bhce->bhde', kc * jnp.exp(last - cum), vc)
        return s_new, o_inter + o_intra

    s_fin, o = lax.scan(step, s0, (to_chunks(q), to_chunks(k), to_chunks(v), to_chunks(g)))
    o = o.transpose(1, 2, 0, 3, 4).reshape(b, h, seq_len, dv)
    return o.astype(out_dtype), s_fin


def _gla_final_state(k, v, g):
    cum = jnp.cumsum(g, axis=2)
    return jnp.einsum('bhld,bhle->bhde', k.astype(jnp.float32) * jnp.exp(cum[:, :, -1:] - cum), v.astype(jnp.float32))


def _project_even(h, w_in, w_gate2, b_gate):
    p = h @ w_in
    offs = np.cumsum(IN_SPLITS)[:-1].tolist()
    qa, ka, va, qb, kb, vb, gb, ab = jnp.split(p, offs, axis=-1)
    log_gates = []
    for d in range(2):
        z = ab[..., d * GLA_GATE_RANK:(d + 1) * GLA_GATE_RANK] @ w_gate2[d] + b_gate[d]
        log_gates.append(_split_heads(jax.nn.log_sigmoid(z.astype(jnp.float32)) / GLA_TAU, GLA_HEADS))
    return (_split_heads(qa, NA_HEADS), _split_heads(ka, NA_HEADS), _split_heads(va, NA_HEADS),
            _split_heads(qb, GLA_HEADS), _split_heads(kb, GLA_HEADS), _split_heads(vb, GLA_HEADS),
            gb, log_gates[0], log_gates[1])


def _combine(oa, ob, gb, gla_g, w_out):
    ob = _rmsnorm(ob, gla_g[:, None, :])
    ob = _merge_heads(ob) * jax.nn.silu(gb)
    return jnp.concatenate([_merge_heads(oa), ob], axis=-1) @ w_out


def _even_mixer(h_lat, h_ctx, w_in, w_gate2, b_gate, rpb, gla_g, w_out, need_ctx):
    qa, ka, va, qb, kb, vb, gb, lgf, lgb = _project_even(h_lat, w_in, w_gate2, b_gate)
    qa_c, ka_c, va_c, qb_c, kb_c, vb_c, gb_c, lgf_c, lgb_c = _project_even(h_ctx, w_in, w_gate2, b_gate)
    seq_len = h_lat.shape[1]
    bsz = h_lat.shape[0]
    oa = _neighbourhood_attention(qa, ka, va, ka_c, va_c, rpb)
    qs = GLA_DK ** -0.5
    qb_r = _axial_rope(qb, seq_len) * qs
    kb_r = _axial_rope(kb, seq_len)
    s0 = jnp.zeros((bsz, GLA_HEADS, GLA_DK, GLA_DV), jnp.float32)
    if need_ctx:
        ob_cf, s_f = _gla_chunked(qb_c * qs, kb_c, vb_c, lgf_c, s0)
        ob_cb, s_b = _gla_chunked(_flip(qb_c * qs), _flip(kb_c), _flip(vb_c), _flip(lgb_c), s0)
    else:
        s_f = _gla_final_state(kb_c, vb_c, lgf_c)
        s_b = _gla_final_state(_flip(kb_c), _flip(vb_c), _flip(lgb_c))
    ob_f, _ = _gla_chunked(qb_r, kb_r, vb, lgf, s_f)
    ob_b, _ = _gla_chunked(_flip(qb_r), _flip(kb_r), _flip(vb), _flip(lgb), s_b)
    y_lat = _combine(oa, ob_f + _flip(ob_b), gb, gla_g, w_out)
    y_ctx = None
    if need_ctx:
        oa_c = _dense_attention(qa_c, ka_c, va_c)
        y_ctx = _combine(oa_c, ob_cf + _flip(ob_cb), gb_c, gla_g, w_out)
    return y_lat, y_ctx


def _pool_mix(h, w_pool, pool_scale):
    b, seq_len, d = h.shape
    hf = h.astype(jnp.float32)
    cs = jnp.concatenate([jnp.zeros((b, 1, d), jnp.float32), jnp.cumsum(hf, axis=1)], axis=1)
    t = jnp.arange(seq_len)
    groups = []
    for gi, w in enumerate(POOL_WINDOWS):
        lo = jnp.clip(t - w // 2, 0, seq_len)
        hi = jnp.clip(t + w // 2, 0, seq_len)
        sl = slice(gi * POOL_GROUP, (gi + 1) * POOL_GROUP)
        csg = cs[:, :, sl]
        cnt = (hi - lo).astype(jnp.float32)[None, :, None]
        groups.append((csg[:, hi] - csg[:, lo]) / cnt - hf[:, :, sl])
    pooled = jnp.stack(groups, axis=2).astype(h.dtype)
    y = jnp.einsum('blgc,gcd->blgd', pooled, w_pool).reshape(b, seq_len, d)
    return y * pool_scale


def _conv_ffn(h, w_up, conv_w, conv_b, w_down):
    u = h @ w_up
    val, gate = jnp.split(u, 2, axis=-1)
    gp = jnp.pad(gate, ((0, 0), (1, 1), (0, 0)))
    gate = gp[:, :-2] * conv_w[0] + gp[:, 1:-1] * conv_w[1] + gp[:, 2:] * conv_w[2] + conv_b
    return (jax.nn.gelu(gate, approximate=False) * val) @ w_down


def setup_inputs(seed: int = 0) -> dict:
    key = jax.random.key(seed)
    ks = jax.random.split(key, 24)
    f32 = jnp.float32

    def nrm(k, shape, scale):
        return jax.random.normal(k, shape, f32) * scale

    d = D_MODEL
    return {
        'x': nrm(ks[0], (BATCH, SEQ, d), 1.0),
        'c': nrm(ks[1], (BATCH, d), 1.0),
        'ctx': nrm(ks[2], (BATCH, CTX_LEN, d), 1.0),
        'c_ctx': nrm(ks[3], (d,), 1.0),
        'w_mod': nrm(ks[4], (DEPTH, d, 6 * d), 0.5 * d ** -0.5),
        'b_mod': nrm(ks[5], (DEPTH, 6 * d), 0.01),
        'norm1_g': 1.0 + nrm(ks[6], (DEPTH, d), 0.01),
        'norm2_g': 1.0 + nrm(ks[7], (DEPTH, d), 0.01),
        'w_in': nrm(ks[8], (N_EVEN, d, D_IN), d ** -0.5),
        'w_gate2': nrm(ks[9], (N_EVEN, 2, GLA_GATE_RANK, BQK_W), GLA_GATE_RANK ** -0.5),
        'b_gate': nrm(ks[10], (N_EVEN, 2, BQK_W), 0.1),
        'rpb': nrm(ks[11], (N_EVEN, NA_HEADS, 2 * NA_ROWS_MAX - 1, 2 * NA_COLS - 1), 0.1),
        'gla_norm_g': 1.0 + nrm(ks[12], (N_EVEN, GLA_HEADS, GLA_DV), 0.01),
        'w_out': nrm(ks[13], (N_EVEN, A_W + BV_W, d), (A_W + BV_W) ** -0.5),
        'pool_w': nrm(ks[14], (N_ODD, 4, POOL_GROUP, POOL_GROUP), POOL_GROUP ** -0.5),
        'pool_scale': 1.0 + nrm(ks[15], (N_ODD, d), 0.05),
        'w_up': nrm(ks[16], (DEPTH, d, 2 * D_FF), d ** -0.5),
        'conv_w': nrm(ks[17], (DEPTH, 3, D_FF), 3 ** -0.5),
        'conv_b': nrm(ks[18], (DEPTH, D_FF), 0.01),
        'w_down': nrm(ks[19], (DEPTH, D_FF, d), D_FF ** -0.5),
        'final_g': 1.0 + nrm(ks[20], (d,), 0.01),
    }


def reference(x, c, ctx, c_ctx, w_mod, b_mod, norm1_g, norm2_g, w_in, w_gate2, b_gate, rpb,
              gla_norm_g, w_out, pool_w, pool_scale, w_up, conv_w, conv_b, w_down, final_g):
    silu_c = jax.nn.silu(c)
    silu_cc = jax.nn.silu(c_ctx)
    x_lat = x
    x_ctx = ctx
    for i in range(DEPTH):
        is_even = i % 2 == 0
        need_ctx = any(j % 2 == 0 for j in range(i + 1, DEPTH))
        ctx_in = is_even or need_ctx
        sh1, sc1, g1, sh2, sc2, g2 = [m[:, None, :] for m in jnp.split(silu_c @ w_mod[i] + b_mod[i], 6, axis=-1)]
        h_lat = _modulate(_rmsnorm(x_lat, norm1_g[i]), sh1, sc1)
        if ctx_in:
            shc1, scc1, gc1, shc2, scc2, gc2 = jnp.split(silu_cc @ w_mod[i] + b_mod[i], 6, axis=-1)
            h_ctx = _modulate(_rmsnorm(x_ctx, norm1_g[i]), shc1, scc1)
        else:
            h_ctx = None
        if is_even:
            e = i // 2
            y_lat, y_ctx = _even_mixer(h_lat, h_ctx, w_in[e], w_gate2[e], b_gate[e], rpb[e],
                                       gla_norm_g[e], w_out[e], need_ctx)
        else:
            o = i // 2
            y_lat = _pool_mix(h_lat, pool_w[o], pool_scale[o])
            y_ctx = _pool_mix(h_ctx, pool_w[o], pool_scale[o]) if need_ctx else None
        x_lat = x_lat + g1 * y_lat
        x_lat = x_lat + g2 * _conv_ffn(_modulate(_rmsnorm(x_lat, norm2_g[i]), sh2, sc2),
                                       w_up[i], conv_w[i], conv_b[i], w_down[i])
        if need_ctx:
            x_ctx = x_ctx + gc1 * y_ctx
            x_ctx = x_ctx + gc2 * _conv_ffn(_modulate(_rmsnorm(x_ctx, norm2_g[i]), shc2, scc2),
                                            w_up[i], conv_w[i], conv_b[i], w_down[i])
    return _rmsnorm(x_lat, final_g)
```

```python
import contextlib
import numpy as np
import concourse.bass as bass
import concourse.mybir as mybir
from concourse.bass_utils import run_bass_kernel_spmd

F32, BF16 = mybir.dt.float32, mybir.dt.bfloat16
AF = mybir.ActivationFunctionType
ALU = mybir.AluOpType
ND = 6
EPS = 1e-6


class Cfg:
    def __init__(s, D=2048, SEQ=4096, CTX=256, DEPTH=4, DFF=5632, TBF=2048, B=4, TBE=1024):
        s.D, s.SEQ, s.CTX, s.DEPTH, s.DFF, s.TBF, s.B = D, SEQ, CTX, DEPTH, DFF, TBF, B
        s.KC = D // 128
        s.TBE = TBE
        s.NH = D // 128
        s.NAH = s.NH // 2
        s.GH = s.NH - s.NAH
        s.AW, s.BQK, s.BV = s.NAH * 128, s.GH * 64, s.GH * 128
        s.DIN = 3 * s.AW + 2 * s.BQK + 2 * s.BV + 32
        s.ROWS = SEQ // 64
        s.PG = D // 4
        s.FC = DFF // 128
        s.NE = (DEPTH + 1) // 2
        s.NO = DEPTH // 2
        s.NBLK = 23


def _split(a, b, mx=256):
    n = b - a
    k = (n + mx - 1) // mx
    out, t = [], a
    for i in range(k):
        m = n // k + (1 if i < n % k else 0)
        out.append((t, m))
        t += m
    return out


class Tk:
    __slots__ = ("w", "r")

    def __init__(s):
        s.w = None
        s.r = {}


class Buf:
    def __init__(s, h):
        s.h = h
        s.tks = {}

    def t(s, key=0):
        if key not in s.tks:
            s.tks[key] = Tk()
        return s.tks[key]

    def seg(s, kc, ca, cb, g=64):
        return [s.t((kc, q)) for q in range(ca // g, (cb - 1) // g + 1)]

    def __getitem__(s, k):
        return s.h[k]


class Prog:
    def __init__(s, nc):
        s.nc = nc
        s.eng = {"pe": nc.tensor, "act": nc.scalar, "dve": nc.vector, "pool": nc.gpsimd, "sp": nc.sync}
        s.sem = {k: nc.alloc_semaphore("s_" + k) for k in s.eng}
        s.cnt = {k: 0 for k in s.eng}
        s.pend = False
        s.waited = {k: {} for k in s.eng}
        s.dsem = {q: [nc.alloc_semaphore("d_%s%d" % (q, i)) for i in range(ND)] for q in ("sp", "pool", "act")}
        s.dcnt = {q: [0] * ND for q in s.dsem}
        s.dnext = {q: 0 for q in s.dsem}
        s.nps = 0
        s.nrot = 7
        s.ps = [Buf(nc.alloc_psum_tensor("ps%d" % i, [128, 512], F32)) for i in range(7)]
        s.psb = Buf(nc.alloc_psum_tensor("psb", [128, 1024], BF16))
        s.stack = None
        s.nalloc = 0

    @contextlib.contextmanager
    def scope(s):
        old = s.stack
        with contextlib.ExitStack() as st:
            s.stack = st
            yield
            s.barrier()
        s.stack = old

    def sb(s, name, shape, dt):
        s.nalloc += 1
        nm = "%s_%d" % (name, s.nalloc)
        if s.stack is None:
            return Buf(s.nc.alloc_sbuf_tensor(nm, list(shape), dt))
        return Buf(s.stack.enter_context(s.nc.sbuf_tensor(nm, list(shape), dt)))

    def psum(s):
        b = s.ps[s.nps % s.nrot]
        s.nps += 1
        return b

    def _wait(s, e, tok):
        if tok is None:
            return
        key, sem, val = tok
        if e == "pe" and key == "pe":
            return
        if s.waited[e].get(key, -1) >= val:
            return
        s.eng[e].wait_ge(sem, val)
        s.waited[e][key] = val

    def _deps(s, e, reads, writes):
        for t in reads:
            s._wait(e, t.w)
        for t in writes:
            s._wait(e, t.w)
            for r in t.r.values():
                s._wait(e, r)

    def _mark(s, tok, reads, writes):
        for t in reads:
            t.r[tok[0]] = tok
        for t in writes:
            t.w = tok
            t.r = {}

    def op(s, e, fn, reads=(), writes=(), sig=True):
        s._deps(e, reads, writes)
        ins = fn(s.eng[e])
        if e == "pe" and not sig:
            tok = ("pe", s.sem["pe"], s.cnt["pe"] + 1)
            s.pend = True
        else:
            s.cnt[e] += 1
            ins.then_inc(s.sem[e], 1)
            tok = (e, s.sem[e], s.cnt[e])
            if e == "pe":
                s.pend = False
        s._mark(tok, reads, writes)
        return tok

    def dma(s, q, out, in_, reads=(), writes=(), **kw):
        i = s.dnext[q]
        s.dnext[q] = (i + 1) % ND
        sem = s.dsem[q][i]
        key = "d_%s%d" % (q, i)
        if s.dcnt[q][i] > 0:
            s._wait(q, (key, sem, s.dcnt[q][i]))
        s._deps(q, reads, writes)
        ins = s.eng[q].dma_start(out=out, in_=in_, **kw)
        s.dcnt[q][i] += 16
        ins.then_inc(sem, 16)
        tok = (key, sem, s.dcnt[q][i])
        s._mark(tok, reads, writes)
        return tok

    def barrier(s):
        assert not s.pend
        toks = [(k, s.sem[k], s.cnt[k]) for k in s.eng if s.cnt[k] > 0]
        for q in s.dsem:
            for i in range(ND):
                if s.dcnt[q][i] > 0:
                    toks.append(("d_%s%d" % (q, i), s.dsem[q][i], s.dcnt[q][i]))
        for e in s.eng:
            for tok in toks:
                s._wait(e, tok)


def build(cfg, flags=None):
    fl = dict(even=True, odd=True, ffn=True)
    if flags:
        fl.update(flags)
    c = cfg
    D, SEQ, CTX, DEPTH, DFF, KC, FC = c.D, c.SEQ, c.CTX, c.DEPTH, c.DFF, c.KC, c.FC
    nc = bass.Bass("TRN2", target_bir_lowering=False)
    P = Prog(nc)

    def din(name, shape):
        return nc.dram_tensor(name, list(shape), F32, kind="ExternalInput").ap()

    def dscr(name, shape, dt=F32):
        return nc.dram_tensor(name, list(shape), dt).ap()

    xT = din("xT", [D, SEQ])
    cT = din("cT", [D, CTX])
    cv = din("cv", [128, KC * 2])
    w_mod = din("w_mod", [DEPTH, D, 6 * D])
    b_modT = din("b_modT", [128, DEPTH * 6 * KC])
    ngT = din("ngT", [128, DEPTH * 2 * KC])
    fgT = din("fgT", [128, KC])
    w_up = din("w_up", [DEPTH, D, 2 * DFF])
    w_down = din("w_down", [DEPTH, DFF, D])
    convT = din("convT", [128, DEPTH * FC * 4])
    pool_w = din("pool_w", [c.NO, 4, c.PG, c.PG])
    pool_scT = din("pool_scT", [128, c.NO * KC])
    icnt_l = din("icnt_l", [4, SEQ])
    icnt_c = din("icnt_c", [4, CTX])
    w_in = din("w_in", [c.NE, D, c.DIN])
    w_out = din("w_out", [c.NE, D, D])
    wg2 = din("wg2", [c.NE, 32, 2 * c.BQK])
    bg = din("bg", [c.NE, 1, 2 * c.BQK])
    rpbx = din("rpbx", [c.NE, c.NAH, 128, c.NBLK * 64])
    nmask = din("nmask", [2, 128, c.NBLK * 64])
    rope = din("rope", [2, 128, SEQ])
    gla_gT = din("gla_gT", [128, c.NE * c.GH])
    cmat = din("cmat", [8, 128, 128])
    outT = nc.dram_tensor("outT", [D, SEQ], F32, kind="ExternalOutput").ap()
    X = [dscr("X0", [D, SEQ]), dscr("X1", [D, SEQ])]
    C = [dscr("C0", [D, CTX]), dscr("C1", [D, CTX])]

    onesD = P.sb("onesD", [128, 128], BF16)
    silc = P.sb("silc", [128, KC * 2], BF16)
    cvs = P.sb("cvs", [128, KC * 2], F32)
    modv = P.sb("modv", [128, DEPTH * 6 * KC * 2], F32)
    bmod = P.sb("bmod", [128, DEPTH * 6 * KC], F32)
    ng = P.sb("ng", [128, DEPTH * 2 * KC], F32)
    fg = P.sb("fg", [128, KC], F32)
    conv = P.sb("conv", [128, DEPTH * FC * 4], F32)
    P.dma("pool", onesD[:], cmat[0], writes=[onesD.t()])
    P.dma("sp", cvs[:], cv, writes=[cvs.t()])
    P.dma("sp", bmod[:], b_modT, writes=[bmod.t()])
    P.dma("sp", ng[:], ngT, writes=[ng.t()])
    P.dma("sp", fg[:], fgT, writes=[fg.t()])
    P.dma("sp", conv[:], convT, writes=[conv.t()])
    P.op("act", lambda e: e.activation(out=silc[:], in_=cvs[:], func=AF.Silu), reads=[cvs.t()], writes=[silc.t()])

    def mvcol(i, v, kc, g):
        return ((i * 6 + v) * KC + kc) * 2 + g

    def mv(i, v, kc, g):
        cidx = mvcol(i, v, kc, g)
        return modv[:, cidx:cidx + 1]

    def modvec_gen(i, gw, nbuf):
        wt = [P.sb("mw", [128, KC, gw], BF16) for _ in range(nbuf)]
        mvt = [P.sb("mvt", [2, gw], F32) for _ in range(2)]
        i2 = P.sb("i2", [2, 2], F32)
        P.dma("sp", i2[:], cmat[3][0:2, 0:2], writes=[i2.t()])
        noc = 6 * KC
        base = i * noc * 2
        ngrp = 6 * D // gw
        no = gw // 128
        for g in range(ngrp):
            w = wt[g % nbuf]
            m_ = mvt[g % 2]
            P.dma("pool", w[:], w_mod[i, :, g * gw:(g + 1) * gw].rearrange("(kc p) n -> p kc n", p=128), writes=[w.t()])
            pg = P.psum()
            for kc in range(KC):
                P.op("pe", lambda e: e.matmul(pg[0:2, 0:gw], silc[:, kc * 2:kc * 2 + 2], w[:, kc, :],
                                              start=(kc == 0), stop=(kc == KC - 1)),
                     reads=[w.t(), silc.t()], writes=[pg.t()], sig=(kc == KC - 1))
            P.op("act", lambda e: e.copy(m_[:, 0:gw], pg[0:2, 0:gw]), reads=[pg.t()], writes=[m_.t()])
            pt = P.psum()
            for j in range(no):
                P.op("pe", lambda e: e.matmul(pt[:, j * 2:j * 2 + 2], m_[:, j * 128:(j + 1) * 128], i2[:], start=True, stop=True),
                     reads=[m_.t(), i2.t()], writes=[pt.t()], sig=(j == no - 1))
            oc0 = g * no
            P.op("dve", lambda e: e.tensor_tensor(modv[:, base + oc0 * 2:base + (oc0 + no) * 2].rearrange("p (o t) -> p o t", t=2),
                                                  pt[:, 0:no * 2].rearrange("p (o t) -> p o t", t=2),
                                                  bmod[:, i * noc + oc0:i * noc + oc0 + no].unsqueeze(2).to_broadcast([128, no, 2]),
                                                  ALU.add),
                 reads=[pt.t(), bmod.t()], writes=[modv.t()])
            yield
        for (v, which) in ((1, 0), (4, 1)):
            for g in range(2):
                b0 = mvcol(i, v, 0, g)
                sl = modv[:, b0:b0 + 2 * KC:2]
                P.op("dve", lambda e: e.scalar_tensor_tensor(sl, sl, 1.0, ng[:, (i * 2 + which) * KC:(i * 2 + which + 1) * KC],
                                                             ALU.add, ALU.mult),
                     reads=[modv.t(), ng.t()], writes=[modv.t()])

    def st_modvec(i):
        with P.scope():
            for _ in modvec_gen(i, 512, 2):
                pass

    class Stepper:
        def __init__(s_, gen, nsteps, total):
            s_.gen, s_.per = gen, (total + nsteps - 1) // nsteps

        def step(s_):
            if s_.gen is None:
                return
            for _ in range(s_.per):
                try:
                    next(s_.gen)
                except StopIteration:
                    s_.gen = None
                    return

        def finish(s_):
            while s_.gen is not None:
                s_.step()

    def rms_rstd(xs, n, rs, sqs):
        ps = P.psum()
        P.op("dve", lambda e: e.tensor_tensor(sqs[:, :, 0:n], xs[:, :, 0:n], xs[:, :, 0:n], ALU.mult),
             reads=[xs.t()], writes=[sqs.t()])
        for kc in range(KC):
            P.op("pe", lambda e: e.matmul(ps[:, 0:n], onesD[:], sqs[:, kc, 0:n], start=(kc == 0), stop=(kc == KC - 1)),
                 reads=[sqs.t(), onesD.t()], writes=[ps.t()], sig=(kc == KC - 1))
        P.op("act", lambda e: e.activation(out=rs[:, 0:n], in_=ps[:, 0:n], func=AF.Ln, bias=EPS),
             reads=[ps.t()], writes=[rs.t()])
        P.op("act", lambda e: e.activation(out=rs[:, 0:n], in_=rs[:, 0:n], func=AF.Exp, scale=-0.5),
             reads=[rs.t()], writes=[rs.t()])

    def norm_mod(src, t0, n, dstbuf, c0, wk, i, va, vs, g, xs, rs, sqs, tmps):
        P.dma("sp", xs[:, :, 0:n], src[:, t0:t0 + n].rearrange("(kc p) n -> p kc n", p=128), writes=[xs.t()])
        rms_rstd(xs, n, rs, sqs)
        for kc in range(KC):
            tmp = tmps[kc % 2]
            P.op("dve", lambda e: e.tensor_tensor(tmp[:, 0:n], xs[:, kc, 0:n], rs[:, 0:n], ALU.mult),
                 reads=[xs.t(), rs.t()], writes=[tmp.t()])
            P.op("act", lambda e: e.activation(out=dstbuf[:, kc, c0:c0 + n], in_=tmp[:, 0:n], func=AF.Identity,
                                               bias=mv(i, vs, kc, g), scale=mv(i, va, kc, g)),
                 reads=[tmp.t(), modv.t()], writes=wk(kc))

    def norm_gen(src, subs, dstbuf, wkf, i, va, vs, g, nb):
        def stage1(k):
            t0, n, c0 = subs[k]
            xs, sq, rs = nb["xs"][k % 2], nb["sq"][k % 2], nb["rs"][k % 2]
            P.dma("sp", xs[:, :, 0:n], src[:, t0:t0 + n].rearrange("(kc p) n -> p kc n", p=128), writes=[xs.t()])
            rms_rstd(xs, n, rs, sq)

        def stage2(k):
            t0, n, c0 = subs[k]
            xs, rs = nb["xs"][k % 2], nb["rs"][k % 2]
            ba, bs = mvcol(i, va, 0, g), mvcol(i, vs, 0, g)
            rb = rs[:, 0:n].unsqueeze(1).to_broadcast([128, KC, n])
            ab = modv[:, ba:ba + 2 * KC:2].unsqueeze(2).to_broadcast([128, KC, n])
            sb_ = modv[:, bs:bs + 2 * KC:2].unsqueeze(2).to_broadcast([128, KC, n])
            P.op("dve", lambda e: e.tensor_tensor(xs[:, :, 0:n], xs[:, :, 0:n], rb, ALU.mult),
                 reads=[xs.t(), rs.t()], writes=[xs.t()])
            P.op("dve", lambda e: e.tensor_tensor(xs[:, :, 0:n], xs[:, :, 0:n], ab, ALU.mult),
                 reads=[xs.t(), modv.t()], writes=[xs.t()])
            P.op("dve", lambda e: e.tensor_tensor(dstbuf[:, :, c0:c0 + n], xs[:, :, 0:n], sb_, ALU.add),
                 reads=[xs.t(), modv.t()], writes=[t_ for kc in range(KC) for t_ in wkf(kc, c0, n)])
        if not subs:
            return
        stage1(0)
        for k in range(len(subs)):
            if k + 1 < len(subs):
                stage1(k + 1)
            stage2(k)
            yield

    def norm_run(*a_, **k_):
        for _ in norm_gen(*a_, **k_):
            pass

    def norm_bufs(ns):
        return dict(xs=[P.sb("xs", [128, KC, ns], F32) for _ in range(2)],
                    sq=[P.sb("sq", [128, KC, ns], BF16) for _ in range(2)],
                    rs=[P.sb("rs", [128, ns], F32) for _ in range(2)],
                    tmp=[])

    ACTS = dscr("ACTS", [FC, 128, SEQ], BF16)
    NSUB = 128

    def st_ffn(i, src, dst, ntok, g):
        PT = min(c.TBF, ntok)
        with P.scope():
            h2 = P.sb("h2", [128, KC, PT + 2], BF16)
            gsb = P.sb("gsb", [128, PT + 2], F32)
            cvt = P.sb("cvt", [128, PT], F32)
            val = [P.sb("val", [128, PT], BF16) for _ in range(2)]
            aj = [P.sb("aj", [128, PT], BF16) for _ in range(2)]
            nb = norm_bufs(NSUB)
            wv = [P.sb("wv", [128, KC, 512], BF16) for _ in range(2)]
            wg = [P.sb("wg", [128, KC, 512], BF16) for _ in range(2)]
            for b0 in range(0, ntok, PT):
                tb = min(PT, ntok - b0)
                lo, hi = b0 - 1, b0 + tb + 1
                if lo < 0:
                    for kc in range(KC):
                        P.op("pool", lambda e: e.memset(h2[:, kc, 0:1], 0.0), writes=h2.seg(kc, 0, 1))
                if hi > ntok:
                    for kc in range(KC):
                        P.op("pool", lambda e: e.memset(h2[:, kc, tb + 1:tb + 2], 0.0), writes=h2.seg(kc, tb + 1, tb + 2))
                a, bnd = max(lo, 0), min(hi, ntok)
                norm_run(src, [(t0, n, t0 - lo) for (t0, n) in _split(a, bnd, NSUB)], h2,
                         (lambda kc, c0, n: h2.seg(kc, c0, c0 + n)), i, 4, 3, g, nb)
                nsub = (tb + 511) // 512
                for jg in range((FC + 3) // 4):
                    nj = min(4, FC - jg * 4)
                    wvj, wgj = wv[jg % 2], wg[jg % 2]
                    P.dma("pool", wvj[:, :, 0:nj * 128], w_up[i, :, jg * 512:jg * 512 + nj * 128].rearrange("(kc p) n -> p kc n", p=128),
                          writes=[wvj.t()])
                    P.dma("pool", wgj[:, :, 0:nj * 128], w_up[i, :, DFF + jg * 512:DFF + jg * 512 + nj * 128].rearrange("(kc p) n -> p kc n", p=128),
                          writes=[wgj.t()])
                    for jj in range(nj):
                        j = jg * 4 + jj
                        ws = slice(jj * 128, (jj + 1) * 128)
                        vj, ajj = val[j % 2], aj[j % 2]
                        for s in range(nsub):
                            n = min(512, tb - s * 512)
                            c0 = 1 + s * 512
                            psv = P.psum()
                            for kc in range(KC):
                                P.op("pe", lambda e: e.matmul(psv[:, 0:n], wvj[:, kc, ws], h2[:, kc, c0:c0 + n],
                                                              start=(kc == 0), stop=(kc == KC - 1)),
                                     reads=[wvj.t()] + h2.seg(kc, c0, c0 + n), writes=[psv.t()], sig=(kc == KC - 1))
                            P.op("act", lambda e: e.copy(vj[:, s * 512:s * 512 + n], psv[:, 0:n]), reads=[psv.t()],
                                 writes=[vj.t(s)])
                            psg = P.psum()
                            for kc in range(KC):
                                P.op("pe", lambda e: e.matmul(psg[:, 0:n], wgj[:, kc, ws], h2[:, kc, c0:c0 + n],
                                                              start=(kc == 0), stop=(kc == KC - 1)),
                                     reads=[wgj.t()] + h2.seg(kc, c0, c0 + n), writes=[psg.t()], sig=(kc == KC - 1))
                            P.op("act", lambda e: e.copy(gsb[:, c0:c0 + n], psg[:, 0:n]), reads=[psg.t()],
                                 writes=[gsb.t(s)])
                        psh = P.psum()
                        for kc in range(KC):
                            P.op("pe", lambda e: e.matmul(psh[:, 0:2], wgj[:, kc, ws], h2[:, kc, 0:tb + 2:tb + 1],
                                                          start=(kc == 0), stop=(kc == KC - 1)),
                                 reads=[wgj.t()] + h2.seg(kc, 0, 1) + h2.seg(kc, tb + 1, tb + 2), writes=[psh.t()],
                                 sig=(kc == KC - 1))
                        P.op("act", lambda e: e.copy(gsb[:, 0:tb + 2:tb + 1], psh[:, 0:2]), reads=[psh.t()],
                             writes=[gsb.t("h")])
                        cb = (i * FC + j) * 4
                        allg = [gsb.t(s) for s in range(nsub)] + [gsb.t("h")]
                        P.op("dve", lambda e: e.tensor_scalar(cvt[:, 0:tb], gsb[:, 0:tb], conv[:, cb:cb + 1], None, ALU.mult),
                             reads=allg + [conv.t()], writes=[cvt.t()])
                        P.op("dve", lambda e: e.scalar_tensor_tensor(cvt[:, 0:tb], gsb[:, 1:tb + 1], conv[:, cb + 1:cb + 2],
                                                                     cvt[:, 0:tb], ALU.mult, ALU.add),
                             reads=allg + [cvt.t()], writes=[cvt.t()])
                        P.op("dve", lambda e: e.scalar_tensor_tensor(cvt[:, 0:tb], gsb[:, 2:tb + 2], conv[:, cb + 2:cb + 3],
                                                                     cvt[:, 0:tb], ALU.mult, ALU.add),
                             reads=allg + [cvt.t()], writes=[cvt.t()])
                        P.op("act", lambda e: e.activation(out=cvt[:, 0:tb], in_=cvt[:, 0:tb], func=AF.Gelu,
                                                           bias=conv[:, cb + 3:cb + 4]),
                             reads=[cvt.t(), conv.t()], writes=[cvt.t()])
                        P.op("dve", lambda e: e.tensor_tensor(ajj[:, 0:tb], cvt[:, 0:tb], vj[:, 0:tb], ALU.mult),
                             reads=[cvt.t()] + [vj.t(s) for s in range(nsub)], writes=[ajj.t()])
                        P.dma("sp", ACTS[j, :, b0:b0 + tb], ajj[:, 0:tb], reads=[ajj.t()])
        TB = min(1024, ntok)
        with P.scope():
            act = P.sb("actT", [128, FC, TB], BF16)
            wd = [P.sb("wd", [128, FC, 512], BF16) for _ in range(2)]
            xo = [P.sb("xo", [128, 512], F32) for _ in range(2)]
            xn = [P.sb("xn", [128, 512], F32) for _ in range(2)]
            kk = 0
            nwd = 0
            for b0 in range(0, ntok, TB):
                tb = min(TB, ntok - b0)
                nsub = (tb + 511) // 512
                for jq in range(0, FC, 11):
                    je = min(jq + 11, FC)
                    P.dma("sp", act[:, jq:je, 0:tb], ACTS[jq:je, :, b0:b0 + tb].rearrange("j p n -> p j n"),
                          writes=[act.t(jq)])
                for ocg in range(D // 512):
                    wdo = wd[nwd % 2]
                    nwd += 1
                    P.dma("pool", wdo[:], w_down[i, :, ocg * 512:(ocg + 1) * 512].rearrange("(fc p) n -> p fc n", p=128),
                          writes=[wdo.t()])
                    for o4 in range(4):
                        oc = ocg * 4 + o4
                        for s in range(nsub):
                            n = min(512, tb - s * 512)
                            k = kk % 2
                            kk += 1
                            P.dma("sp", xo[k][:, 0:n], src[oc * 128:(oc + 1) * 128, b0 + s * 512:b0 + s * 512 + n],
                                  writes=[xo[k].t()])
                            ps = P.psum()
                            for j in range(FC):
                                P.op("pe", lambda e: e.matmul(ps[:, 0:n], wdo[:, j, o4 * 128:(o4 + 1) * 128], act[:, j, s * 512:s * 512 + n],
                                                              start=(j == 0), stop=(j == FC - 1)),
                                     reads=[wdo.t(), act.t((j // 11) * 11)], writes=[ps.t()], sig=(j == FC - 1))
                            P.op("dve", lambda e: e.scalar_tensor_tensor(xn[k][:, 0:n], ps[:, 0:n], mv(i, 5, oc, g),
                                                                         xo[k][:, 0:n], ALU.mult, ALU.add),
                                 reads=[ps.t(), xo[k].t(), modv.t()], writes=[xn[k].t()])
                            P.dma("sp", dst[oc * 128:(oc + 1) * 128, b0 + s * 512:b0 + s * 512 + n], xn[k][:, 0:n],
                                  reads=[xn[k].t()])

    def st_copy(src, dst, ntok):
        with P.scope():
            xs = [P.sb("cp", [128, KC, 512], F32) for _ in range(2)]
            k = 0
            for t0 in range(0, ntok, 512):
                n = min(512, ntok - t0)
                P.dma("sp", xs[k][:, :, 0:n], src[:, t0:t0 + n].rearrange("(kc p) n -> p kc n", p=128), writes=[xs[k].t()])
                P.dma("sp", dst[:, t0:t0 + n].rearrange("(kc p) n -> p kc n", p=128), xs[k][:, :, 0:n], reads=[xs[k].t()])
                k ^= 1

    def st_final(src):
        with P.scope():
            xs = P.sb("xs", [128, KC, 512], F32)
            rs = P.sb("rs", [128, 512], F32)
            sqs = P.sb("sq", [128, KC, 512], BF16)
            ot = [P.sb("ot", [128, KC, 512], F32) for _ in range(2)]
            k = 0
            for t0 in range(0, SEQ, 512):
                n = min(512, SEQ - t0)
                P.dma("sp", xs[:, :, 0:n], src[:, t0:t0 + n].rearrange("(kc p) n -> p kc n", p=128), writes=[xs.t()])
                rms_rstd(xs, n, rs, sqs)
                o = ot[k]
                for kc in range(KC):
                    P.op("dve", lambda e: e.scalar_tensor_tensor(o[:, kc, 0:n], xs[:, kc, 0:n], fg[:, kc:kc + 1],
                                                                 rs[:, 0:n], ALU.mult, ALU.mult),
                         reads=[xs.t(), rs.t(), fg.t()], writes=[o.t(kc)])
                P.dma("sp", outT[:, t0:t0 + n].rearrange("(kc p) n -> p kc n", p=128), o[:, :, 0:n],
                      reads=[o.t(kc) for kc in range(KC)])
                k ^= 1

    def st_pool(i, src, dst, ntok, g, nxt=None):
        o = i // 2
        K4 = KC // 4
        TBP = 512
        icn = icnt_l if g == 0 else icnt_c
        with P.scope():
            hp = P.sb("hp", [128, KC, TBP + 16], F32)
            pl = P.sb("pl", [128, KC, TBP], BF16)
            ic = P.sb("ic", [128, 4, TBP], F32)
            WA = P.sb("WA", [128, KC, TBP + 16], F32)
            WB = P.sb("WB", [128, KC, TBP + 16], F32)
            nb = norm_bufs(128)
            mst = Stepper(modvec_gen(nxt, 256, 3), (ntok + TBP - 1) // TBP, 6 * D // 256) if nxt is not None else None
            wp = P.sb("wp", [128, 4, K4, c.PG], BF16)
            gp = P.sb("gp", [128, KC], F32)
            psc = P.sb("psc", [128, KC], F32)
            xo = [P.sb("xo", [128, 512], F32) for _ in range(2)]
            xn = [P.sb("xn", [128, 512], F32) for _ in range(2)]
            for gi in range(4):
                P.dma("pool", wp[:, gi, :, :], pool_w[o, gi].rearrange("(k p) n -> p k n", p=128), writes=[wp.t(gi)])
            P.dma("sp", psc[:], pool_scT[:, o * KC:(o + 1) * KC], writes=[psc.t()])
            b0c = mvcol(i, 2, 0, g)
            P.op("dve", lambda e: e.tensor_tensor(gp[:], modv[:, b0c:b0c + 2 * KC:2], psc[:], ALU.mult),
                 reads=[modv.t(), psc.t()], writes=[gp.t()])
            kk = 0
            for b0 in range(0, ntok, TBP):
                tb = min(TBP, ntok - b0)
                L = tb + 16
                lo, hi = b0 - 8, b0 + tb + 8
                a, bnd = max(lo, 0), min(hi, ntok)
                hkeys = []
                if lo < 0:
                    P.op("pool", lambda e: e.memset(hp[:, :, 0:8], 0.0), writes=[t_ for kc in range(KC) for t_ in hp.seg(kc, 0, 8)])
                if hi > ntok:
                    P.op("pool", lambda e: e.memset(hp[:, :, tb + 8:tb + 16], 0.0), writes=[t_ for kc in range(KC) for t_ in hp.seg(kc, tb + 8, tb + 16)])
                subs = [(t0, n, t0 - lo) for (t0, n) in _split(a, bnd, 128)]
                if mst:
                    mst.step()
                norm_run(src, subs, hp, (lambda kc, c0, n: hp.seg(kc, c0, c0 + n)), i, 1, 0, g, nb)
                for w in range(4):
                    P.dma("sp", ic[:, w, 0:tb], icn[w:w + 1, b0:b0 + tb].partition_broadcast(128), writes=[ic.t(w)])
                hall = [t_ for kc in range(KC) for t_ in hp.seg(kc, 0, L)]
                P.op("dve", lambda e: e.tensor_tensor(WA[:, :, 1:L], hp[:, :, 0:L - 1], hp[:, :, 1:L], ALU.add),
                     reads=hall, writes=[WA.t()])
                P.op("dve", lambda e: e.tensor_tensor(WB[:, K4:KC, 2:L - 1], WA[:, K4:KC, 1:L - 2], WA[:, K4:KC, 3:L], ALU.add),
                     reads=[WA.t()], writes=[WB.t()])
                P.op("dve", lambda e: e.tensor_tensor(WA[:, 2 * K4:KC, 4:L - 3], WB[:, 2 * K4:KC, 2:L - 5], WB[:, 2 * K4:KC, 6:L - 1], ALU.add),
                     reads=[WB.t()], writes=[WA.t()])
                P.op("dve", lambda e: e.tensor_tensor(WB[:, 3 * K4:KC, 8:L - 7], WA[:, 3 * K4:KC, 4:L - 11], WA[:, 3 * K4:KC, 12:L - 3], ALU.add),
                     reads=[WA.t()], writes=[WB.t()])
                for gi in range(4):
                    cur = WA if gi % 2 == 0 else WB
                    en = "dve" if gi < 2 else "pool"
                    ka, kb = gi * K4, (gi + 1) * K4
                    icb = ic[:, gi, 0:tb].unsqueeze(1).to_broadcast([128, K4, tb])
                    P.op(en, lambda e: e.tensor_tensor(cur[:, ka:kb, 8:8 + tb], cur[:, ka:kb, 8:8 + tb], icb, ALU.mult),
                         reads=[WA.t(), WB.t(), ic.t(gi)], writes=[cur.t(("f", gi))])
                    P.op(en, lambda e: e.tensor_tensor(pl[:, ka:kb, 0:tb], cur[:, ka:kb, 8:8 + tb], hp[:, ka:kb, 8:8 + tb], ALU.subtract),
                         reads=[cur.t(("f", gi))] + hall, writes=[pl.t(gi)])
                for gi in range(4):
                    for oc in range(K4):
                        ocg = gi * K4 + oc
                        k = kk % 2
                        kk += 1
                        P.dma("sp", xo[k][:, 0:tb], src[ocg * 128:(ocg + 1) * 128, b0:b0 + tb], writes=[xo[k].t()])
                        ps = P.psum()
                        for k4 in range(K4):
                            P.op("pe", lambda e: e.matmul(ps[:, 0:tb], wp[:, gi, k4, oc * 128:(oc + 1) * 128],
                                                          pl[:, gi * K4 + k4, 0:tb], start=(k4 == 0), stop=(k4 == K4 - 1)),
                                 reads=[wp.t(gi), pl.t(gi)], writes=[ps.t()], sig=(k4 == K4 - 1))
                        P.op("dve", lambda e: e.scalar_tensor_tensor(xn[k][:, 0:tb], ps[:, 0:tb], gp[:, ocg:ocg + 1],
                                                                     xo[k][:, 0:tb], ALU.mult, ALU.add),
                             reads=[ps.t(), xo[k].t(), gp.t()], writes=[xn[k].t()])
                        P.dma("sp", dst[ocg * 128:(ocg + 1) * 128, b0:b0 + tb], xn[k][:, 0:tb], reads=[xn[k].t()])

            if mst:
                mst.finish()

    NT = CTX + SEQ
    AW, BQK, BV, NAH, GH, DIN = c.AW, c.BQK, c.BV, c.NAH, c.GH, c.DIN
    NCC = CTX // 128
    NCH = NT // 128
    ROWS = c.ROWS
    NB64 = c.NBLK * 64
    QKA = dscr("QKA", [2 * AW, NT], BF16)
    VA = dscr("VA", [NT, AW], BF16)
    QKB = dscr("QKB", [2 * BQK, NT], BF16)
    VBm = dscr("VBm", [NT, BV], BF16)
    GBs = dscr("GBs", [BV, NT], BF16)
    ABT = dscr("ABT", [32, NT], F32)
    OT = dscr("OT", [D, NT], BF16)
    segs = [("qa", 0, AW), ("ka", AW, 2 * AW), ("va", 2 * AW, 3 * AW), ("qb", 3 * AW, 3 * AW + BQK),
            ("kb", 3 * AW + BQK, 3 * AW + 2 * BQK), ("vb", 3 * AW + 2 * BQK, 3 * AW + 2 * BQK + BV),
            ("gb", 3 * AW + 2 * BQK + BV, 3 * AW + 2 * BQK + 2 * BV), ("ab", DIN - 32, DIN)]

    def ctype(col):
        for (nm, a, b) in segs:
            if a <= col < b:
                return nm, col - a
        raise ValueError

    def e_proj(i, e_, src, tok0, ntok, g):
        TB = min(c.TBE, ntok)
        with P.scope():
            hTs = [P.sb("hT", [128, KC, TB], BF16) for _ in range(2)]
            nb = norm_bufs(256)
            wt = [P.sb("wi", [128, KC, 512], BF16) for _ in range(2)]
            st = [P.sb("st", [128, 512], BF16) for _ in range(3)]
            stf = [P.sb("stf", [32, 512], F32) for _ in range(2)]
            nst = 0
            blocks = list(range(0, ntok, TB))

            def mk_norm(bi):
                b0_ = blocks[bi]
                tb_ = min(TB, ntok - b0_)
                hb = hTs[bi % 2]
                subs_ = [(t0, min(256, tb_ - t0)) for t0 in range(0, tb_, 256)]
                return norm_gen(src, [(b0_ + t0, n, t0) for (t0, n) in subs_], hb, (lambda kc, c0, n: [hb.t((c0, kc))]), i, 1, 0, g, nb)

            for _ in mk_norm(0):
                pass
            for bi, b0 in enumerate(blocks):
                tb = min(TB, ntok - b0)
                hT = hTs[bi % 2]
                subs = [(t0, min(256, tb - t0)) for t0 in range(0, tb, 256)]
                nxt_norm = mk_norm(bi + 1) if bi + 1 < len(blocks) else None

                def hk(kc, ca, cb):
                    return [hT.t((t0, kc)) for (t0, n) in subs if t0 < cb and t0 + n > ca]

                ng_ = (DIN + 511) // 512
                for cg in range(ng_):
                    if nxt_norm is not None and cg % 3 == 2:
                        try:
                            next(nxt_norm)
                        except StopIteration:
                            nxt_norm = None
                    w = wt[cg % 2]
                    ncol = min(512, DIN - cg * 512)
                    P.dma("pool", w[:, :, 0:ncol], w_in[e_, :, cg * 512:cg * 512 + ncol].rearrange("(kc p) n -> p kc n", p=128),
                          writes=[w.t()])
                    j = 0
                    while j * 128 < ncol:
                        col = cg * 512 + j * 128
                        nm, off = ctype(col)
                        if nm in ("va", "vb"):
                            j2 = j
                            while (j2 + 1) * 128 < ncol and ctype(cg * 512 + (j2 + 1) * 128)[0] == nm:
                                j2 += 1
                            nr = (j2 - j + 1) * 128
                            dstT = VA if nm == "va" else VBm
                            for tc in range(0, tb, 128):
                                ps = P.psum()
                                for kc in range(KC):
                                    P.op("pe", lambda e: e.matmul(ps[:, 0:nr], hT[:, kc, tc:tc + 128], w[:, kc, j * 128:j * 128 + nr],
                                                                  start=(kc == 0), stop=(kc == KC - 1)),
                                         reads=[w.t()] + hk(kc, tc, tc + 128), writes=[ps.t()], sig=(kc == KC - 1))
                                s_ = st[nst % 3]
                                nst += 1
                                P.op("act", lambda e: e.copy(s_[:, 0:nr], ps[:, 0:nr]), reads=[ps.t()], writes=[s_.t()])
                                P.dma("sp", dstT[tok0 + b0 + tc:tok0 + b0 + tc + 128, off:off + nr], s_[:, 0:nr], reads=[s_.t()])
                            j = j2 + 1
                            continue
                        m = 32 if nm == "ab" else 128
                        for s0 in range(0, tb, 512):
                            n = min(512, tb - s0)
                            ps = P.psum()
                            for kc in range(KC):
                                P.op("pe", lambda e: e.matmul(ps[0:m, 0:n], w[:, kc, j * 128:j * 128 + m], hT[:, kc, s0:s0 + n],
                                                              start=(kc == 0), stop=(kc == KC - 1)),
                                     reads=[w.t()] + hk(kc, s0, s0 + n), writes=[ps.t()], sig=(kc == KC - 1))
                            tcol = tok0 + b0 + s0
                            if nm == "ab":
                                s_ = stf[nst % 2]
                                nst += 1
                                P.op("act", lambda e: e.copy(s_[:, 0:n], ps[0:32, 0:n]), reads=[ps.t()], writes=[s_.t()])
                                P.dma("sp", ABT[:, tcol:tcol + n], s_[:, 0:n], reads=[s_.t()])
                            else:
                                s_ = st[nst % 3]
                                nst += 1
                                if nm == "qb":
                                    P.op("act", lambda e: e.mul(s_[:, 0:n], ps[:, 0:n], 0.125), reads=[ps.t()], writes=[s_.t()])
                                elif nm == "gb":
                                    P.op("act", lambda e: e.activation(out=s_[:, 0:n], in_=ps[:, 0:n], func=AF.Silu),
                                         reads=[ps.t()], writes=[s_.t()])
                                else:
                                    P.op("act", lambda e: e.copy(s_[:, 0:n], ps[:, 0:n]), reads=[ps.t()], writes=[s_.t()])
                                if nm in ("qa", "ka"):
                                    r0 = off + (AW if nm == "ka" else 0)
                                    dd = QKA[r0:r0 + 128, tcol:tcol + n]
                                elif nm in ("qb", "kb"):
                                    r0 = off + (BQK if nm == "kb" else 0)
                                    dd = QKB[r0:r0 + 128, tcol:tcol + n]
                                else:
                                    dd = GBs[off:off + 128, tcol:tcol + n]
                                P.dma("sp", dd, s_[:, 0:n], reads=[s_.t()])
                        j += 1

                if nxt_norm is not None:
                    for _ in nxt_norm:
                        pass

    def e_na(e_, need_ctx, nxt=None):
        scale = 128 ** -0.5
        with P.scope():
            P.nrot = 3
            kT = [P.sb("kT", [128, NT], BF16) for _ in range(2)]
            qT = [P.sb("qT", [128, NT], BF16) for _ in range(2)]
            V = [P.sb("V", [128, NCH, 128], BF16) for _ in range(2)]
            TF = [P.sb("TF", [128, NB64], F32) for _ in range(2)]
            TZ = [P.sb("TZ", [128, NB64], F32) for _ in range(2)]
            mF = P.sb("mF", [128, NB64], F32)
            mZ = P.sb("mZ", [128, NB64], F32)
            ones_bf = P.sb("ones_bf", [128, 128], BF16)
            ex = [P.sb("ex", [128, 512], F32) for _ in range(2)]
            pT = [P.sb("pT", [128, 512], BF16) for _ in range(4)]
            rd = [P.sb("rd", [128, 512], F32) for _ in range(2)]
            oT = [P.sb("oT", [128, 512], BF16) for _ in range(2)]
            P.dma("sp", mF[:], nmask[0], writes=[mF.t()])
            P.dma("sp", mZ[:], nmask[1], writes=[mZ.t()])
            P.dma("pool", ones_bf[:], cmat[2], writes=[ones_bf.t()])
            cnt = 0
            ntile = 0
            mst = Stepper(modvec_gen(nxt, 512, 3), NAH, 6 * D // 512) if nxt is not None else None
            for h in range(NAH):
                if mst:
                    mst.step()
                k_, q_, v_, tf, tz = kT[h % 2], qT[h % 2], V[h % 2], TF[h % 2], TZ[h % 2]
                P.dma("sp", k_[:], QKA[AW + h * 128:AW + (h + 1) * 128, :], writes=[k_.t()])
                P.dma("sp", q_[:], QKA[h * 128:(h + 1) * 128, :], writes=[q_.t()])
                P.dma("sp", v_[:], VA[:, h * 128:(h + 1) * 128].rearrange("(ch p) d -> p ch d", p=128), writes=[v_.t()])
                P.dma("sp", tf[:], rpbx[e_, h], writes=[tf.t()])
                P.op("act", lambda e: e.activation(out=tf[:], in_=tf[:], func=AF.Exp), reads=[tf.t()], writes=[tf.t()])
                P.op("dve", lambda e: e.tensor_tensor(tz[:], tf[:], mZ[:], ALU.mult), reads=[tf.t(), mZ.t()], writes=[tz.t()])
                P.op("dve", lambda e: e.tensor_tensor(tf[:], tf[:], mF[:], ALU.mult), reads=[tf.t(), mF.t()], writes=[tf.t()])
                tiles = [("lat", qt) for qt in range(ROWS // 8)] + ([("ctx", 0)] if need_ctx else [])
                items = []
                for (kind, qt) in tiles:
                    if kind == "lat":
                        r0 = qt * 8
                        qc0, nq = CTX + qt * 512, 512
                        c_lo, c_hi = max(0, (r0 - 4) // 2), min(ROWS // 2 - 1, (r0 + 10) // 2)
                        chunks = [("c", cc) for cc in range(NCC)] + [("l", cc) for cc in range(c_lo, c_hi + 1)]
                    else:
                        r0 = 0
                        qc0, nq = 0, CTX
                        chunks = [("c", cc) for cc in range(NCC)]
                    acc = (P.ps[3 + 2 * (ntile % 2)], P.ps[4 + 2 * (ntile % 2)], ntile)
                    ntile += 1
                    for idx, (ck, cc) in enumerate(chunks):
                        items.append(dict(r0=r0, qc0=qc0, nq=nq, ck=ck, cc=cc, first=(idx == 0), last=(idx == len(chunks) - 1),
                                          acc=acc))

                def emit_s(it):
                    ck, cc, nq, qc0, r0 = it["ck"], it["cc"], it["nq"], it["qc0"], it["r0"]
                    kc0 = cc * 128 if ck == "c" else CTX + cc * 128
                    ps_s = P.psum()
                    P.op("pe", lambda e: e.matmul(ps_s[:, 0:nq], k_[:, kc0:kc0 + 128], q_[:, qc0:qc0 + nq], start=True, stop=True),
                         reads=[k_.t(), q_.t()], writes=[ps_s.t()])
                    p_ = pT[it["n"] % 4]
                    it["p"] = p_
                    if ck == "c":
                        P.op("act", lambda e: e.activation(out=p_[:, 0:nq], in_=ps_s[:, 0:nq], func=AF.Exp, scale=scale),
                             reads=[ps_s.t()], writes=[p_.t()])
                    else:
                        x_ = ex[it["n"] % 2]
                        P.op("act", lambda e: e.activation(out=x_[:, 0:nq], in_=ps_s[:, 0:nq], func=AF.Exp, scale=scale),
                             reads=[ps_s.t()], writes=[x_.t()])
                        bb = r0 - 2 * cc + 11
                        fr = None
                        if r0 == 0 and cc <= 3:
                            fr = (0, 256)
                        if r0 == ROWS - 8 and cc >= ROWS // 2 - 4:
                            fr = (256, 512)
                        rngs = [(0, 512, tz)] if fr is None else [(fr[0], fr[1], tf), (256 - fr[0], 512 - fr[0], tz)]
                        for (ca, cb, tab) in rngs:
                            P.op("dve", lambda e: e.tensor_tensor(p_[:, ca:cb], x_[:, ca:cb], tab[:, bb * 64 + ca:bb * 64 + cb], ALU.mult),
                                 reads=[x_.t(), tab.t()], writes=[p_.t()])

                def emit_pv(it):
                    ps_o, ps_d, nt = it["acc"]
                    nq, qc0, p_ = it["nq"], it["qc0"], it["p"]
                    chi = it["cc"] if it["ck"] == "c" else NCC + it["cc"]
                    P.op("pe", lambda e: e.matmul(ps_o[:, 0:nq], v_[:, chi, :], p_[:, 0:nq], start=it["first"], stop=it["last"]),
                         reads=[v_.t(), p_.t()], writes=[ps_o.t()], sig=True)
                    P.op("pe", lambda e: e.matmul(ps_d[:, 0:nq], ones_bf[:], p_[:, 0:nq], start=it["first"], stop=it["last"]),
                         reads=[ones_bf.t(), p_.t()], writes=[ps_d.t()], sig=True)
                    if it["last"]:
                        r_, o_ = rd[nt % 2], oT[nt % 2]
                        P.op("dve", lambda e: e.reciprocal(r_[:, 0:nq], ps_d[:, 0:nq]), reads=[ps_d.t()], writes=[r_.t()])
                        P.op("dve", lambda e: e.tensor_tensor(o_[:, 0:nq], ps_o[:, 0:nq], r_[:, 0:nq], ALU.mult),
                             reads=[ps_o.t(), r_.t()], writes=[o_.t()])
                        P.dma("sp", OT[h * 128:(h + 1) * 128, qc0:qc0 + nq], o_[:, 0:nq], reads=[o_.t()])

                LA = 2
                for n_, it in enumerate(items):
                    it["n"] = cnt + n_
                for n_ in range(len(items) + LA):
                    if n_ < len(items):
                        emit_s(items[n_])
                    if n_ >= LA:
                        emit_pv(items[n_ - LA])
                cnt += len(items)
            if mst:
                mst.finish()
            P.nrot = 7

    def e_gla(e_, need_ctx):
        NFC = BQK // 128
        with P.scope():
            Cs = [P.sb("ropeC", [128, 512], F32) for _ in range(2)]
            Ss = [P.sb("ropeS", [128, 512], F32) for _ in range(2)]
            permb = P.sb("permb", [128, 128], BF16)
            identb = P.sb("identb", [128, 128], BF16)
            ones128 = P.sb("ones128", [128, 128], F32)
            onesr = P.sb("onesr", [1, 128], F32)
            tri = [P.sb("tri", [128, 128], F32) for _ in range(2)]
            wg2s = P.sb("wg2s", [32, 2 * BQK], F32)
            bgs = P.sb("bgs", [1, 2 * BQK], F32)
            abT = P.sb("abT", [32, NT], F32)
            gg = P.sb("gg", [128, GH], F32)
            qr = P.sb("qr", [128, NT], BF16)
            kr = P.sb("kr", [128, NT], BF16)
            qtil = P.sb("qtil", [128, NT], BF16)
            ktil = P.sb("ktil", [128, NT], BF16)
            khat = P.sb("khat", [128, NCH, 128], BF16)
            vv = P.sb("vv", [128, NCH, 256], BF16)
            dec = P.sb("dec", [128, NCH], F32)
            ob = [P.sb("ob", [128, NT], F32) for _ in range(2)]
            St = P.sb("St", [128, 128], F32)
            t1 = [P.sb("t1", [128, 512], F32) for _ in range(2)]
            t2 = [P.sb("t2", [128, 512], F32) for _ in range(2)]
            kh = [P.sb("kh", [128, 128], BF16) for _ in range(4)]
            xk = [P.sb("xk", [128, 128], F32) for _ in range(4)]
            sg = [P.sb("sg", [128, 512], BF16) for _ in range(2)]
            yo = [P.sb("yo", [128, 512], BF16) for _ in range(2)]
            P.dma("pool", permb[:], cmat[4], writes=[permb.t()])
            P.dma("pool", identb[:], cmat[3], writes=[identb.t()])
            P.dma("sp", ones128[:], cmat[1], writes=[ones128.t()])
            P.dma("sp", onesr[:], cmat[2][0:1, :], writes=[onesr.t()])
            P.dma("sp", tri[0][:], cmat[5], writes=[tri[0].t()])
            P.dma("sp", tri[1][:], cmat[6], writes=[tri[1].t()])
            P.dma("sp", wg2s[:], wg2[e_], writes=[wg2s.t()])
            P.dma("sp", bgs[:], bg[e_], writes=[bgs.t()])
            P.dma("sp", abT[:], ABT, writes=[abT.t()])
            P.dma("sp", gg[:], gla_gT[:, e_ * GH:(e_ + 1) * GH], writes=[gg.t()])
            k2 = 0
            for fc in range(NFC):
                P.dma("sp", qr[:], QKB[fc * 128:(fc + 1) * 128, :], writes=[qr.t()])
                P.dma("sp", kr[:], QKB[BQK + fc * 128:BQK + (fc + 1) * 128, :], writes=[kr.t()])
                P.dma("sp", vv[:], VBm[:, fc * 256:(fc + 1) * 256].rearrange("(ch p) d -> p ch d", p=128), writes=[vv.t()])
                for buf in (qr, kr):
                    for s0 in range(0, SEQ, 512):
                        a, b = CTX + s0, CTX + s0 + 512
                        ps = P.psum()
                        P.op("pe", lambda e: e.matmul(ps[:, 0:512], permb[:], buf[:, a:b], start=True, stop=True),
                             reads=[permb.t(), buf.t()], writes=[ps.t()])
                        x1, x2 = t1[k2 % 2], t2[k2 % 2]
                        C_, S_ = Cs[k2 % 2], Ss[k2 % 2]
                        k2 += 1
                        P.dma("sp", C_[:], rope[0][:, s0:s0 + 512], writes=[C_.t()])
                        P.dma("sp", S_[:], rope[1][:, s0:s0 + 512], writes=[S_.t()])
                        P.op("dve", lambda e: e.tensor_tensor(x1[:], ps[:, 0:512], S_[:], ALU.mult),
                             reads=[ps.t(), S_.t()], writes=[x1.t()])
                        P.op("pool", lambda e: e.tensor_tensor(x2[:], buf[:, a:b], C_[:], ALU.mult),
                             reads=[buf.t(), C_.t()], writes=[x2.t()])
                        P.op("dve", lambda e: e.tensor_tensor(buf[:, a:b], x1[:], x2[:], ALU.add),
                             reads=[x1.t(), x2.t(), ps.t()], writes=[buf.t()])
                for d in range(2):
                  with P.scope():
                    sp_ = P.sb("sp", [128, NCH, 128], F32)
                    cum = P.sb("cum", [128, NT], F32)
                    zc = d * BQK + fc * 128
                    for ch in range(NCH):
                        ps = P.psum()
                        P.op("pe", lambda e: e.matmul(ps[:, 0:128], abT[:, ch * 128:(ch + 1) * 128], wg2s[:, zc:zc + 128], start=True, stop=False),
                             reads=[abT.t(), wg2s.t()], writes=[ps.t()], sig=False)
                        P.op("pe", lambda e: e.matmul(ps[:, 0:128], onesr[:], bgs[:, zc:zc + 128], start=False, stop=True),
                             reads=[onesr.t(), bgs.t()], writes=[ps.t()])
                        P.op("act", lambda e: e.activation(out=sp_[:, ch, :], in_=ps[:, 0:128], func=AF.Exp, scale=-1.0),
                             reads=[ps.t()], writes=[sp_.t(ch)])
                    for ch in range(NCH):
                        P.op("act", lambda e: e.activation(out=sp_[:, ch, :], in_=sp_[:, ch, :], func=AF.Ln, bias=1.0),
                             reads=[sp_.t(ch)], writes=[sp_.t(ch)])
                    for c4 in range(0, NCH, 4):
                        nn = min(4, NCH - c4)
                        ps = P.psum()
                        for u in range(nn):
                            P.op("pe", lambda e: e.matmul(ps[:, u * 128:(u + 1) * 128], sp_[:, c4 + u, :], tri[d][:], start=True, stop=True),
                                 reads=[sp_.t(c4 + u), tri[d].t()], writes=[ps.t()])
                        P.op("dve", lambda e: e.tensor_scalar(cum[:, c4 * 128:(c4 + nn) * 128], ps[:, 0:nn * 128], -1.0 / 16, None, ALU.mult),
                             reads=[ps.t()], writes=[cum.t(c4)])
                    allcum = [cum.t(c4) for c4 in range(0, NCH, 4)]
                    lastcol = 127 if d == 0 else 0
                    P.op("act", lambda e: e.activation(out=dec[:], in_=cum[:, lastcol:NT:128], func=AF.Exp),
                         reads=allcum, writes=[dec.t()])
                    for s0 in range(0, NT, 512):
                        n = min(512, NT - s0)
                        x1, x2 = t1[k2 % 2], t2[k2 % 2]
                        k2 += 1
                        P.op("act", lambda e: e.activation(out=x1[:, 0:n], in_=cum[:, s0:s0 + n], func=AF.Exp),
                             reads=allcum, writes=[x1.t()])
                        P.op("dve", lambda e: e.tensor_tensor(qtil[:, s0:s0 + n], qr[:, s0:s0 + n], x1[:, 0:n], ALU.mult),
                             reads=[qr.t(), x1.t()], writes=[qtil.t()])
                        P.op("act", lambda e: e.activation(out=x2[:, 0:n], in_=cum[:, s0:s0 + n], func=AF.Exp, scale=-1.0),
                             reads=allcum, writes=[x2.t()])
                        P.op("pool", lambda e: e.tensor_tensor(ktil[:, s0:s0 + n], kr[:, s0:s0 + n], x2[:, 0:n], ALU.mult),
                             reads=[kr.t(), x2.t()], writes=[ktil.t()])
                    for ch in range(NCH):
                        x1 = t1[k2 % 2]
                        khb = kh[k2 % 2]
                        k2 += 1
                        lc = ch * 128 + lastcol
                        P.op("act", lambda e: e.activation(out=x1[:, 0:128], in_=cum[:, ch * 128:(ch + 1) * 128], func=AF.Exp,
                                                           scale=-1.0, bias=cum[:, lc:lc + 1]),
                             reads=allcum, writes=[x1.t()])
                        P.op("dve", lambda e: e.tensor_tensor(khb[:], kr[:, ch * 128:(ch + 1) * 128], x1[:, 0:128], ALU.mult),
                             reads=[kr.t(), x1.t()], writes=[khb.t()])
                        P.op("pe", lambda e: e.transpose(P.psb[:, (ch % 8) * 128:(ch % 8 + 1) * 128], khb[:], identb[:]),
                             reads=[khb.t(), identb.t()], writes=[P.psb.t(ch % 8)])
                        P.op("act", lambda e: e.copy(khat[:, ch, :], P.psb[:, (ch % 8) * 128:(ch % 8 + 1) * 128]),
                             reads=[P.psb.t(ch % 8)], writes=[khat.t(ch)])
                  with P.scope():
                    aTa = P.sb("aTa", [128, NCH, 2, 128], BF16)
                    Sba = P.sb("Sba", [128, NCH, 128], BF16)
                    P.op("dve", lambda e: e.memset(St[:], 0.0), writes=[St.t()])
                    if d == 0:
                        order = list(range(NCH))
                    else:
                        order = list(range(NCC - 1, -1, -1)) + list(range(NCH - 1, NCC - 1, -1))
                    for ch in order:
                        a, b = ch * 128, (ch + 1) * 128
                        P.op("dve", lambda e: e.tensor_copy(Sba[:, ch, :], St[:]), reads=[St.t()], writes=[Sba.t(ch)])
                        ps_kv = P.psum()
                        P.op("pe", lambda e: e.matmul(ps_kv[:, 0:256], khat[:, ch, :], vv[:, ch, :], start=True, stop=True),
                             reads=[khat.t(ch), vv.t()], writes=[ps_kv.t()])
                        for hh in range(2):
                            pa, pb = hh * 64, (hh + 1) * 64
                            ps_a = P.psum()
                            P.op("pe", lambda e: e.matmul(ps_a[:, 0:128], ktil[pa:pb, a:b], qtil[pa:pb, a:b], start=True, stop=True),
                                 reads=[ktil.t(), qtil.t()], writes=[ps_a.t()])
                            P.op("dve", lambda e: e.tensor_tensor(aTa[:, ch, hh, :], ps_a[:, 0:128], tri[d][:], ALU.mult),
                                 reads=[ps_a.t(), tri[d].t()], writes=[aTa.t((ch, hh))])
                        for hh in range(2):
                            pa, pb = hh * 64, (hh + 1) * 64
                            P.op("dve", lambda e: e.scalar_tensor_tensor(St[pa:pb, :], St[pa:pb, :], dec[pa:pb, ch:ch + 1],
                                                                         ps_kv[pa:pb, hh * 128:(hh + 1) * 128], ALU.mult, ALU.add),
                                 reads=[St.t(), dec.t(), ps_kv.t()], writes=[St.t()])
                    for ch in order:
                        a, b = ch * 128, (ch + 1) * 128
                        for hh in range(2):
                            pa, pb = hh * 64, (hh + 1) * 64
                            ps_o = P.psum()
                            P.op("pe", lambda e: e.matmul(ps_o[:, 0:128], Sba[pa:pb, ch, :], qtil[pa:pb, a:b], start=True, stop=False),
                                 reads=[Sba.t(ch), qtil.t()], writes=[ps_o.t()], sig=False)
                            P.op("pe", lambda e: e.matmul(ps_o[:, 0:128], vv[:, ch, hh * 128:(hh + 1) * 128], aTa[:, ch, hh, :], start=False, stop=True),
                                 reads=[vv.t(), aTa.t((ch, hh))], writes=[ps_o.t()])
                            if d == 0:
                                P.op("act", lambda e: e.copy(ob[hh][:, a:b], ps_o[:, 0:128]), reads=[ps_o.t()], writes=[ob[hh].t(ch)])
                            else:
                                P.op("dve", lambda e: e.tensor_tensor(ob[hh][:, a:b], ob[hh][:, a:b], ps_o[:, 0:128], ALU.add),
                                     reads=[ps_o.t(), ob[hh].t(ch)], writes=[ob[hh].t(ch)])
                for hh in range(2):
                    hg = fc * 2 + hh
                    t_lo = 0 if need_ctx else CTX
                    for s0 in range(t_lo, NT, 512):
                        n = min(512, NT - s0)
                        chs = [ob[hh].t(ch) for ch in range(s0 // 128, (s0 + n) // 128)]
                        x1, x2 = t1[k2 % 2], t2[k2 % 2]
                        s_, y_ = sg[k2 % 2], yo[k2 % 2]
                        k2 += 1
                        P.dma("sp", s_[:, 0:n], GBs[hg * 128:(hg + 1) * 128, s0:s0 + n], writes=[s_.t()])
                        P.op("dve", lambda e: e.tensor_tensor(x1[:, 0:n], ob[hh][:, s0:s0 + n], ob[hh][:, s0:s0 + n], ALU.mult),
                             reads=chs, writes=[x1.t()])
                        ps = P.psum()
                        P.op("pe", lambda e: e.matmul(ps[:, 0:n], ones128[:], x1[:, 0:n], start=True, stop=True),
                             reads=[ones128.t(), x1.t()], writes=[ps.t()])
                        P.op("act", lambda e: e.activation(out=x2[:, 0:n], in_=ps[:, 0:n], func=AF.Ln, bias=EPS),
                             reads=[ps.t()], writes=[x2.t()])
                        P.op("act", lambda e: e.activation(out=x2[:, 0:n], in_=x2[:, 0:n], func=AF.Exp, scale=-0.5),
                             reads=[x2.t()], writes=[x2.t()])
                        P.op("dve", lambda e: e.tensor_tensor(x1[:, 0:n], ob[hh][:, s0:s0 + n], x2[:, 0:n], ALU.mult),
                             reads=chs + [x2.t(), x1.t()], writes=[x1.t()])
                        P.op("dve", lambda e: e.scalar_tensor_tensor(y_[:, 0:n], x1[:, 0:n], gg[:, hg:hg + 1], s_[:, 0:n], ALU.mult, ALU.mult),
                             reads=[x1.t(), gg.t(), s_.t()], writes=[y_.t()])
                        P.dma("sp", OT[AW + hg * 128:AW + (hg + 1) * 128, s0:s0 + n], y_[:, 0:n], reads=[y_.t()])

    def e_out(i, e_, src, dst, tok0, ntok, g):
        TB = min(1024, ntok)
        with P.scope():
            ob_ = P.sb("otb", [128, KC, TB], BF16)
            wt = [P.sb("wo", [128, KC, 512], BF16) for _ in range(2)]
            xo = [P.sb("xo", [128, 512], F32) for _ in range(2)]
            xn = [P.sb("xn", [128, 512], F32) for _ in range(2)]
            kk = 0
            for b0 in range(0, ntok, TB):
                tb = min(TB, ntok - b0)
                P.dma("sp", ob_[:, :, 0:tb], OT[:, tok0 + b0:tok0 + b0 + tb].rearrange("(kc p) n -> p kc n", p=128), writes=[ob_.t()])
                for cg in range(D // 512):
                    w = wt[cg % 2]
                    P.dma("pool", w[:], w_out[e_, :, cg * 512:(cg + 1) * 512].rearrange("(kc p) n -> p kc n", p=128), writes=[w.t()])
                    for j in range(4):
                        oc = cg * 4 + j
                        for s0 in range(0, tb, 512):
                            n = min(512, tb - s0)
                            k = kk % 2
                            kk += 1
                            P.dma("sp", xo[k][:, 0:n], src[oc * 128:(oc + 1) * 128, b0 + s0:b0 + s0 + n], writes=[xo[k].t()])
                            ps = P.psum()
                            for kc in range(KC):
                                P.op("pe", lambda e: e.matmul(ps[:, 0:n], w[:, kc, j * 128:(j + 1) * 128], ob_[:, kc, s0:s0 + n],
                                                              start=(kc == 0), stop=(kc == KC - 1)),
                                     reads=[w.t(), ob_.t()], writes=[ps.t()], sig=(kc == KC - 1))
                            P.op("dve", lambda e: e.scalar_tensor_tensor(xn[k][:, 0:n], ps[:, 0:n], mv(i, 2, oc, g), xo[k][:, 0:n], ALU.mult, ALU.add),
                                 reads=[ps.t(), xo[k].t(), modv.t()], writes=[xn[k].t()])
                            P.dma("sp", dst[oc * 128:(oc + 1) * 128, b0 + s0:b0 + s0 + n], xn[k][:, 0:n], reads=[xn[k].t()])

    def st_even(i, xsrc, csrc, xdst, cdst, need_ctx, nxt):
        e_ = i // 2
        e_proj(i, e_, csrc, 0, CTX, 1)
        e_proj(i, e_, xsrc, CTX, SEQ, 0)
        e_na(e_, need_ctx, nxt)
        e_gla(e_, need_ctx)
        e_out(i, e_, xsrc, xdst, CTX, SEQ, 0)
        if need_ctx:
            e_out(i, e_, csrc, cdst, 0, CTX, 1)

    st_modvec(0)
    xsrc, csrc = xT, cT
    for i in range(DEPTH):
        is_even = i % 2 == 0
        need_ctx = any(j % 2 == 0 for j in range(i + 1, DEPTH))
        nxt = i + 1 if i + 1 < DEPTH else None
        used = False
        if is_even and fl["even"]:
            st_even(i, xsrc, csrc, X[1], C[1], need_ctx, nxt)
            used = True
        elif (not is_even) and fl["odd"]:
            st_pool(i, xsrc, X[1], SEQ, 0, None)
            if need_ctx:
                st_pool(i, csrc, C[1], CTX, 1, None)
        else:
            st_copy(xsrc, X[1], SEQ)
            if need_ctx:
                st_copy(csrc, C[1], CTX)
        if nxt is not None and not used:
            st_modvec(nxt)
        if fl["ffn"]:
            st_ffn(i, X[1], X[0], SEQ, 0)
            if need_ctx:
                st_ffn(i, C[1], C[0], CTX, 1)
        else:
            st_copy(X[1], X[0], SEQ)
            if need_ctx:
                st_copy(C[1], C[0], CTX)
        xsrc, csrc = X[0], C[0]
    st_final(xsrc)
    P.barrier()
    return nc


def _fm(v, KC):
    return np.ascontiguousarray(np.asarray(v, np.float32).reshape(KC, 128).T)


def prep_shared(cfg, inp):
    c = cfg
    KC, FC, DEPTH = c.KC, c.FC, c.DEPTH
    f = lambda a: np.ascontiguousarray(np.asarray(a, np.float32))
    sh = {}
    sh["w_mod"] = f(inp["w_mod"])
    sh["b_modT"] = f(np.stack([_fm(inp["b_mod"][i], 6 * KC) for i in range(DEPTH)], 1).reshape(128, -1))
    ngs = np.stack([np.stack([_fm(inp["norm1_g"][i], KC), _fm(inp["norm2_g"][i], KC)], 1) for i in range(DEPTH)], 1)
    sh["ngT"] = f(ngs.reshape(128, -1))
    sh["fgT"] = _fm(inp["final_g"], KC)
    sh["w_up"] = f(inp["w_up"])
    sh["w_down"] = f(inp["w_down"])
    cw = np.asarray(inp["conv_w"], np.float32)
    cb = np.asarray(inp["conv_b"], np.float32)
    cvt = np.zeros((128, DEPTH, FC, 4), np.float32)
    for i in range(DEPTH):
        for k in range(3):
            cvt[:, i, :, k] = _fm(cw[i, k], FC)
        cvt[:, i, :, 3] = _fm(cb[i], FC)
    sh["convT"] = f(cvt.reshape(128, -1))
    cm = np.zeros((8, 128, 128), np.float32)
    cm[0] = 1.0 / c.D
    cm[1] = 1.0 / 128
    cm[2] = 1.0
    cm[3] = np.eye(128)
    p = np.arange(128)
    perm = np.where((p % 32) < 16, p + 16, p - 16)
    cm[4][perm, p] = 1.0
    jj, ii = np.meshgrid(p, p, indexing="ij")
    cm[5] = np.where(jj <= ii, 1.0, 0.0)
    cm[6] = np.where(jj >= ii, 1.0, 0.0)
    sh["cmat"] = cm
    sh["pool_w"] = f(inp["pool_w"])
    NE, BQK, NAH, NBLK = c.NE, c.BQK, c.NAH, c.NBLK
    sh["w_in"] = f(inp["w_in"])
    sh["w_out"] = f(inp["w_out"])
    wg = np.zeros((NE, 32, 2 * BQK), np.float32)
    w2 = np.asarray(inp["w_gate2"], np.float32)
    for d in range(2):
        wg[:, d * 16:(d + 1) * 16, d * BQK:(d + 1) * BQK] = w2[:, d]
    sh["wg2"] = wg
    sh["bg"] = f(np.asarray(inp["b_gate"], np.float32).reshape(NE, 1, 2 * BQK))
    rp = np.asarray(inp["rpb"], np.float32)
    pp = np.arange(128)
    half = pp // 64
    kcol = pp % 64
    bb = np.arange(NBLK)
    qcol = np.arange(64)
    e_idx = bb[None, :] - 4 - half[:, None]
    ev = (e_idx >= 0) & (e_idx <= 14)
    dr = np.clip(14 - e_idx, 0, 14)
    dc = np.clip(kcol[:, None] - qcol[None, :] + 15, 0, 30)
    rx = rp[:, :, dr[:, :, None], dc[:, None, :]]
    rx = np.where(ev[None, None, :, :, None], rx, 0.0).astype(np.float32)
    sh["rpbx"] = f(rx.reshape(NE, NAH, 128, NBLK * 64))
    cstart = np.clip(qcol - 8, 0, 48)
    col_in = (kcol[:, None] >= cstart[None, :]) & (kcol[:, None] < cstart[None, :] + 16)
    mF = (ev[:, :, None] & col_in[:, None, :])
    mZ = mF & ((e_idx >= 4) & (e_idx <= 11))[:, :, None]
    sh["nmask"] = f(np.stack([mF, mZ], 0).astype(np.float32).reshape(2, 128, NBLK * 64))
    t = np.arange(c.SEQ)
    row = (t // 64).astype(np.float32)
    colp = (t % 64).astype(np.float32)
    dd = pp % 64
    ii = dd % 32
    inv = (np.float32(10000.0) ** (-(np.arange(16, dtype=np.float32)) / np.float32(16))).astype(np.float32)
    pos = np.where((dd // 32)[:, None] == 0, row[None, :], colp[None, :]).astype(np.float32)
    ang = (pos * inv[ii % 16][:, None]).astype(np.float32)
    sgn = np.where(ii < 16, -1.0, 1.0).astype(np.float32)
    sh["rope"] = f(np.stack([np.cos(ang), np.sin(ang) * sgn[:, None]], 0))
    gg = np.asarray(inp["gla_norm_g"], np.float32)
    sh["gla_gT"] = f(gg.transpose(2, 0, 1).reshape(128, -1))
    sh["pool_scT"] = f(np.stack([_fm(inp["pool_scale"][o], KC) for o in range(c.NO)], 1).reshape(128, -1))
    def icnt(L):
        t = np.arange(L)
        out = np.zeros((4, L), np.float32)
        for wi, w in enumerate((2, 4, 8, 16)):
            lo = np.clip(t - w // 2, 0, L); hi = np.clip(t + w // 2, 0, L)
            out[wi] = 1.0 / (hi - lo).astype(np.float32)
        return out
    sh["icnt_l"] = icnt(c.SEQ)
    sh["icnt_c"] = icnt(c.CTX)
    return sh


def prep_core(cfg, inp, b, zero=False):
    c = cfg
    d = {}
    x = np.asarray(inp["x"][b], np.float32)
    cx = np.asarray(inp["ctx"][b], np.float32)
    cvec = np.stack([_fm(inp["c"][b], c.KC), _fm(inp["c_ctx"], c.KC)], 2).reshape(128, -1)
    if zero:
        d["xT"] = np.zeros((c.D, c.SEQ), np.float32)
        d["cT"] = np.zeros((c.D, c.CTX), np.float32)
        d["cv"] = np.zeros_like(cvec)
    else:
        d["xT"] = np.ascontiguousarray(x.T)
        d["cT"] = np.ascontiguousarray(cx.T)
        d["cv"] = np.ascontiguousarray(cvec)
    return d


def run(cfg, inp, flags=None):
    nc = build(cfg, flags)
    sh = prep_shared(cfg, inp)
    names = set()
    in_maps = []
    ncore = 8
    for k in range(ncore):
        b = (k // 2) % cfg.B
        m = dict(sh)
        m.update(prep_core(cfg, inp, b, zero=(k % 2 == 1 or k // 2 >= cfg.B)))
        in_maps.append(m)
    res = run_bass_kernel_spmd(nc, in_maps, core_ids=list(range(ncore)))
    out = np.stack([np.ascontiguousarray(res.results[2 * b]["outT"].T) for b in range(cfg.B)], 0)
    return out.astype(np.float32)


def kernel(**inputs):
    return run(Cfg(), inputs)
```

```python
import contextlib
import numpy as np
import concourse.bass as bass
import concourse.mybir as mybir
from concourse.bass_utils import run_bass_kernel_spmd

F32, BF16 = mybir.dt.float32, mybir.dt.bfloat16
AF = mybir.ActivationFunctionType
ALU = mybir.AluOpType
ND = 6
EPS = 1e-6


class Cfg:
    def __init__(s, D=2048, SEQ=4096, CTX=256, DEPTH=4, DFF=5632, TBF=2048, B=4, TBE=1024):
        s.D, s.SEQ, s.CTX, s.DEPTH, s.DFF, s.TBF, s.B = D, SEQ, CTX, DEPTH, DFF, TBF, B
        s.KC = D // 128
        s.TBE = TBE
        s.NH = D // 128
        s.NAH = s.NH // 2
        s.GH = s.NH - s.NAH
        s.AW, s.BQK, s.BV = s.NAH * 128, s.GH * 64, s.GH * 128
        s.DIN = 3 * s.AW + 2 * s.BQK + 2 * s.BV + 32
        s.ROWS = SEQ // 64
        s.PG = D // 4
        s.FC = DFF // 128
        s.NE = (DEPTH + 1) // 2
        s.NO = DEPTH // 2
        s.NBLK = 23


def _split(a, b, mx=256):
    n = b - a
    k = (n + mx - 1) // mx
    out, t = [], a
    for i in range(k):
        m = n // k + (1 if i < n % k else 0)
        out.append((t, m))
        t += m
    return out


class Tk:
    __slots__ = ("w", "r")

    def __init__(s):
        s.w = None
        s.r = {}


class Buf:
    def __init__(s, h):
        s.h = h
        s.tks = {}

    def t(s, key=0):
        if key not in s.tks:
            s.tks[key] = Tk()
        return s.tks[key]

    def seg(s, kc, ca, cb, g=64):
        return [s.t((kc, q)) for q in range(ca // g, (cb - 1) // g + 1)]

    def __getitem__(s, k):
        return s.h[k]


class Prog:
    def __init__(s, nc):
        s.nc = nc
        s.eng = {"pe": nc.tensor, "act": nc.scalar, "dve": nc.vector, "pool": nc.gpsimd, "sp": nc.sync}
        s.sem = {k: nc.alloc_semaphore("s_" + k) for k in s.eng}
        s.cnt = {k: 0 for k in s.eng}
        s.pend = False
        s.waited = {k: {} for k in s.eng}
        s.dsem = {q: [nc.alloc_semaphore("d_%s%d" % (q, i)) for i in range(ND)] for q in ("sp", "pool", "act")}
        s.dcnt = {q: [0] * ND for q in s.dsem}
        s.dnext = {q: 0 for q in s.dsem}
        s.nps = 0
        s.nrot = 7
        s.ps = [Buf(nc.alloc_psum_tensor("ps%d" % i, [128, 512], F32)) for i in range(7)]
        s.psb = Buf(nc.alloc_psum_tensor("psb", [128, 1024], BF16))
        s.stack = None
        s.nalloc = 0

    @contextlib.contextmanager
    def scope(s):
        old = s.stack
        with contextlib.ExitStack() as st:
            s.stack = st
            yield
            s.barrier()
        s.stack = old

    def sb(s, name, shape, dt):
        s.nalloc += 1
        nm = "%s_%d" % (name, s.nalloc)
        if s.stack is None:
            return Buf(s.nc.alloc_sbuf_tensor(nm, list(shape), dt))
        return Buf(s.stack.enter_context(s.nc.sbuf_tensor(nm, list(shape), dt)))

    def psum(s):
        b = s.ps[s.nps % s.nrot]
        s.nps += 1
        return b

    def _wait(s, e, tok):
        if tok is None:
            return
        key, sem, val = tok
        if e == "pe" and key == "pe":
            return
        if s.waited[e].get(key, -1) >= val:
            return
        s.eng[e].wait_ge(sem, val)
        s.waited[e][key] = val

    def _deps(s, e, reads, writes):
        for t in reads:
            s._wait(e, t.w)
        for t in writes:
            s._wait(e, t.w)
            for r in t.r.values():
                s._wait(e, r)

    def _mark(s, tok, reads, writes):
        for t in reads:
            t.r[tok[0]] = tok
        for t in writes:
            t.w = tok
            t.r = {}

    def op(s, e, fn, reads=(), writes=(), sig=True):
        s._deps(e, reads, writes)
        ins = fn(s.eng[e])
        if e == "pe" and not sig:
            tok = ("pe", s.sem["pe"], s.cnt["pe"] + 1)
            s.pend = True
        else:
            s.cnt[e] += 1
            ins.then_inc(s.sem[e], 1)
            tok = (e, s.sem[e], s.cnt[e])
            if e == "pe":
                s.pend = False
        s._mark(tok, reads, writes)
        return tok

    def dma(s, q, out, in_, reads=(), writes=(), **kw):
        i = s.dnext[q]
        s.dnext[q] = (i + 1) % ND
        sem = s.dsem[q][i]
        key = "d_%s%d" % (q, i)
        if s.dcnt[q][i] > 0:
            s._wait(q, (key, sem, s.dcnt[q][i]))
        s._deps(q, reads, writes)
        ins = s.eng[q].dma_start(out=out, in_=in_, **kw)
        s.dcnt[q][i] += 16
        ins.then_inc(sem, 16)
        tok = (key, sem, s.dcnt[q][i])
        s._mark(tok, reads, writes)
        return tok

    def barrier(s):
        assert not s.pend
        toks = [(k, s.sem[k], s.cnt[k]) for k in s.eng if s.cnt[k] > 0]
        for q in s.dsem:
            for i in range(ND):
                if s.dcnt[q][i] > 0:
                    toks.append(("d_%s%d" % (q, i), s.dsem[q][i], s.dcnt[q][i]))
        for e in s.eng:
            for tok in toks:
                s._wait(e, tok)


def build(cfg, flags=None):
    fl = dict(even=True, odd=True, ffn=True)
    if flags:
        fl.update(flags)
    c = cfg
    D, SEQ, CTX, DEPTH, DFF, KC, FC = c.D, c.SEQ, c.CTX, c.DEPTH, c.DFF, c.KC, c.FC
    nc = bass.Bass("TRN2", target_bir_lowering=False)
    P = Prog(nc)

    def din(name, shape):
        return nc.dram_tensor(name, list(shape), F32, kind="ExternalInput").ap()

    def dscr(name, shape, dt=F32):
        return nc.dram_tensor(name, list(shape), dt).ap()

    xT = din("xT", [D, SEQ])
    cT = din("cT", [D, CTX])
    cv = din("cv", [128, KC * 2])
    w_mod = din("w_mod", [DEPTH, D, 6 * D])
    b_modT = din("b_modT", [128, DEPTH * 6 * KC])
    ngT = din("ngT", [128, DEPTH * 2 * KC])
    fgT = din("fgT", [128, KC])
    w_up = din("w_up", [DEPTH, D, 2 * DFF])
    w_down = din("w_down", [DEPTH, DFF, D])
    convT = din("convT", [128, DEPTH * FC * 4])
    pool_w = din("pool_w", [c.NO, 4, c.PG, c.PG])
    pool_scT = din("pool_scT", [128, c.NO * KC])
    icnt_l = din("icnt_l", [4, SEQ])
    icnt_c = din("icnt_c", [4, CTX])
    w_in = din("w_in", [c.NE, D, c.DIN])
    w_out = din("w_out", [c.NE, D, D])
    wg2 = din("wg2", [c.NE, 32, 2 * c.BQK])
    bg = din("bg", [c.NE, 1, 2 * c.BQK])
    rpbx = din("rpbx", [c.NE, c.NAH, 128, c.NBLK * 64])
    nmask = din("nmask", [2, 128, c.NBLK * 64])
    rope = din("rope", [2, 128, SEQ])
    gla_gT = din("gla_gT", [128, c.NE * c.GH])
    cmat = din("cmat", [8, 128, 128])
    outT = nc.dram_tensor("outT", [D, SEQ], F32, kind="ExternalOutput").ap()
    X = [dscr("X0", [D, SEQ]), dscr("X1", [D, SEQ])]
    C = [dscr("C0", [D, CTX]), dscr("C1", [D, CTX])]

    onesD = P.sb("onesD", [128, 128], BF16)
    silc = P.sb("silc", [128, KC * 2], BF16)
    cvs = P.sb("cvs", [128, KC * 2], F32)
    modv = P.sb("modv", [128, DEPTH * 6 * KC * 2], F32)
    bmod = P.sb("bmod", [128, DEPTH * 6 * KC], F32)
    ng = P.sb("ng", [128, DEPTH * 2 * KC], F32)
    fg = P.sb("fg", [128, KC], F32)
    conv = P.sb("conv", [128, DEPTH * FC * 4], F32)
    P.dma("pool", onesD[:], cmat[0], writes=[onesD.t()])
    P.dma("sp", cvs[:], cv, writes=[cvs.t()])
    P.dma("sp", bmod[:], b_modT, writes=[bmod.t()])
    P.dma("sp", ng[:], ngT, writes=[ng.t()])
    P.dma("sp", fg[:], fgT, writes=[fg.t()])
    P.dma("sp", conv[:], convT, writes=[conv.t()])
    P.op("act", lambda e: e.activation(out=silc[:], in_=cvs[:], func=AF.Silu), reads=[cvs.t()], writes=[silc.t()])

    def mvcol(i, v, kc, g):
        return ((i * 6 + v) * KC + kc) * 2 + g

    def mv(i, v, kc, g):
        cidx = mvcol(i, v, kc, g)
        return modv[:, cidx:cidx + 1]

    def modvec_gen(i, gw, nbuf):
        wt = [P.sb("mw", [128, KC, gw], BF16) for _ in range(nbuf)]
        mvt = [P.sb("mvt", [2, gw], F32) for _ in range(2)]
        i2 = P.sb("i2", [2, 2], F32)
        P.dma("sp", i2[:], cmat[3][0:2, 0:2], writes=[i2.t()])
        noc = 6 * KC
        base = i * noc * 2
        ngrp = 6 * D // gw
        no = gw // 128
        for g in range(ngrp):
            w = wt[g % nbuf]
            m_ = mvt[g % 2]
            P.dma("pool", w[:], w_mod[i, :, g * gw:(g + 1) * gw].rearrange("(kc p) n -> p kc n", p=128), writes=[w.t()])
            pg = P.psum()
            for kc in range(KC):
                P.op("pe", lambda e: e.matmul(pg[0:2, 0:gw], silc[:, kc * 2:kc * 2 + 2], w[:, kc, :],
                                              start=(kc == 0), stop=(kc == KC - 1)),
                     reads=[w.t(), silc.t()], writes=[pg.t()], sig=(kc == KC - 1))
            P.op("act", lambda e: e.copy(m_[:, 0:gw], pg[0:2, 0:gw]), reads=[pg.t()], writes=[m_.t()])
            pt = P.psum()
            for j in range(no):
                P.op("pe", lambda e: e.matmul(pt[:, j * 2:j * 2 + 2], m_[:, j * 128:(j + 1) * 128], i2[:], start=True, stop=True),
                     reads=[m_.t(), i2.t()], writes=[pt.t()], sig=(j == no - 1))
            oc0 = g * no
            P.op("dve", lambda e: e.tensor_tensor(modv[:, base + oc0 * 2:base + (oc0 + no) * 2].rearrange("p (o t) -> p o t", t=2),
                                                  pt[:, 0:no * 2].rearrange("p (o t) -> p o t", t=2),
                                                  bmod[:, i * noc + oc0:i * noc + oc0 + no].unsqueeze(2).to_broadcast([128, no, 2]),
                                                  ALU.add),
                 reads=[pt.t(), bmod.t()], writes=[modv.t()])
            yield
        for (v, which) in ((1, 0), (4, 1)):
            for g in range(2):
                b0 = mvcol(i, v, 0, g)
                sl = modv[:, b0:b0 + 2 * KC:2]
                P.op("dve", lambda e: e.scalar_tensor_tensor(sl, sl, 1.0, ng[:, (i * 2 + which) * KC:(i * 2 + which + 1) * KC],
                                                             ALU.add, ALU.mult),
                     reads=[modv.t(), ng.t()], writes=[modv.t()])

    def st_modvec(i):
        with P.scope():
            for _ in modvec_gen(i, 512, 2):
                pass

    class Stepper:
        def __init__(s_, gen, nsteps, total):
            s_.gen, s_.per = gen, (total + nsteps - 1) // nsteps

        def step(s_):
            if s_.gen is None:
                return
            for _ in range(s_.per):
                try:
                    next(s_.gen)
                except StopIteration:
                    s_.gen = None
                    return

        def finish(s_):
            while s_.gen is not None:
                s_.step()

    def rms_rstd(xs, n, rs, sqs):
        ps = P.psum()
        P.op("dve", lambda e: e.tensor_tensor(sqs[:, :, 0:n], xs[:, :, 0:n], xs[:, :, 0:n], ALU.mult),
             reads=[xs.t()], writes=[sqs.t()])
        for kc in range(KC):
            P.op("pe", lambda e: e.matmul(ps[:, 0:n], onesD[:], sqs[:, kc, 0:n], start=(kc == 0), stop=(kc == KC - 1)),
                 reads=[sqs.t(), onesD.t()], writes=[ps.t()], sig=(kc == KC - 1))
        P.op("act", lambda e: e.activation(out=rs[:, 0:n], in_=ps[:, 0:n], func=AF.Ln, bias=EPS),
             reads=[ps.t()], writes=[rs.t()])
        P.op("act", lambda e: e.activation(out=rs[:, 0:n], in_=rs[:, 0:n], func=AF.Exp, scale=-0.5),
             reads=[rs.t()], writes=[rs.t()])

    def norm_mod(src, t0, n, dstbuf, c0, wk, i, va, vs, g, xs, rs, sqs, tmps):
        P.dma("sp", xs[:, :, 0:n], src[:, t0:t0 + n].rearrange("(kc p) n -> p kc n", p=128), writes=[xs.t()])
        rms_rstd(xs, n, rs, sqs)
        for kc in range(KC):
            tmp = tmps[kc % 2]
            P.op("dve", lambda e: e.tensor_tensor(tmp[:, 0:n], xs[:, kc, 0:n], rs[:, 0:n], ALU.mult),
                 reads=[xs.t(), rs.t()], writes=[tmp.t()])
            P.op("act", lambda e: e.activation(out=dstbuf[:, kc, c0:c0 + n], in_=tmp[:, 0:n], func=AF.Identity,
                                               bias=mv(i, vs, kc, g), scale=mv(i, va, kc, g)),
                 reads=[tmp.t(), modv.t()], writes=wk(kc))

    def norm_gen(src, subs, dstbuf, wkf, i, va, vs, g, nb):
        def stage1(k):
            t0, n, c0 = subs[k]
            xs, sq, rs = nb["xs"][k % 3], nb["sq"][k % 3], nb["rs"][k % 3]
            P.dma("sp", xs[:, :, 0:n], src[:, t0:t0 + n].rearrange("(kc p) n -> p kc n", p=128), writes=[xs.t()])
            rms_rstd(xs, n, rs, sq)

        def stage2(k):
            t0, n, c0 = subs[k]
            xs, rs = nb["xs"][k % 3], nb["rs"][k % 3]
            ba, bs = mvcol(i, va, 0, g), mvcol(i, vs, 0, g)
            rb = rs[:, 0:n].unsqueeze(1).to_broadcast([128, KC, n])
            ab = modv[:, ba:ba + 2 * KC:2].unsqueeze(2).to_broadcast([128, KC, n])
            sb_ = modv[:, bs:bs + 2 * KC:2].unsqueeze(2).to_broadcast([128, KC, n])
            P.op("dve", lambda e: e.tensor_tensor(xs[:, :, 0:n], xs[:, :, 0:n], rb, ALU.mult),
                 reads=[xs.t(), rs.t()], writes=[xs.t()])
            P.op("dve", lambda e: e.tensor_tensor(xs[:, :, 0:n], xs[:, :, 0:n], ab, ALU.mult),
                 reads=[xs.t(), modv.t()], writes=[xs.t()])
            P.op("dve", lambda e: e.tensor_tensor(dstbuf[:, :, c0:c0 + n], xs[:, :, 0:n], sb_, ALU.add),
                 reads=[xs.t(), modv.t()], writes=[t_ for kc in range(KC) for t_ in wkf(kc, c0, n)])
        if not subs:
            return
        stage1(0)
        if len(subs) > 1:
            stage1(1)
        for k in range(len(subs)):
            if k + 2 < len(subs):
                stage1(k + 2)
            stage2(k)
            yield

    def norm_run(*a_, **k_):
        for _ in norm_gen(*a_, **k_):
            pass

    def norm_bufs(ns):
        return dict(xs=[P.sb("xs", [128, KC, ns], F32) for _ in range(3)],
                    sq=[P.sb("sq", [128, KC, ns], BF16) for _ in range(3)],
                    rs=[P.sb("rs", [128, ns], F32) for _ in range(3)],
                    tmp=[])

    ACTS = dscr("ACTS", [FC, 128, SEQ], BF16)
    NSUB = 96

    def st_ffn(i, src, dst, ntok, g):
        PT = min(c.TBF, ntok)
        with P.scope():
            h2 = P.sb("h2", [128, KC, PT + 2], BF16)
            gsb = P.sb("gsb", [128, PT + 2], F32)
            cvt = P.sb("cvt", [128, PT], F32)
            val = [P.sb("val", [128, PT], BF16) for _ in range(2)]
            aj = [P.sb("aj", [128, PT], BF16) for _ in range(2)]
            nb = norm_bufs(NSUB)
            wv = [P.sb("wv", [128, KC, 512], BF16) for _ in range(2)]
            wg = [P.sb("wg", [128, KC, 512], BF16) for _ in range(2)]
            for b0 in range(0, ntok, PT):
                tb = min(PT, ntok - b0)
                lo, hi = b0 - 1, b0 + tb + 1
                if lo < 0:
                    for kc in range(KC):
                        P.op("pool", lambda e: e.memset(h2[:, kc, 0:1], 0.0), writes=h2.seg(kc, 0, 1))
                if hi > ntok:
                    for kc in range(KC):
                        P.op("pool", lambda e: e.memset(h2[:, kc, tb + 1:tb + 2], 0.0), writes=h2.seg(kc, tb + 1, tb + 2))
                a, bnd = max(lo, 0), min(hi, ntok)
                norm_run(src, [(t0, n, t0 - lo) for (t0, n) in _split(a, bnd, NSUB)], h2,
                         (lambda kc, c0, n: h2.seg(kc, c0, c0 + n)), i, 4, 3, g, nb)
                nsub = (tb + 511) // 512
                for jg in range((FC + 3) // 4):
                    nj = min(4, FC - jg * 4)
                    wvj, wgj = wv[jg % 2], wg[jg % 2]
                    P.dma("pool", wvj[:, :, 0:nj * 128], w_up[i, :, jg * 512:jg * 512 + nj * 128].rearrange("(kc p) n -> p kc n", p=128),
                          writes=[wvj.t()])
                    P.dma("pool", wgj[:, :, 0:nj * 128], w_up[i, :, DFF + jg * 512:DFF + jg * 512 + nj * 128].rearrange("(kc p) n -> p kc n", p=128),
                          writes=[wgj.t()])
                    for jj in range(nj):
                        j = jg * 4 + jj
                        ws = slice(jj * 128, (jj + 1) * 128)
                        vj, ajj = val[j % 2], aj[j % 2]
                        for s in range(nsub):
                            n = min(512, tb - s * 512)
                            c0 = 1 + s * 512
                            psv = P.psum()
                            for kc in range(KC):
                                P.op("pe", lambda e: e.matmul(psv[:, 0:n], wvj[:, kc, ws], h2[:, kc, c0:c0 + n],
                                                              start=(kc == 0), stop=(kc == KC - 1)),
                                     reads=[wvj.t()] + h2.seg(kc, c0, c0 + n), writes=[psv.t()], sig=(kc == KC - 1))
                            P.op("act", lambda e: e.copy(vj[:, s * 512:s * 512 + n], psv[:, 0:n]), reads=[psv.t()],
                                 writes=[vj.t(s)])
                            psg = P.psum()
                            for kc in range(KC):
                                P.op("pe", lambda e: e.matmul(psg[:, 0:n], wgj[:, kc, ws], h2[:, kc, c0:c0 + n],
                                                              start=(kc == 0), stop=(kc == KC - 1)),
                                     reads=[wgj.t()] + h2.seg(kc, c0, c0 + n), writes=[psg.t()], sig=(kc == KC - 1))
                            P.op("act", lambda e: e.copy(gsb[:, c0:c0 + n], psg[:, 0:n]), reads=[psg.t()],
                                 writes=[gsb.t(s)])
                        psh = P.psum()
                        for kc in range(KC):
                            P.op("pe", lambda e: e.matmul(psh[:, 0:2], wgj[:, kc, ws], h2[:, kc, 0:tb + 2:tb + 1],
                                                          start=(kc == 0), stop=(kc == KC - 1)),
                                 reads=[wgj.t()] + h2.seg(kc, 0, 1) + h2.seg(kc, tb + 1, tb + 2), writes=[psh.t()],
                                 sig=(kc == KC - 1))
                        P.op("act", lambda e: e.copy(gsb[:, 0:tb + 2:tb + 1], psh[:, 0:2]), reads=[psh.t()],
                             writes=[gsb.t("h")])
                        cb = (i * FC + j) * 4
                        allg = [gsb.t(s) for s in range(nsub)] + [gsb.t("h")]
                        P.op("dve", lambda e: e.tensor_scalar(cvt[:, 0:tb], gsb[:, 0:tb], conv[:, cb:cb + 1], None, ALU.mult),
                             reads=allg + [conv.t()], writes=[cvt.t()])
                        P.op("dve", lambda e: e.scalar_tensor_tensor(cvt[:, 0:tb], gsb[:, 1:tb + 1], conv[:, cb + 1:cb + 2],
                                                                     cvt[:, 0:tb], ALU.mult, ALU.add),
                             reads=allg + [cvt.t()], writes=[cvt.t()])
                        P.op("dve", lambda e: e.scalar_tensor_tensor(cvt[:, 0:tb], gsb[:, 2:tb + 2], conv[:, cb + 2:cb + 3],
                                                                     cvt[:, 0:tb], ALU.mult, ALU.add),
                             reads=allg + [cvt.t()], writes=[cvt.t()])
                        P.op("act", lambda e: e.activation(out=cvt[:, 0:tb], in_=cvt[:, 0:tb], func=AF.Gelu,
                                                           bias=conv[:, cb + 3:cb + 4]),
                             reads=[cvt.t(), conv.t()], writes=[cvt.t()])
                        P.op("dve", lambda e: e.tensor_tensor(ajj[:, 0:tb], cvt[:, 0:tb], vj[:, 0:tb], ALU.mult),
                             reads=[cvt.t()] + [vj.t(s) for s in range(nsub)], writes=[ajj.t()])
                        P.dma("sp", ACTS[j, :, b0:b0 + tb], ajj[:, 0:tb], reads=[ajj.t()])
        TB = min(1024, ntok)
        with P.scope():
            act = P.sb("actT", [128, FC, TB], BF16)
            wd = [P.sb("wd", [128, FC, 512], BF16) for _ in range(2)]
            xo = [P.sb("xo", [128, 512], F32) for _ in range(2)]
            xn = [P.sb("xn", [128, 512], F32) for _ in range(2)]
            kk = 0
            nwd = 0
            for b0 in range(0, ntok, TB):
                tb = min(TB, ntok - b0)
                nsub = (tb + 511) // 512
                for jq in range(0, FC, 11):
                    je = min(jq + 11, FC)
                    P.dma("sp", act[:, jq:je, 0:tb], ACTS[jq:je, :, b0:b0 + tb].rearrange("j p n -> p j n"),
                          writes=[act.t(jq)])
                for ocg in range(D // 512):
                    wdo = wd[nwd % 2]
                    nwd += 1
                    P.dma("pool", wdo[:], w_down[i, :, ocg * 512:(ocg + 1) * 512].rearrange("(fc p) n -> p fc n", p=128),
                          writes=[wdo.t()])
                    for o4 in range(4):
                        oc = ocg * 4 + o4
                        for s in range(nsub):
                            n = min(512, tb - s * 512)
                            k = kk % 2
                            kk += 1
                            P.dma("sp", xo[k][:, 0:n], src[oc * 128:(oc + 1) * 128, b0 + s * 512:b0 + s * 512 + n],
                                  writes=[xo[k].t()])
                            ps = P.psum()
                            for j in range(FC):
                                P.op("pe", lambda e: e.matmul(ps[:, 0:n], wdo[:, j, o4 * 128:(o4 + 1) * 128], act[:, j, s * 512:s * 512 + n],
                                                              start=(j == 0), stop=(j == FC - 1)),
                                     reads=[wdo.t(), act.t((j // 11) * 11)], writes=[ps.t()], sig=(j == FC - 1))
                            P.op("dve", lambda e: e.scalar_tensor_tensor(xn[k][:, 0:n], ps[:, 0:n], mv(i, 5, oc, g),
                                                                         xo[k][:, 0:n], ALU.mult, ALU.add),
                                 reads=[ps.t(), xo[k].t(), modv.t()], writes=[xn[k].t()])
                            P.dma("sp", dst[oc * 128:(oc + 1) * 128, b0 + s * 512:b0 + s * 512 + n], xn[k][:, 0:n],
                                  reads=[xn[k].t()])

    def st_copy(src, dst, ntok):
        with P.scope():
            xs = [P.sb("cp", [128, KC, 512], F32) for _ in range(2)]
            k = 0
            for t0 in range(0, ntok, 512):
                n = min(512, ntok - t0)
                P.dma("sp", xs[k][:, :, 0:n], src[:, t0:t0 + n].rearrange("(kc p) n -> p kc n", p=128), writes=[xs[k].t()])
                P.dma("sp", dst[:, t0:t0 + n].rearrange("(kc p) n -> p kc n", p=128), xs[k][:, :, 0:n], reads=[xs[k].t()])
                k ^= 1

    def st_final(src):
        with P.scope():
            xs = P.sb("xs", [128, KC, 512], F32)
            rs = P.sb("rs", [128, 512], F32)
            sqs = P.sb("sq", [128, KC, 512], BF16)
            ot = [P.sb("ot", [128, KC, 512], F32) for _ in range(2)]
            k = 0
            for t0 in range(0, SEQ, 512):
                n = min(512, SEQ - t0)
                P.dma("sp", xs[:, :, 0:n], src[:, t0:t0 + n].rearrange("(kc p) n -> p kc n", p=128), writes=[xs.t()])
                rms_rstd(xs, n, rs, sqs)
                o = ot[k]
                for kc in range(KC):
                    P.op("dve", lambda e: e.scalar_tensor_tensor(o[:, kc, 0:n], xs[:, kc, 0:n], fg[:, kc:kc + 1],
                                                                 rs[:, 0:n], ALU.mult, ALU.mult),
                         reads=[xs.t(), rs.t(), fg.t()], writes=[o.t(kc)])
                P.dma("sp", outT[:, t0:t0 + n].rearrange("(kc p) n -> p kc n", p=128), o[:, :, 0:n],
                      reads=[o.t(kc) for kc in range(KC)])
                k ^= 1

    def st_pool(i, src, dst, ntok, g, nxt=None):
        o = i // 2
        K4 = KC // 4
        TBP = 512
        icn = icnt_l if g == 0 else icnt_c
        with P.scope():
            hp = P.sb("hp", [128, KC, TBP + 16], F32)
            pl = P.sb("pl", [128, KC, TBP], BF16)
            ic = P.sb("ic", [128, 4, TBP], F32)
            WA = P.sb("WA", [128, KC, TBP + 16], F32)
            WB = P.sb("WB", [128, KC, TBP + 16], F32)
            nb = norm_bufs(128)
            mst = Stepper(modvec_gen(nxt, 256, 3), (ntok + TBP - 1) // TBP, 6 * D // 256) if nxt is not None else None
            wp = P.sb("wp", [128, 4, K4, c.PG], BF16)
            gp = P.sb("gp", [128, KC], F32)
            psc = P.sb("psc", [128, KC], F32)
            xo = [P.sb("xo", [128, 512], F32) for _ in range(2)]
            xn = [P.sb("xn", [128, 512], F32) for _ in range(2)]
            for gi in range(4):
                P.dma("pool", wp[:, gi, :, :], pool_w[o, gi].rearrange("(k p) n -> p k n", p=128), writes=[wp.t(gi)])
            P.dma("sp", psc[:], pool_scT[:, o * KC:(o + 1) * KC], writes=[psc.t()])
            b0c = mvcol(i, 2, 0, g)
            P.op("dve", lambda e: e.tensor_tensor(gp[:], modv[:, b0c:b0c + 2 * KC:2], psc[:], ALU.mult),
                 reads=[modv.t(), psc.t()], writes=[gp.t()])
            kk = 0
            for b0 in range(0, ntok, TBP):
                tb = min(TBP, ntok - b0)
                L = tb + 16
                lo, hi = b0 - 8, b0 + tb + 8
                a, bnd = max(lo, 0), min(hi, ntok)
                hkeys = []
                if lo < 0:
                    P.op("pool", lambda e: e.memset(hp[:, :, 0:8], 0.0), writes=[t_ for kc in range(KC) for t_ in hp.seg(kc, 0, 8)])
                if hi > ntok:
                    P.op("pool", lambda e: e.memset(hp[:, :, tb + 8:tb + 16], 0.0), writes=[t_ for kc in range(KC) for t_ in hp.seg(kc, tb + 8, tb + 16)])
                subs = [(t0, n, t0 - lo) for (t0, n) in _split(a, bnd, 128)]
                if mst:
                    mst.step()
                norm_run(src, subs, hp, (lambda kc, c0, n: hp.seg(kc, c0, c0 + n)), i, 1, 0, g, nb)
                for w in range(4):
                    P.dma("sp", ic[:, w, 0:tb], icn[w:w + 1, b0:b0 + tb].partition_broadcast(128), writes=[ic.t(w)])
                hall = [t_ for kc in range(KC) for t_ in hp.seg(kc, 0, L)]
                P.op("dve", lambda e: e.tensor_tensor(WA[:, :, 1:L], hp[:, :, 0:L - 1], hp[:, :, 1:L], ALU.add),
                     reads=hall, writes=[WA.t()])
                P.op("dve", lambda e: e.tensor_tensor(WB[:, K4:KC, 2:L - 1], WA[:, K4:KC, 1:L - 2], WA[:, K4:KC, 3:L], ALU.add),
                     reads=[WA.t()], writes=[WB.t()])
                P.op("dve", lambda e: e.tensor_tensor(WA[:, 2 * K4:KC, 4:L - 3], WB[:, 2 * K4:KC, 2:L - 5], WB[:, 2 * K4:KC, 6:L - 1], ALU.add),
                     reads=[WB.t()], writes=[WA.t()])
                P.op("dve", lambda e: e.tensor_tensor(WB[:, 3 * K4:KC, 8:L - 7], WA[:, 3 * K4:KC, 4:L - 11], WA[:, 3 * K4:KC, 12:L - 3], ALU.add),
                     reads=[WA.t()], writes=[WB.t()])
                for gi in range(4):
                    cur = WA if gi % 2 == 0 else WB
                    en = "dve" if gi < 2 else "pool"
                    ka, kb = gi * K4, (gi + 1) * K4
                    icb = ic[:, gi, 0:tb].unsqueeze(1).to_broadcast([128, K4, tb])
                    P.op(en, lambda e: e.tensor_tensor(cur[:, ka:kb, 8:8 + tb], cur[:, ka:kb, 8:8 + tb], icb, ALU.mult),
                         reads=[WA.t(), WB.t(), ic.t(gi)], writes=[cur.t(("f", gi))])
                    P.op(en, lambda e: e.tensor_tensor(pl[:, ka:kb, 0:tb], cur[:, ka:kb, 8:8 + tb], hp[:, ka:kb, 8:8 + tb], ALU.subtract),
                         reads=[cur.t(("f", gi))] + hall, writes=[pl.t(gi)])
                for gi in range(4):
                    for oc in range(K4):
                        ocg = gi * K4 + oc
                        k = kk % 2
                        kk += 1
                        P.dma("sp", xo[k][:, 0:tb], src[ocg * 128:(ocg + 1) * 128, b0:b0 + tb], writes=[xo[k].t()])
                        ps = P.psum()
                        for k4 in range(K4):
                            P.op("pe", lambda e: e.matmul(ps[:, 0:tb], wp[:, gi, k4, oc * 128:(oc + 1) * 128],
                                                          pl[:, gi * K4 + k4, 0:tb], start=(k4 == 0), stop=(k4 == K4 - 1)),
                                 reads=[wp.t(gi), pl.t(gi)], writes=[ps.t()], sig=(k4 == K4 - 1))
                        P.op("dve", lambda e: e.scalar_tensor_tensor(xn[k][:, 0:tb], ps[:, 0:tb], gp[:, ocg:ocg + 1],
                                                                     xo[k][:, 0:tb], ALU.mult, ALU.add),
                             reads=[ps.t(), xo[k].t(), gp.t()], writes=[xn[k].t()])
                        P.dma("sp", dst[ocg * 128:(ocg + 1) * 128, b0:b0 + tb], xn[k][:, 0:tb], reads=[xn[k].t()])

            if mst:
                mst.finish()

    NT = CTX + SEQ
    AW, BQK, BV, NAH, GH, DIN = c.AW, c.BQK, c.BV, c.NAH, c.GH, c.DIN
    NCC = CTX // 128
    NCH = NT // 128
    ROWS = c.ROWS
    NB64 = c.NBLK * 64
    QKA = dscr("QKA", [2 * AW, NT], BF16)
    VA = dscr("VA", [NT, AW], BF16)
    QKB = dscr("QKB", [2 * BQK, NT], BF16)
    VBm = dscr("VBm", [NT, BV], BF16)
    GBs = dscr("GBs", [BV, NT], BF16)
    ABT = dscr("ABT", [32, NT], F32)
    OT = dscr("OT", [D, NT], BF16)
    segs = [("qa", 0, AW), ("ka", AW, 2 * AW), ("va", 2 * AW, 3 * AW), ("qb", 3 * AW, 3 * AW + BQK),
            ("kb", 3 * AW + BQK, 3 * AW + 2 * BQK), ("vb", 3 * AW + 2 * BQK, 3 * AW + 2 * BQK + BV),
            ("gb", 3 * AW + 2 * BQK + BV, 3 * AW + 2 * BQK + 2 * BV), ("ab", DIN - 32, DIN)]

    def ctype(col):
        for (nm, a, b) in segs:
            if a <= col < b:
                return nm, col - a
        raise ValueError

    def e_proj(i, e_, src, tok0, ntok, g):
        TB = min(c.TBE, ntok)
        with P.scope():
            hTs = [P.sb("hT", [128, KC, TB], BF16) for _ in range(2)]
            nb = norm_bufs(256)
            wt = [P.sb("wi", [128, KC, 512], BF16) for _ in range(2)]
            st = [P.sb("st", [128, 512], BF16) for _ in range(3)]
            stf = [P.sb("stf", [32, 512], F32) for _ in range(2)]
            nst = 0
            blocks = list(range(0, ntok, TB))

            def mk_norm(bi):
                b0_ = blocks[bi]
                tb_ = min(TB, ntok - b0_)
                hb = hTs[bi % 2]
                subs_ = [(t0, min(256, tb_ - t0)) for t0 in range(0, tb_, 256)]
                return norm_gen(src, [(b0_ + t0, n, t0) for (t0, n) in subs_], hb, (lambda kc, c0, n: [hb.t((c0, kc))]), i, 1, 0, g, nb)

            for _ in mk_norm(0):
                pass
            for bi, b0 in enumerate(blocks):
                tb = min(TB, ntok - b0)
                hT = hTs[bi % 2]
                subs = [(t0, min(256, tb - t0)) for t0 in range(0, tb, 256)]
                nxt_norm = mk_norm(bi + 1) if bi + 1 < len(blocks) else None

                def hk(kc, ca, cb):
                    return [hT.t((t0, kc)) for (t0, n) in subs if t0 < cb and t0 + n > ca]

                ng_ = (DIN + 511) // 512
                for cg in range(ng_):
                    if nxt_norm is not None and cg % 3 == 2:
                        try:
                            next(nxt_norm)
                        except StopIteration:
                            nxt_norm = None
                    w = wt[cg % 2]
                    ncol = min(512, DIN - cg * 512)
                    P.dma("pool", w[:, :, 0:ncol], w_in[e_, :, cg * 512:cg * 512 + ncol].rearrange("(kc p) n -> p kc n", p=128),
                          writes=[w.t()])
                    j = 0
                    while j * 128 < ncol:
                        col = cg * 512 + j * 128
                        nm, off = ctype(col)
                        if nm in ("va", "vb"):
                            j2 = j
                            while (j2 + 1) * 128 < ncol and ctype(cg * 512 + (j2 + 1) * 128)[0] == nm:
                                j2 += 1
                            nr = (j2 - j + 1) * 128
                            dstT = VA if nm == "va" else VBm
                            for tc in range(0, tb, 128):
                                ps = P.psum()
                                for kc in range(KC):
                                    P.op("pe", lambda e: e.matmul(ps[:, 0:nr], hT[:, kc, tc:tc + 128], w[:, kc, j * 128:j * 128 + nr],
                                                                  start=(kc == 0), stop=(kc == KC - 1)),
                                         reads=[w.t()] + hk(kc, tc, tc + 128), writes=[ps.t()], sig=(kc == KC - 1))
                                s_ = st[nst % 3]
                                nst += 1
                                P.op("act", lambda e: e.copy(s_[:, 0:nr], ps[:, 0:nr]), reads=[ps.t()], writes=[s_.t()])
                                P.dma("sp", dstT[tok0 + b0 + tc:tok0 + b0 + tc + 128, off:off + nr], s_[:, 0:nr], reads=[s_.t()])
                            j = j2 + 1
                            continue
                        m = 32 if nm == "ab" else 128
                        for s0 in range(0, tb, 512):
                            n = min(512, tb - s0)
                            ps = P.psum()
                            for kc in range(KC):
                                P.op("pe", lambda e: e.matmul(ps[0:m, 0:n], w[:, kc, j * 128:j * 128 + m], hT[:, kc, s0:s0 + n],
                                                              start=(kc == 0), stop=(kc == KC - 1)),
                                     reads=[w.t()] + hk(kc, s0, s0 + n), writes=[ps.t()], sig=(kc == KC - 1))
                            tcol = tok0 + b0 + s0
                            if nm == "ab":
                                s_ = stf[nst % 2]
                                nst += 1
                                P.op("act", lambda e: e.copy(s_[:, 0:n], ps[0:32, 0:n]), reads=[ps.t()], writes=[s_.t()])
                                P.dma("sp", ABT[:, tcol:tcol + n], s_[:, 0:n], reads=[s_.t()])
                            else:
                                s_ = st[nst % 3]
                                nst += 1
                                if nm == "qb":
                                    P.op("act", lambda e: e.mul(s_[:, 0:n], ps[:, 0:n], 0.125), reads=[ps.t()], writes=[s_.t()])
                                elif nm == "gb":
                                    P.op("act", lambda e: e.activation(out=s_[:, 0:n], in_=ps[:, 0:n], func=AF.Silu),
                                         reads=[ps.t()], writes=[s_.t()])
                                else:
                                    P.op("act", lambda e: e.copy(s_[:, 0:n], ps[:, 0:n]), reads=[ps.t()], writes=[s_.t()])
                                if nm in ("qa", "ka"):
                                    r0 = off + (AW if nm == "ka" else 0)
                                    dd = QKA[r0:r0 + 128, tcol:tcol + n]
                                elif nm in ("qb", "kb"):
                                    r0 = off + (BQK if nm == "kb" else 0)
                                    dd = QKB[r0:r0 + 128, tcol:tcol + n]
                                else:
                                    dd = GBs[off:off + 128, tcol:tcol + n]
                                P.dma("sp", dd, s_[:, 0:n], reads=[s_.t()])
                        j += 1

                if nxt_norm is not None:
                    for _ in nxt_norm:
                        pass

    def e_na(e_, need_ctx, nxt=None):
        scale = 128 ** -0.5
        with P.scope():
            P.nrot = 3
            kT = [P.sb("kT", [128, NT], BF16) for _ in range(2)]
            qT = [P.sb("qT", [128, NT], BF16) for _ in range(2)]
            V = [P.sb("V", [128, NCH, 128], BF16) for _ in range(2)]
            TF = [P.sb("TF", [128, NB64], F32) for _ in range(2)]
            TZ = [P.sb("TZ", [128, NB64], F32) for _ in range(2)]
            mF = P.sb("mF", [128, NB64], F32)
            mZ = P.sb("mZ", [128, NB64], F32)
            ones_bf = P.sb("ones_bf", [128, 128], BF16)
            ex = [P.sb("ex", [128, 512], F32) for _ in range(2)]
            pT = [P.sb("pT", [128, 512], BF16) for _ in range(4)]
            rd = [P.sb("rd", [128, 512], F32) for _ in range(2)]
            oT = [P.sb("oT", [128, 512], BF16) for _ in range(2)]
            P.dma("sp", mF[:], nmask[0], writes=[mF.t()])
            P.dma("sp", mZ[:], nmask[1], writes=[mZ.t()])
            P.dma("pool", ones_bf[:], cmat[2], writes=[ones_bf.t()])
            cnt = 0
            ntile = 0
            mst = Stepper(modvec_gen(nxt, 512, 3), NAH, 6 * D // 512) if nxt is not None else None
            for h in range(NAH):
                if mst:
                    mst.step()
                k_, q_, v_, tf, tz = kT[h % 2], qT[h % 2], V[h % 2], TF[h % 2], TZ[h % 2]
                P.dma("sp", k_[:], QKA[AW + h * 128:AW + (h + 1) * 128, :], writes=[k_.t()])
                P.dma("sp", q_[:], QKA[h * 128:(h + 1) * 128, :], writes=[q_.t()])
                P.dma("sp", v_[:], VA[:, h * 128:(h + 1) * 128].rearrange("(ch p) d -> p ch d", p=128), writes=[v_.t()])
                P.dma("sp", tf[:], rpbx[e_, h], writes=[tf.t()])
                P.op("act", lambda e: e.activation(out=tf[:], in_=tf[:], func=AF.Exp), reads=[tf.t()], writes=[tf.t()])
                P.op("dve", lambda e: e.tensor_tensor(tz[:], tf[:], mZ[:], ALU.mult), reads=[tf.t(), mZ.t()], writes=[tz.t()])
                P.op("dve", lambda e: e.tensor_tensor(tf[:], tf[:], mF[:], ALU.mult), reads=[tf.t(), mF.t()], writes=[tf.t()])
                tiles = [("lat", qt) for qt in range(ROWS // 8)] + ([("ctx", 0)] if need_ctx else [])
                items = []
                for (kind, qt) in tiles:
                    if kind == "lat":
                        r0 = qt * 8
                        qc0, nq = CTX + qt * 512, 512
                        c_lo, c_hi = max(0, (r0 - 4) // 2), min(ROWS // 2 - 1, (r0 + 10) // 2)
                        chunks = [("c", cc) for cc in range(NCC)] + [("l", cc) for cc in range(c_lo, c_hi + 1)]
                    else:
                        r0 = 0
                        qc0, nq = 0, CTX
                        chunks = [("c", cc) for cc in range(NCC)]
                    acc = (P.ps[3 + 2 * (ntile % 2)], P.ps[4 + 2 * (ntile % 2)], ntile)
                    ntile += 1
                    for idx, (ck, cc) in enumerate(chunks):
                        items.append(dict(r0=r0, qc0=qc0, nq=nq, ck=ck, cc=cc, first=(idx == 0), last=(idx == len(chunks) - 1),
                                          acc=acc))

                def emit_s(it):
                    ck, cc, nq, qc0, r0 = it["ck"], it["cc"], it["nq"], it["qc0"], it["r0"]
                    kc0 = cc * 128 if ck == "c" else CTX + cc * 128
                    ps_s = P.psum()
                    P.op("pe", lambda e: e.matmul(ps_s[:, 0:nq], k_[:, kc0:kc0 + 128], q_[:, qc0:qc0 + nq], start=True, stop=True),
                         reads=[k_.t(), q_.t()], writes=[ps_s.t()])
                    p_ = pT[it["n"] % 4]
                    it["p"] = p_
                    if ck == "c":
                        P.op("act", lambda e: e.activation(out=p_[:, 0:nq], in_=ps_s[:, 0:nq], func=AF.Exp, scale=scale),
                             reads=[ps_s.t()], writes=[p_.t()])
                    else:
                        x_ = ex[it["n"] % 2]
                        P.op("act", lambda e: e.activation(out=x_[:, 0:nq], in_=ps_s[:, 0:nq], func=AF.Exp, scale=scale),
                             reads=[ps_s.t()], writes=[x_.t()])
                        bb = r0 - 2 * cc + 11
                        fr = None
                        if r0 == 0 and cc <= 3:
                            fr = (0, 256)
                        if r0 == ROWS - 8 and cc >= ROWS // 2 - 4:
                            fr = (256, 512)
                        rngs = [(0, 512, tz)] if fr is None else [(fr[0], fr[1], tf), (256 - fr[0], 512 - fr[0], tz)]
                        for (ca, cb, tab) in rngs:
                            P.op("dve", lambda e: e.tensor_tensor(p_[:, ca:cb], x_[:, ca:cb], tab[:, bb * 64 + ca:bb * 64 + cb], ALU.mult),
                                 reads=[x_.t(), tab.t()], writes=[p_.t()])

                def emit_pv(it):
                    ps_o, ps_d, nt = it["acc"]
                    nq, qc0, p_ = it["nq"], it["qc0"], it["p"]
                    chi = it["cc"] if it["ck"] == "c" else NCC + it["cc"]
                    P.op("pe", lambda e: e.matmul(ps_o[:, 0:nq], v_[:, chi, :], p_[:, 0:nq], start=it["first"], stop=it["last"]),
                         reads=[v_.t(), p_.t()], writes=[ps_o.t()], sig=True)
                    P.op("pe", lambda e: e.matmul(ps_d[:, 0:nq], ones_bf[:], p_[:, 0:nq], start=it["first"], stop=it["last"]),
                         reads=[ones_bf.t(), p_.t()], writes=[ps_d.t()], sig=True)
                    if it["last"]:
                        r_, o_ = rd[nt % 2], oT[nt % 2]
                        P.op("dve", lambda e: e.reciprocal(r_[:, 0:nq], ps_d[:, 0:nq]), reads=[ps_d.t()], writes=[r_.t()])
                        P.op("dve", lambda e: e.tensor_tensor(o_[:, 0:nq], ps_o[:, 0:nq], r_[:, 0:nq], ALU.mult),
                             reads=[ps_o.t(), r_.t()], writes=[o_.t()])
                        P.dma("sp", OT[h * 128:(h + 1) * 128, qc0:qc0 + nq], o_[:, 0:nq], reads=[o_.t()])

                LA = 2
                for n_, it in enumerate(items):
                    it["n"] = cnt + n_
                for n_ in range(len(items) + LA):
                    if n_ < len(items):
                        emit_s(items[n_])
                    if n_ >= LA:
                        emit_pv(items[n_ - LA])
                cnt += len(items)
            if mst:
                mst.finish()
            P.nrot = 7

    def e_gla(e_, need_ctx):
        NFC = BQK // 128
        with P.scope():
            Cs = [P.sb("ropeC", [128, 512], F32) for _ in range(2)]
            Ss = [P.sb("ropeS", [128, 512], F32) for _ in range(2)]
            permb = P.sb("permb", [128, 128], BF16)
            identb = P.sb("identb", [128, 128], BF16)
            ones128 = P.sb("ones128", [128, 128], F32)
            onesr = P.sb("onesr", [1, 128], F32)
            tri = [P.sb("tri", [128, 128], F32) for _ in range(2)]
            wg2s = P.sb("wg2s", [32, 2 * BQK], F32)
            bgs = P.sb("bgs", [1, 2 * BQK], F32)
            abT = P.sb("abT", [32, NT], F32)
            gg = P.sb("gg", [128, GH], F32)
            qr = P.sb("qr", [128, NT], BF16)
            kr = P.sb("kr", [128, NT], BF16)
            qtil = P.sb("qtil", [128, NT], BF16)
            ktil = P.sb("ktil", [128, NT], BF16)
            khat = P.sb("khat", [128, NCH, 128], BF16)
            vv = P.sb("vv", [128, NCH, 256], BF16)
            dec = P.sb("dec", [128, NCH], F32)
            ob = [P.sb("ob", [128, NT], F32) for _ in range(2)]
            St = P.sb("St", [128, 128], F32)
            t1 = [P.sb("t1", [128, 512], F32) for _ in range(2)]
            t2 = [P.sb("t2", [128, 512], F32) for _ in range(2)]
            kh = [P.sb("kh", [128, 128], BF16) for _ in range(4)]
            xk = [P.sb("xk", [128, 128], F32) for _ in range(4)]
            sg = [P.sb("sg", [128, 512], BF16) for _ in range(2)]
            yo = [P.sb("yo", [128, 512], BF16) for _ in range(2)]
            P.dma("pool", permb[:], cmat[4], writes=[permb.t()])
            P.dma("pool", identb[:], cmat[3], writes=[identb.t()])
            P.dma("sp", ones128[:], cmat[1], writes=[ones128.t()])
            P.dma("sp", onesr[:], cmat[2][0:1, :], writes=[onesr.t()])
            P.dma("sp", tri[0][:], cmat[5], writes=[tri[0].t()])
            P.dma("sp", tri[1][:], cmat[6], writes=[tri[1].t()])
            P.dma("sp", wg2s[:], wg2[e_], writes=[wg2s.t()])
            P.dma("sp", bgs[:], bg[e_], writes=[bgs.t()])
            P.dma("sp", abT[:], ABT, writes=[abT.t()])
            P.dma("sp", gg[:], gla_gT[:, e_ * GH:(e_ + 1) * GH], writes=[gg.t()])
            k2 = 0
            for fc in range(NFC):
                P.dma("sp", qr[:], QKB[fc * 128:(fc + 1) * 128, :], writes=[qr.t()])
                P.dma("sp", kr[:], QKB[BQK + fc * 128:BQK + (fc + 1) * 128, :], writes=[kr.t()])
                P.dma("sp", vv[:], VBm[:, fc * 256:(fc + 1) * 256].rearrange("(ch p) d -> p ch d", p=128), writes=[vv.t()])
                for buf in (qr, kr):
                    for s0 in range(0, SEQ, 512):
                        a, b = CTX + s0, CTX + s0 + 512
                        ps = P.psum()
                        P.op("pe", lambda e: e.matmul(ps[:, 0:512], permb[:], buf[:, a:b], start=True, stop=True),
                             reads=[permb.t(), buf.t()], writes=[ps.t()])
                        x1, x2 = t1[k2 % 2], t2[k2 % 2]
                        C_, S_ = Cs[k2 % 2], Ss[k2 % 2]
                        k2 += 1
                        P.dma("sp", C_[:], rope[0][:, s0:s0 + 512], writes=[C_.t()])
                        P.dma("sp", S_[:], rope[1][:, s0:s0 + 512], writes=[S_.t()])
                        P.op("dve", lambda e: e.tensor_tensor(x1[:], ps[:, 0:512], S_[:], ALU.mult),
                             reads=[ps.t(), S_.t()], writes=[x1.t()])
                        P.op("pool", lambda e: e.tensor_tensor(x2[:], buf[:, a:b], C_[:], ALU.mult),
                             reads=[buf.t(), C_.t()], writes=[x2.t()])
                        P.op("dve", lambda e: e.tensor_tensor(buf[:, a:b], x1[:], x2[:], ALU.add),
                             reads=[x1.t(), x2.t(), ps.t()], writes=[buf.t()])
                for d in range(2):
                  with P.scope():
                    sp_ = P.sb("sp", [128, NCH, 128], F32)
                    cum = P.sb("cum", [128, NT], F32)
                    zc = d * BQK + fc * 128
                    for ch in range(NCH):
                        ps = P.psum()
                        P.op("pe", lambda e: e.matmul(ps[:, 0:128], abT[:, ch * 128:(ch + 1) * 128], wg2s[:, zc:zc + 128], start=True, stop=False),
                             reads=[abT.t(), wg2s.t()], writes=[ps.t()], sig=False)
                        P.op("pe", lambda e: e.matmul(ps[:, 0:128], onesr[:], bgs[:, zc:zc + 128], start=False, stop=True),
                             reads=[onesr.t(), bgs.t()], writes=[ps.t()])
                        P.op("act", lambda e: e.activation(out=sp_[:, ch, :], in_=ps[:, 0:128], func=AF.Exp, scale=-1.0),
                             reads=[ps.t()], writes=[sp_.t(ch)])
                    for ch in range(NCH):
                        P.op("act", lambda e: e.activation(out=sp_[:, ch, :], in_=sp_[:, ch, :], func=AF.Ln, bias=1.0),
                             reads=[sp_.t(ch)], writes=[sp_.t(ch)])
                    for c4 in range(0, NCH, 4):
                        nn = min(4, NCH - c4)
                        ps = P.psum()
                        for u in range(nn):
                            P.op("pe", lambda e: e.matmul(ps[:, u * 128:(u + 1) * 128], sp_[:, c4 + u, :], tri[d][:], start=True, stop=True),
                                 reads=[sp_.t(c4 + u), tri[d].t()], writes=[ps.t()])
                        P.op("dve", lambda e: e.tensor_scalar(cum[:, c4 * 128:(c4 + nn) * 128], ps[:, 0:nn * 128], -1.0 / 16, None, ALU.mult),
                             reads=[ps.t()], writes=[cum.t(c4)])
                    allcum = [cum.t(c4) for c4 in range(0, NCH, 4)]
                    lastcol = 127 if d == 0 else 0
                    P.op("act", lambda e: e.activation(out=dec[:], in_=cum[:, lastcol:NT:128], func=AF.Exp),
                         reads=allcum, writes=[dec.t()])
                    for s0 in range(0, NT, 512):
                        n = min(512, NT - s0)
                        x1, x2 = t1[k2 % 2], t2[k2 % 2]
                        k2 += 1
                        P.op("act", lambda e: e.activation(out=x1[:, 0:n], in_=cum[:, s0:s0 + n], func=AF.Exp),
                             reads=allcum, writes=[x1.t()])
                        P.op("dve", lambda e: e.tensor_tensor(qtil[:, s0:s0 + n], qr[:, s0:s0 + n], x1[:, 0:n], ALU.mult),
                             reads=[qr.t(), x1.t()], writes=[qtil.t()])
                        P.op("act", lambda e: e.activation(out=x2[:, 0:n], in_=cum[:, s0:s0 + n], func=AF.Exp, scale=-1.0),
                             reads=allcum, writes=[x2.t()])
                        P.op("pool", lambda e: e.tensor_tensor(ktil[:, s0:s0 + n], kr[:, s0:s0 + n], x2[:, 0:n], ALU.mult),
                             reads=[kr.t(), x2.t()], writes=[ktil.t()])
                    for ch in range(NCH):
                        x1 = t1[k2 % 2]
                        khb = kh[k2 % 2]
                        k2 += 1
                        lc = ch * 128 + lastcol
                        P.op("act", lambda e: e.activation(out=x1[:, 0:128], in_=cum[:, ch * 128:(ch + 1) * 128], func=AF.Exp,
                                                           scale=-1.0, bias=cum[:, lc:lc + 1]),
                             reads=allcum, writes=[x1.t()])
                        P.op("dve", lambda e: e.tensor_tensor(khb[:], kr[:, ch * 128:(ch + 1) * 128], x1[:, 0:128], ALU.mult),
                             reads=[kr.t(), x1.t()], writes=[khb.t()])
                        P.op("pe", lambda e: e.transpose(P.psb[:, (ch % 8) * 128:(ch % 8 + 1) * 128], khb[:], identb[:]),
                             reads=[khb.t(), identb.t()], writes=[P.psb.t(ch % 8)])
                        P.op("act", lambda e: e.copy(khat[:, ch, :], P.psb[:, (ch % 8) * 128:(ch % 8 + 1) * 128]),
                             reads=[P.psb.t(ch % 8)], writes=[khat.t(ch)])
                  with P.scope():
                    aTa = P.sb("aTa", [128, NCH, 2, 128], BF16)
                    Sba = P.sb("Sba", [128, NCH, 128], BF16)
                    P.op("dve", lambda e: e.memset(St[:], 0.0), writes=[St.t()])
                    if d == 0:
                        order = list(range(NCH))
                    else:
                        order = list(range(NCC - 1, -1, -1)) + list(range(NCH - 1, NCC - 1, -1))
                    for ch in order:
                        a, b = ch * 128, (ch + 1) * 128
                        P.op("dve", lambda e: e.tensor_copy(Sba[:, ch, :], St[:]), reads=[St.t()], writes=[Sba.t(ch)])
                        ps_kv = P.psum()
                        P.op("pe", lambda e: e.matmul(ps_kv[:, 0:256], khat[:, ch, :], vv[:, ch, :], start=True, stop=True),
                             reads=[khat.t(ch), vv.t()], writes=[ps_kv.t()])
                        for hh in range(2):
                            pa, pb = hh * 64, (hh + 1) * 64
                            ps_a = P.psum()
                            P.op("pe", lambda e: e.matmul(ps_a[:, 0:128], ktil[pa:pb, a:b], qtil[pa:pb, a:b], start=True, stop=True),
                                 reads=[ktil.t(), qtil.t()], writes=[ps_a.t()])
                            P.op("dve", lambda e: e.tensor_tensor(aTa[:, ch, hh, :], ps_a[:, 0:128], tri[d][:], ALU.mult),
                                 reads=[ps_a.t(), tri[d].t()], writes=[aTa.t((ch, hh))])
                        for hh in range(2):
                            pa, pb = hh * 64, (hh + 1) * 64
                            P.op("dve", lambda e: e.scalar_tensor_tensor(St[pa:pb, :], St[pa:pb, :], dec[pa:pb, ch:ch + 1],
                                                                         ps_kv[pa:pb, hh * 128:(hh + 1) * 128], ALU.mult, ALU.add),
                                 reads=[St.t(), dec.t(), ps_kv.t()], writes=[St.t()])
                    for ch in order:
                        a, b = ch * 128, (ch + 1) * 128
                        for hh in range(2):
                            pa, pb = hh * 64, (hh + 1) * 64
                            ps_o = P.psum()
                            P.op("pe", lambda e: e.matmul(ps_o[:, 0:128], Sba[pa:pb, ch, :], qtil[pa:pb, a:b], start=True, stop=False),
                                 reads=[Sba.t(ch), qtil.t()], writes=[ps_o.t()], sig=False)
                            P.op("pe", lambda e: e.matmul(ps_o[:, 0:128], vv[:, ch, hh * 128:(hh + 1) * 128], aTa[:, ch, hh, :], start=False, stop=True),
                                 reads=[vv.t(), aTa.t((ch, hh))], writes=[ps_o.t()])
                            if d == 0:
                                P.op("act", lambda e: e.copy(ob[hh][:, a:b], ps_o[:, 0:128]), reads=[ps_o.t()], writes=[ob[hh].t(ch)])
                            else:
                                P.op("dve", lambda e: e.tensor_tensor(ob[hh][:, a:b], ob[hh][:, a:b], ps_o[:, 0:128], ALU.add),
                                     reads=[ps_o.t(), ob[hh].t(ch)], writes=[ob[hh].t(ch)])
                for hh in range(2):
                    hg = fc * 2 + hh
                    t_lo = 0 if need_ctx else CTX
                    for s0 in range(t_lo, NT, 512):
                        n = min(512, NT - s0)
                        chs = [ob[hh].t(ch) for ch in range(s0 // 128, (s0 + n) // 128)]
                        x1, x2 = t1[k2 % 2], t2[k2 % 2]
                        s_, y_ = sg[k2 % 2], yo[k2 % 2]
                        k2 += 1
                        P.dma("sp", s_[:, 0:n], GBs[hg * 128:(hg + 1) * 128, s0:s0 + n], writes=[s_.t()])
                        P.op("dve", lambda e: e.tensor_tensor(x1[:, 0:n], ob[hh][:, s0:s0 + n], ob[hh][:, s0:s0 + n], ALU.mult),
                             reads=chs, writes=[x1.t()])
                        ps = P.psum()
                        P.op("pe", lambda e: e.matmul(ps[:, 0:n], ones128[:], x1[:, 0:n], start=True, stop=True),
                             reads=[ones128.t(), x1.t()], writes=[ps.t()])
                        P.op("act", lambda e: e.activation(out=x2[:, 0:n], in_=ps[:, 0:n], func=AF.Ln, bias=EPS),
                             reads=[ps.t()], writes=[x2.t()])
                        P.op("act", lambda e: e.activation(out=x2[:, 0:n], in_=x2[:, 0:n], func=AF.Exp, scale=-0.5),
                             reads=[x2.t()], writes=[x2.t()])
                        P.op("dve", lambda e: e.tensor_tensor(x1[:, 0:n], ob[hh][:, s0:s0 + n], x2[:, 0:n], ALU.mult),
                             reads=chs + [x2.t(), x1.t()], writes=[x1.t()])
                        P.op("dve", lambda e: e.scalar_tensor_tensor(y_[:, 0:n], x1[:, 0:n], gg[:, hg:hg + 1], s_[:, 0:n], ALU.mult, ALU.mult),
                             reads=[x1.t(), gg.t(), s_.t()], writes=[y_.t()])
                        P.dma("sp", OT[AW + hg * 128:AW + (hg + 1) * 128, s0:s0 + n], y_[:, 0:n], reads=[y_.t()])

    def e_out(i, e_, src, dst, tok0, ntok, g):
        TB = min(1024, ntok)
        with P.scope():
            ob_ = P.sb("otb", [128, KC, TB], BF16)
            wt = [P.sb("wo", [128, KC, 512], BF16) for _ in range(2)]
            xo = [P.sb("xo", [128, 512], F32) for _ in range(2)]
            xn = [P.sb("xn", [128, 512], F32) for _ in range(2)]
            kk = 0
            for b0 in range(0, ntok, TB):
                tb = min(TB, ntok - b0)
                P.dma("sp", ob_[:, :, 0:tb], OT[:, tok0 + b0:tok0 + b0 + tb].rearrange("(kc p) n -> p kc n", p=128), writes=[ob_.t()])
                for cg in range(D // 512):
                    w = wt[cg % 2]
                    P.dma("pool", w[:], w_out[e_, :, cg * 512:(cg + 1) * 512].rearrange("(kc p) n -> p kc n", p=128), writes=[w.t()])
                    for j in range(4):
                        oc = cg * 4 + j
                        for s0 in range(0, tb, 512):
                            n = min(512, tb - s0)
                            k = kk % 2
                            kk += 1
                            P.dma("sp", xo[k][:, 0:n], src[oc * 128:(oc + 1) * 128, b0 + s0:b0 + s0 + n], writes=[xo[k].t()])
                            ps = P.psum()
                            for kc in range(KC):
                                P.op("pe", lambda e: e.matmul(ps[:, 0:n], w[:, kc, j * 128:(j + 1) * 128], ob_[:, kc, s0:s0 + n],
                                                              start=(kc == 0), stop=(kc == KC - 1)),
                                     reads=[w.t(), ob_.t()], writes=[ps.t()], sig=(kc == KC - 1))
                            P.op("dve", lambda e: e.scalar_tensor_tensor(xn[k][:, 0:n], ps[:, 0:n], mv(i, 2, oc, g), xo[k][:, 0:n], ALU.mult, ALU.add),
                                 reads=[ps.t(), xo[k].t(), modv.t()], writes=[xn[k].t()])
                            P.dma("sp", dst[oc * 128:(oc + 1) * 128, b0 + s0:b0 + s0 + n], xn[k][:, 0:n], reads=[xn[k].t()])

    def st_even(i, xsrc, csrc, xdst, cdst, need_ctx, nxt):
        e_ = i // 2
        e_proj(i, e_, csrc, 0, CTX, 1)
        e_proj(i, e_, xsrc, CTX, SEQ, 0)
        e_na(e_, need_ctx, nxt)
        e_gla(e_, need_ctx)
        e_out(i, e_, xsrc, xdst, CTX, SEQ, 0)
        if need_ctx:
            e_out(i, e_, csrc, cdst, 0, CTX, 1)

    st_modvec(0)
    xsrc, csrc = xT, cT
    for i in range(DEPTH):
        is_even = i % 2 == 0
        need_ctx = any(j % 2 == 0 for j in range(i + 1, DEPTH))
        nxt = i + 1 if i + 1 < DEPTH else None
        used = False
        if is_even and fl["even"]:
            st_even(i, xsrc, csrc, X[1], C[1], need_ctx, nxt)
            used = True
        elif (not is_even) and fl["odd"]:
            st_pool(i, xsrc, X[1], SEQ, 0, None)
            if need_ctx:
                st_pool(i, csrc, C[1], CTX, 1, None)
        else:
            st_copy(xsrc, X[1], SEQ)
            if need_ctx:
                st_copy(csrc, C[1], CTX)
        if nxt is not None and not used:
            st_modvec(nxt)
        if fl["ffn"]:
            st_ffn(i, X[1], X[0], SEQ, 0)
            if need_ctx:
                st_ffn(i, C[1], C[0], CTX, 1)
        else:
            st_copy(X[1], X[0], SEQ)
            if need_ctx:
                st_copy(C[1], C[0], CTX)
        xsrc, csrc = X[0], C[0]
    st_final(xsrc)
    P.barrier()
    return nc


def _fm(v, KC):
    return np.ascontiguousarray(np.asarray(v, np.float32).reshape(KC, 128).T)


def prep_shared(cfg, inp):
    c = cfg
    KC, FC, DEPTH = c.KC, c.FC, c.DEPTH
    f = lambda a: np.ascontiguousarray(np.asarray(a, np.float32))
    sh = {}
    sh["w_mod"] = f(inp["w_mod"])
    sh["b_modT"] = f(np.stack([_fm(inp["b_mod"][i], 6 * KC) for i in range(DEPTH)], 1).reshape(128, -1))
    ngs = np.stack([np.stack([_fm(inp["norm1_g"][i], KC), _fm(inp["norm2_g"][i], KC)], 1) for i in range(DEPTH)], 1)
    sh["ngT"] = f(ngs.reshape(128, -1))
    sh["fgT"] = _fm(inp["final_g"], KC)
    sh["w_up"] = f(inp["w_up"])
    sh["w_down"] = f(inp["w_down"])
    cw = np.asarray(inp["conv_w"], np.float32)
    cb = np.asarray(inp["conv_b"], np.float32)
    cvt = np.zeros((128, DEPTH, FC, 4), np.float32)
    for i in range(DEPTH):
        for k in range(3):
            cvt[:, i, :, k] = _fm(cw[i, k], FC)
        cvt[:, i, :, 3] = _fm(cb[i], FC)
    sh["convT"] = f(cvt.reshape(128, -1))
    cm = np.zeros((8, 128, 128), np.float32)
    cm[0] = 1.0 / c.D
    cm[1] = 1.0 / 128
    cm[2] = 1.0
    cm[3] = np.eye(128)
    p = np.arange(128)
    perm = np.where((p % 32) < 16, p + 16, p - 16)
    cm[4][perm, p] = 1.0
    jj, ii = np.meshgrid(p, p, indexing="ij")
    cm[5] = np.where(jj <= ii, 1.0, 0.0)
    cm[6] = np.where(jj >= ii, 1.0, 0.0)
    sh["cmat"] = cm
    sh["pool_w"] = f(inp["pool_w"])
    NE, BQK, NAH, NBLK = c.NE, c.BQK, c.NAH, c.NBLK
    sh["w_in"] = f(inp["w_in"])
    sh["w_out"] = f(inp["w_out"])
    wg = np.zeros((NE, 32, 2 * BQK), np.float32)
    w2 = np.asarray(inp["w_gate2"], np.float32)
    for d in range(2):
        wg[:, d * 16:(d + 1) * 16, d * BQK:(d + 1) * BQK] = w2[:, d]
    sh["wg2"] = wg
    sh["bg"] = f(np.asarray(inp["b_gate"], np.float32).reshape(NE, 1, 2 * BQK))
    rp = np.asarray(inp["rpb"], np.float32)
    pp = np.arange(128)
    half = pp // 64
    kcol = pp % 64
    bb = np.arange(NBLK)
    qcol = np.arange(64)
    e_idx = bb[None, :] - 4 - half[:, None]
    ev = (e_idx >= 0) & (e_idx <= 14)
    dr = np.clip(14 - e_idx, 0, 14)
    dc = np.clip(kcol[:, None] - qcol[None, :] + 15, 0, 30)
    rx = rp[:, :, dr[:, :, None], dc[:, None, :]]
    rx = np.where(ev[None, None, :, :, None], rx, 0.0).astype(np.float32)
    sh["rpbx"] = f(rx.reshape(NE, NAH, 128, NBLK * 64))
    cstart = np.clip(qcol - 8, 0, 48)
    col_in = (kcol[:, None] >= cstart[None, :]) & (kcol[:, None] < cstart[None, :] + 16)
    mF = (ev[:, :, None] & col_in[:, None, :])
    mZ = mF & ((e_idx >= 4) & (e_idx <= 11))[:, :, None]
    sh["nmask"] = f(np.stack([mF, mZ], 0).astype(np.float32).reshape(2, 128, NBLK * 64))
    t = np.arange(c.SEQ)
    row = (t // 64).astype(np.float32)
    colp = (t % 64).astype(np.float32)
    dd = pp % 64
    ii = dd % 32
    inv = (np.float32(10000.0) ** (-(np.arange(16, dtype=np.float32)) / np.float32(16))).astype(np.float32)
    pos = np.where((dd // 32)[:, None] == 0, row[None, :], colp[None, :]).astype(np.float32)
    ang = (pos * inv[ii % 16][:, None]).astype(np.float32)
    sgn = np.where(ii < 16, -1.0, 1.0).astype(np.float32)
    sh["rope"] = f(np.stack([np.cos(ang), np.sin(ang) * sgn[:, None]], 0))
    gg = np.asarray(inp["gla_norm_g"], np.float32)
    sh["gla_gT"] = f(gg.transpose(2, 0, 1).reshape(128, -1))
    sh["pool_scT"] = f(np.stack([_fm(inp["pool_scale"][o], KC) for o in range(c.NO)], 1).reshape(128, -1))
    def icnt(L):
        t = np.arange(L)
        out = np.zeros((4, L), np.float32)
        for wi, w in enumerate((2, 4, 8, 16)):
            lo = np.clip(t - w // 2, 0, L); hi = np.clip(t + w // 2, 0, L)
            out[wi] = 1.0 / (hi - lo).astype(np.float32)
        return out
    sh["icnt_l"] = icnt(c.SEQ)
    sh["icnt_c"] = icnt(c.CTX)
    return sh


def prep_core(cfg, inp, b, zero=False):
    c = cfg
    d = {}
    x = np.asarray(inp["x"][b], np.float32)
    cx = np.asarray(inp["ctx"][b], np.float32)
    cvec = np.stack([_fm(inp["c"][b], c.KC), _fm(inp["c_ctx"], c.KC)], 2).reshape(128, -1)
    if zero:
        d["xT"] = np.zeros((c.D, c.SEQ), np.float32)
        d["cT"] = np.zeros((c.D, c.CTX), np.float32)
        d["cv"] = np.zeros_like(cvec)
    else:
        d["xT"] = np.ascontiguousarray(x.T)
        d["cT"] = np.ascontiguousarray(cx.T)
        d["cv"] = np.ascontiguousarray(cvec)
    return d


def run(cfg, inp, flags=None):
    nc = build(cfg, flags)
    sh = prep_shared(cfg, inp)
    names = set()
    in_maps = []
    ncore = 8
    for k in range(ncore):
        b = (k // 2) % cfg.B
        m = dict(sh)
        m.update(prep_core(cfg, inp, b, zero=(k % 2 == 1 or k // 2 >= cfg.B)))
        in_maps.append(m)
    res = run_bass_kernel_spmd(nc, in_maps, core_ids=list(range(ncore)))
    out = np.stack([np.ascontiguousarray(res.results[2 * b]["outT"].T) for b in range(cfg.B)], 0)
    return out.astype(np.float32)


def kernel(**inputs):
    return run(Cfg(), inputs)
```

```python
import contextlib
import numpy as np
import concourse.bass as bass
import concourse.mybir as mybir
from concourse.bass_utils import run_bass_kernel_spmd

F32, BF16 = mybir.dt.float32, mybir.dt.bfloat16
AF = mybir.ActivationFunctionType
ALU = mybir.AluOpType
ND = 6
EPS = 1e-6


class Cfg:
    def __init__(s, D=2048, SEQ=4096, CTX=256, DEPTH=4, DFF=5632, TBF=2048, B=4, TBE=1024):
        s.D, s.SEQ, s.CTX, s.DEPTH, s.DFF, s.TBF, s.B = D, SEQ, CTX, DEPTH, DFF, TBF, B
        s.KC = D // 128
        s.TBE = TBE
        s.NH = D // 128
        s.NAH = s.NH // 2
        s.GH = s.NH - s.NAH
        s.AW, s.BQK, s.BV = s.NAH * 128, s.GH * 64, s.GH * 128
        s.DIN = 3 * s.AW + 2 * s.BQK + 2 * s.BV + 32
        s.ROWS = SEQ // 64
        s.PG = D // 4
        s.FC = DFF // 128
        s.NE = (DEPTH + 1) // 2
        s.NO = DEPTH // 2
        s.NBLK = 23


def _split(a, b, mx=256):
    n = b - a
    k = (n + mx - 1) // mx
    out, t = [], a
    for i in range(k):
        m = n // k + (1 if i < n % k else 0)
        out.append((t, m))
        t += m
    return out


class Tk:
    __slots__ = ("w", "r")

    def __init__(s):
        s.w = None
        s.r = {}


class Buf:
    def __init__(s, h):
        s.h = h
        s.tks = {}

    def t(s, key=0):
        if key not in s.tks:
            s.tks[key] = Tk()
        return s.tks[key]

    def seg(s, kc, ca, cb, g=64):
        return [s.t((kc, q)) for q in range(ca // g, (cb - 1) // g + 1)]

    def __getitem__(s, k):
        return s.h[k]


class Prog:
    def __init__(s, nc):
        s.nc = nc
        s.eng = {"pe": nc.tensor, "act": nc.scalar, "dve": nc.vector, "pool": nc.gpsimd, "sp": nc.sync}
        s.sem = {k: nc.alloc_semaphore("s_" + k) for k in s.eng}
        s.cnt = {k: 0 for k in s.eng}
        s.pend = False
        s.waited = {k: {} for k in s.eng}
        s.dsem = {q: [nc.alloc_semaphore("d_%s%d" % (q, i)) for i in range(ND)] for q in ("sp", "pool", "act")}
        s.dcnt = {q: [0] * ND for q in s.dsem}
        s.dnext = {q: 0 for q in s.dsem}
        s.nps = 0
        s.nrot = 7
        s.ps = [Buf(nc.alloc_psum_tensor("ps%d" % i, [128, 512], F32)) for i in range(7)]
        s.psb = Buf(nc.alloc_psum_tensor("psb", [128, 1024], BF16))
        s.stack = None
        s.nalloc = 0

    @contextlib.contextmanager
    def scope(s):
        old = s.stack
        with contextlib.ExitStack() as st:
            s.stack = st
            yield
            s.barrier()
        s.stack = old

    def sb(s, name, shape, dt):
        s.nalloc += 1
        nm = "%s_%d" % (name, s.nalloc)
        if s.stack is None:
            return Buf(s.nc.alloc_sbuf_tensor(nm, list(shape), dt))
        return Buf(s.stack.enter_context(s.nc.sbuf_tensor(nm, list(shape), dt)))

    def psum(s):
        b = s.ps[s.nps % s.nrot]
        s.nps += 1
        return b

    def _wait(s, e, tok):
        if tok is None:
            return
        key, sem, val = tok
        if e == "pe" and key == "pe":
            return
        if s.waited[e].get(key, -1) >= val:
            return
        s.eng[e].wait_ge(sem, val)
        s.waited[e][key] = val

    def _deps(s, e, reads, writes):
        for t in reads:
            s._wait(e, t.w)
        for t in writes:
            s._wait(e, t.w)
            for r in t.r.values():
                s._wait(e, r)

    def _mark(s, tok, reads, writes):
        for t in reads:
            t.r[tok[0]] = tok
        for t in writes:
            t.w = tok
            t.r = {}

    def op(s, e, fn, reads=(), writes=(), sig=True):
        s._deps(e, reads, writes)
        ins = fn(s.eng[e])
        if e == "pe" and not sig:
            tok = ("pe", s.sem["pe"], s.cnt["pe"] + 1)
            s.pend = True
        else:
            s.cnt[e] += 1
            ins.then_inc(s.sem[e], 1)
            tok = (e, s.sem[e], s.cnt[e])
            if e == "pe":
                s.pend = False
        s._mark(tok, reads, writes)
        return tok

    def dma(s, q, out, in_, reads=(), writes=(), **kw):
        i = s.dnext[q]
        s.dnext[q] = (i + 1) % ND
        sem = s.dsem[q][i]
        key = "d_%s%d" % (q, i)
        if s.dcnt[q][i] > 0:
            s._wait(q, (key, sem, s.dcnt[q][i]))
        s._deps(q, reads, writes)
        ins = s.eng[q].dma_start(out=out, in_=in_, **kw)
        s.dcnt[q][i] += 16
        ins.then_inc(sem, 16)
        tok = (key, sem, s.dcnt[q][i])
        s._mark(tok, reads, writes)
        return tok

    def barrier(s):
        assert not s.pend
        toks = [(k, s.sem[k], s.cnt[k]) for k in s.eng if s.cnt[k] > 0]
        for q in s.dsem:
            for i in range(ND):
                if s.dcnt[q][i] > 0:
                    toks.append(("d_%s%d" % (q, i), s.dsem[q][i], s.dcnt[q][i]))
        for e in s.eng:
            for tok in toks:
                s._wait(e, tok)


def build(cfg, flags=None):
    fl = dict(even=True, odd=True, ffn=True)
    if flags:
        fl.update(flags)
    c = cfg
    D, SEQ, CTX, DEPTH, DFF, KC, FC = c.D, c.SEQ, c.CTX, c.DEPTH, c.DFF, c.KC, c.FC
    nc = bass.Bass("TRN2", target_bir_lowering=False)
    P = Prog(nc)

    def din(name, shape):
        return nc.dram_tensor(name, list(shape), F32, kind="ExternalInput").ap()

    def dscr(name, shape, dt=F32):
        return nc.dram_tensor(name, list(shape), dt).ap()

    xT = din("xT", [D, SEQ])
    cT = din("cT", [D, CTX])
    cv = din("cv", [128, KC * 2])
    w_mod = din("w_mod", [DEPTH, D, 6 * D])
    b_modT = din("b_modT", [128, DEPTH * 6 * KC])
    ngT = din("ngT", [128, DEPTH * 2 * KC])
    fgT = din("fgT", [128, KC])
    w_up = din("w_up", [DEPTH, D, 2 * DFF])
    w_down = din("w_down", [DEPTH, DFF, D])
    convT = din("convT", [128, DEPTH * FC * 4])
    pool_w = din("pool_w", [c.NO, 4, c.PG, c.PG])
    pool_scT = din("pool_scT", [128, c.NO * KC])
    icnt_l = din("icnt_l", [4, SEQ])
    icnt_c = din("icnt_c", [4, CTX])
    w_in = din("w_in", [c.NE, D, c.DIN])
    w_out = din("w_out", [c.NE, D, D])
    wg2 = din("wg2", [c.NE, 32, 2 * c.BQK])
    bg = din("bg", [c.NE, 1, 2 * c.BQK])
    rpbx = din("rpbx", [c.NE, c.NAH, 128, c.NBLK * 64])
    nmask = din("nmask", [2, 128, c.NBLK * 64])
    rope = din("rope", [2, 128, SEQ])
    gla_gT = din("gla_gT", [128, c.NE * c.GH])
    cmat = din("cmat", [8, 128, 128])
    outT = nc.dram_tensor("outT", [D, SEQ], F32, kind="ExternalOutput").ap()
    X = [dscr("X0", [D, SEQ]), dscr("X1", [D, SEQ])]
    C = [dscr("C0", [D, CTX]), dscr("C1", [D, CTX])]

    onesD = P.sb("onesD", [128, 128], BF16)
    silc = P.sb("silc", [128, KC * 2], BF16)
    cvs = P.sb("cvs", [128, KC * 2], F32)
    modv = P.sb("modv", [128, DEPTH * 6 * KC * 2], F32)
    bmod = P.sb("bmod", [128, DEPTH * 6 * KC], F32)
    ng = P.sb("ng", [128, DEPTH * 2 * KC], F32)
    fg = P.sb("fg", [128, KC], F32)
    conv = P.sb("conv", [128, DEPTH * FC * 4], F32)
    P.dma("pool", onesD[:], cmat[0], writes=[onesD.t()])
    P.dma("sp", cvs[:], cv, writes=[cvs.t()])
    P.dma("sp", bmod[:], b_modT, writes=[bmod.t()])
    P.dma("sp", ng[:], ngT, writes=[ng.t()])
    P.dma("sp", fg[:], fgT, writes=[fg.t()])
    P.dma("sp", conv[:], convT, writes=[conv.t()])
    P.op("act", lambda e: e.activation(out=silc[:], in_=cvs[:], func=AF.Silu), reads=[cvs.t()], writes=[silc.t()])

    def mvcol(i, v, kc, g):
        return ((i * 6 + v) * KC + kc) * 2 + g

    def mv(i, v, kc, g):
        cidx = mvcol(i, v, kc, g)
        return modv[:, cidx:cidx + 1]

    def modvec_gen(i, gw, nbuf):
        wt = [P.sb("mw", [128, KC, gw], BF16) for _ in range(nbuf)]
        mvt = [P.sb("mvt", [2, gw], F32) for _ in range(2)]
        i2 = P.sb("i2", [2, 2], F32)
        P.dma("sp", i2[:], cmat[3][0:2, 0:2], writes=[i2.t()])
        noc = 6 * KC
        base = i * noc * 2
        ngrp = 6 * D // gw
        no = gw // 128
        for g in range(ngrp):
            w = wt[g % nbuf]
            m_ = mvt[g % 2]
            P.dma("pool", w[:], w_mod[i, :, g * gw:(g + 1) * gw].rearrange("(kc p) n -> p kc n", p=128), writes=[w.t()])
            pg = P.psum()
            for kc in range(KC):
                P.op("pe", lambda e: e.matmul(pg[0:2, 0:gw], silc[:, kc * 2:kc * 2 + 2], w[:, kc, :],
                                              start=(kc == 0), stop=(kc == KC - 1)),
                     reads=[w.t(), silc.t()], writes=[pg.t()], sig=(kc == KC - 1))
            P.op("act", lambda e: e.copy(m_[:, 0:gw], pg[0:2, 0:gw]), reads=[pg.t()], writes=[m_.t()])
            pt = P.psum()
            for j in range(no):
                P.op("pe", lambda e: e.matmul(pt[:, j * 2:j * 2 + 2], m_[:, j * 128:(j + 1) * 128], i2[:], start=True, stop=True),
                     reads=[m_.t(), i2.t()], writes=[pt.t()], sig=(j == no - 1))
            oc0 = g * no
            P.op("dve", lambda e: e.tensor_tensor(modv[:, base + oc0 * 2:base + (oc0 + no) * 2].rearrange("p (o t) -> p o t", t=2),
                                                  pt[:, 0:no * 2].rearrange("p (o t) -> p o t", t=2),
                                                  bmod[:, i * noc + oc0:i * noc + oc0 + no].unsqueeze(2).to_broadcast([128, no, 2]),
                                                  ALU.add),
                 reads=[pt.t(), bmod.t()], writes=[modv.t()])
            yield
        for (v, which) in ((1, 0), (4, 1)):
            for g in range(2):
                b0 = mvcol(i, v, 0, g)
                sl = modv[:, b0:b0 + 2 * KC:2]
                P.op("dve", lambda e: e.scalar_tensor_tensor(sl, sl, 1.0, ng[:, (i * 2 + which) * KC:(i * 2 + which + 1) * KC],
                                                             ALU.add, ALU.mult),
                     reads=[modv.t(), ng.t()], writes=[modv.t()])

    def st_modvec(i):
        with P.scope():
            for _ in modvec_gen(i, 512, 2):
                pass

    class Stepper:
        def __init__(s_, gen, nsteps, total):
            s_.gen, s_.per = gen, (total + nsteps - 1) // nsteps

        def step(s_):
            if s_.gen is None:
                return
            for _ in range(s_.per):
                try:
                    next(s_.gen)
                except StopIteration:
                    s_.gen = None
                    return

        def finish(s_):
            while s_.gen is not None:
                s_.step()

    def rms_rstd(xs, n, rs, sqs):
        ps = P.psum()
        P.op("dve", lambda e: e.tensor_tensor(sqs[:, :, 0:n], xs[:, :, 0:n], xs[:, :, 0:n], ALU.mult),
             reads=[xs.t()], writes=[sqs.t()])
        for kc in range(KC):
            P.op("pe", lambda e: e.matmul(ps[:, 0:n], onesD[:], sqs[:, kc, 0:n], start=(kc == 0), stop=(kc == KC - 1)),
                 reads=[sqs.t(), onesD.t()], writes=[ps.t()], sig=(kc == KC - 1))
        P.op("act", lambda e: e.activation(out=rs[:, 0:n], in_=ps[:, 0:n], func=AF.Ln, bias=EPS),
             reads=[ps.t()], writes=[rs.t()])
        P.op("act", lambda e: e.activation(out=rs[:, 0:n], in_=rs[:, 0:n], func=AF.Exp, scale=-0.5),
             reads=[rs.t()], writes=[rs.t()])

    def norm_mod(src, t0, n, dstbuf, c0, wk, i, va, vs, g, xs, rs, sqs, tmps):
        P.dma("sp", xs[:, :, 0:n], src[:, t0:t0 + n].rearrange("(kc p) n -> p kc n", p=128), writes=[xs.t()])
        rms_rstd(xs, n, rs, sqs)
        for kc in range(KC):
            tmp = tmps[kc % 2]
            P.op("dve", lambda e: e.tensor_tensor(tmp[:, 0:n], xs[:, kc, 0:n], rs[:, 0:n], ALU.mult),
                 reads=[xs.t(), rs.t()], writes=[tmp.t()])
            P.op("act", lambda e: e.activation(out=dstbuf[:, kc, c0:c0 + n], in_=tmp[:, 0:n], func=AF.Identity,
                                               bias=mv(i, vs, kc, g), scale=mv(i, va, kc, g)),
                 reads=[tmp.t(), modv.t()], writes=wk(kc))

    def norm_gen(src, subs, dstbuf, wkf, i, va, vs, g, nb):
        def stage1(k):
            t0, n, c0 = subs[k]
            xs, sq, rs = nb["xs"][k % 2], nb["sq"][k % 2], nb["rs"][k % 2]
            P.dma("sp", xs[:, :, 0:n], src[:, t0:t0 + n].rearrange("(kc p) n -> p kc n", p=128), writes=[xs.t()])
            rms_rstd(xs, n, rs, sq)

        def stage2(k):
            t0, n, c0 = subs[k]
            xs, rs = nb["xs"][k % 2], nb["rs"][k % 2]
            ba, bs = mvcol(i, va, 0, g), mvcol(i, vs, 0, g)
            rb = rs[:, 0:n].unsqueeze(1).to_broadcast([128, KC, n])
            ab = modv[:, ba:ba + 2 * KC:2].unsqueeze(2).to_broadcast([128, KC, n])
            sb_ = modv[:, bs:bs + 2 * KC:2].unsqueeze(2).to_broadcast([128, KC, n])
            P.op("dve", lambda e: e.tensor_tensor(xs[:, :, 0:n], xs[:, :, 0:n], rb, ALU.mult),
                 reads=[xs.t(), rs.t()], writes=[xs.t()])
            P.op("dve", lambda e: e.tensor_tensor(xs[:, :, 0:n], xs[:, :, 0:n], ab, ALU.mult),
                 reads=[xs.t(), modv.t()], writes=[xs.t()])
            P.op("dve", lambda e: e.tensor_tensor(dstbuf[:, :, c0:c0 + n], xs[:, :, 0:n], sb_, ALU.add),
                 reads=[xs.t(), modv.t()], writes=[t_ for kc in range(KC) for t_ in wkf(kc, c0, n)])
        if not subs:
            return
        stage1(0)
        for k in range(len(subs)):
            if k + 1 < len(subs):
                stage1(k + 1)
            stage2(k)
            yield

    def norm_run(*a_, **k_):
        for _ in norm_gen(*a_, **k_):
            pass

    def norm_bufs(ns):
        return dict(xs=[P.sb("xs", [128, KC, ns], F32) for _ in range(2)],
                    sq=[P.sb("sq", [128, KC, ns], BF16) for _ in range(2)],
                    rs=[P.sb("rs", [128, ns], F32) for _ in range(2)],
                    tmp=[])

    ACTS = dscr("ACTS", [FC, 128, SEQ], BF16)
    NSUB = 128

    def st_ffn(i, src, dst, ntok, g):
        PT = min(c.TBF, ntok)
        with P.scope():
            h2 = P.sb("h2", [128, KC, PT + 2], BF16)
            gsb = P.sb("gsb", [128, PT + 2], F32)
            cvt = P.sb("cvt", [128, PT], F32)
            val = [P.sb("val", [128, PT], BF16) for _ in range(2)]
            aj = [P.sb("aj", [128, PT], BF16) for _ in range(2)]
            nb = norm_bufs(NSUB)
            wv = [P.sb("wv", [128, KC, 512], BF16) for _ in range(2)]
            wg = [P.sb("wg", [128, KC, 512], BF16) for _ in range(2)]
            for b0 in range(0, ntok, PT):
                tb = min(PT, ntok - b0)
                lo, hi = b0 - 1, b0 + tb + 1
                if lo < 0:
                    for kc in range(KC):
                        P.op("pool", lambda e: e.memset(h2[:, kc, 0:1], 0.0), writes=h2.seg(kc, 0, 1))
                if hi > ntok:
                    for kc in range(KC):
                        P.op("pool", lambda e: e.memset(h2[:, kc, tb + 1:tb + 2], 0.0), writes=h2.seg(kc, tb + 1, tb + 2))
                a, bnd = max(lo, 0), min(hi, ntok)
                norm_run(src, [(t0, n, t0 - lo) for (t0, n) in _split(a, bnd, NSUB)], h2,
                         (lambda kc, c0, n: h2.seg(kc, c0, c0 + n)), i, 4, 3, g, nb)
                nsub = (tb + 511) // 512
                for jg in range((FC + 3) // 4):
                    nj = min(4, FC - jg * 4)
                    wvj, wgj = wv[jg % 2], wg[jg % 2]
                    P.dma("pool", wvj[:, :, 0:nj * 128], w_up[i, :, jg * 512:jg * 512 + nj * 128].rearrange("(kc p) n -> p kc n", p=128),
                          writes=[wvj.t()])
                    P.dma("pool", wgj[:, :, 0:nj * 128], w_up[i, :, DFF + jg * 512:DFF + jg * 512 + nj * 128].rearrange("(kc p) n -> p kc n", p=128),
                          writes=[wgj.t()])
                    for jj in range(nj):
                        j = jg * 4 + jj
                        ws = slice(jj * 128, (jj + 1) * 128)
                        vj, ajj = val[j % 2], aj[j % 2]
                        for s in range(nsub):
                            n = min(512, tb - s * 512)
                            c0 = 1 + s * 512
                            psv = P.psum()
                            for kc in range(KC):
                                P.op("pe", lambda e: e.matmul(psv[:, 0:n], wvj[:, kc, ws], h2[:, kc, c0:c0 + n],
                                                              start=(kc == 0), stop=(kc == KC - 1)),
                                     reads=[wvj.t()] + h2.seg(kc, c0, c0 + n), writes=[psv.t()], sig=(kc == KC - 1))
                            P.op("act", lambda e: e.copy(vj[:, s * 512:s * 512 + n], psv[:, 0:n]), reads=[psv.t()],
                                 writes=[vj.t(s)])
                            psg = P.psum()
                            for kc in range(KC):
                                P.op("pe", lambda e: e.matmul(psg[:, 0:n], wgj[:, kc, ws], h2[:, kc, c0:c0 + n],
                                                              start=(kc == 0), stop=(kc == KC - 1)),
                                     reads=[wgj.t()] + h2.seg(kc, c0, c0 + n), writes=[psg.t()], sig=(kc == KC - 1))
                            P.op("act", lambda e: e.copy(gsb[:, c0:c0 + n], psg[:, 0:n]), reads=[psg.t()],
                                 writes=[gsb.t(s)])
                        psh = P.psum()
                        for kc in range(KC):
                            P.op("pe", lambda e: e.matmul(psh[:, 0:2], wgj[:, kc, ws], h2[:, kc, 0:tb + 2:tb + 1],
                                                          start=(kc == 0), stop=(kc == KC - 1)),
                                 reads=[wgj.t()] + h2.seg(kc, 0, 1) + h2.seg(kc, tb + 1, tb + 2), writes=[psh.t()],
                                 sig=(kc == KC - 1))
                        P.op("act", lambda e: e.copy(gsb[:, 0:tb + 2:tb + 1], psh[:, 0:2]), reads=[psh.t()],
                             writes=[gsb.t("h")])
                        cb = (i * FC + j) * 4
                        allg = [gsb.t(s) for s in range(nsub)] + [gsb.t("h")]
                        P.op("dve", lambda e: e.tensor_scalar(cvt[:, 0:tb], gsb[:, 0:tb], conv[:, cb:cb + 1], None, ALU.mult),
                             reads=allg + [conv.t()], writes=[cvt.t()])
                        P.op("dve", lambda e: e.scalar_tensor_tensor(cvt[:, 0:tb], gsb[:, 1:tb + 1], conv[:, cb + 1:cb + 2],
                                                                     cvt[:, 0:tb], ALU.mult, ALU.add),
                             reads=allg + [cvt.t()], writes=[cvt.t()])
                        P.op("dve", lambda e: e.scalar_tensor_tensor(cvt[:, 0:tb], gsb[:, 2:tb + 2], conv[:, cb + 2:cb + 3],
                                                                     cvt[:, 0:tb], ALU.mult, ALU.add),
                             reads=allg + [cvt.t()], writes=[cvt.t()])
                        P.op("act", lambda e: e.activation(out=cvt[:, 0:tb], in_=cvt[:, 0:tb], func=AF.Gelu,
                                                           bias=conv[:, cb + 3:cb + 4]),
                             reads=[cvt.t(), conv.t()], writes=[cvt.t()])
                        P.op("dve", lambda e: e.tensor_tensor(ajj[:, 0:tb], cvt[:, 0:tb], vj[:, 0:tb], ALU.mult),
                             reads=[cvt.t()] + [vj.t(s) for s in range(nsub)], writes=[ajj.t()])
                        P.dma("sp", ACTS[j, :, b0:b0 + tb], ajj[:, 0:tb], reads=[ajj.t()])
        TB = min(1024, ntok)
        with P.scope():
            act = P.sb("actT", [128, FC, TB], BF16)
            wd = [P.sb("wd", [128, FC, 512], BF16) for _ in range(2)]
            xo = [P.sb("xo", [128, 512], F32) for _ in range(2)]
            xn = [P.sb("xn", [128, 512], F32) for _ in range(2)]
            kk = 0
            nwd = 0
            for b0 in range(0, ntok, TB):
                tb = min(TB, ntok - b0)
                nsub = (tb + 511) // 512
                for jq in range(0, FC, 11):
                    je = min(jq + 11, FC)
                    P.dma("sp", act[:, jq:je, 0:tb], ACTS[jq:je, :, b0:b0 + tb].rearrange("j p n -> p j n"),
                          writes=[act.t(jq)])
                for ocg in range(D // 512):
                    wdo = wd[nwd % 2]
                    nwd += 1
                    P.dma("pool", wdo[:], w_down[i, :, ocg * 512:(ocg + 1) * 512].rearrange("(fc p) n -> p fc n", p=128),
                          writes=[wdo.t()])
                    for o4 in range(4):
                        oc = ocg * 4 + o4
                        for s in range(nsub):
                            n = min(512, tb - s * 512)
                            k = kk % 2
                            kk += 1
                            P.dma("sp", xo[k][:, 0:n], src[oc * 128:(oc + 1) * 128, b0 + s * 512:b0 + s * 512 + n],
                                  writes=[xo[k].t()])
                            ps = P.psum()
                            for j in range(FC):
                                P.op("pe", lambda e: e.matmul(ps[:, 0:n], wdo[:, j, o4 * 128:(o4 + 1) * 128], act[:, j, s * 512:s * 512 + n],
                                                              start=(j == 0), stop=(j == FC - 1)),
                                     reads=[wdo.t(), act.t((j // 11) * 11)], writes=[ps.t()], sig=(j == FC - 1))
                            P.op("dve", lambda e: e.scalar_tensor_tensor(xn[k][:, 0:n], ps[:, 0:n], mv(i, 5, oc, g),
                                                                         xo[k][:, 0:n], ALU.mult, ALU.add),
                                 reads=[ps.t(), xo[k].t(), modv.t()], writes=[xn[k].t()])
                            P.dma("sp", dst[oc * 128:(oc + 1) * 128, b0 + s * 512:b0 + s * 512 + n], xn[k][:, 0:n],
                                  reads=[xn[k].t()])

    def st_copy(src, dst, ntok):
        with P.scope():
            xs = [P.sb("cp", [128, KC, 512], F32) for _ in range(2)]
            k = 0
            for t0 in range(0, ntok, 512):
                n = min(512, ntok - t0)
                P.dma("sp", xs[k][:, :, 0:n], src[:, t0:t0 + n].rearrange("(kc p) n -> p kc n", p=128), writes=[xs[k].t()])
                P.dma("sp", dst[:, t0:t0 + n].rearrange("(kc p) n -> p kc n", p=128), xs[k][:, :, 0:n], reads=[xs[k].t()])
                k ^= 1

    def st_final(src):
        with P.scope():
            xs = P.sb("xs", [128, KC, 512], F32)
            rs = P.sb("rs", [128, 512], F32)
            sqs = P.sb("sq", [128, KC, 512], BF16)
            ot = [P.sb("ot", [128, KC, 512], F32) for _ in range(2)]
            k = 0
            for t0 in range(0, SEQ, 512):
                n = min(512, SEQ - t0)
                P.dma("sp", xs[:, :, 0:n], src[:, t0:t0 + n].rearrange("(kc p) n -> p kc n", p=128), writes=[xs.t()])
                rms_rstd(xs, n, rs, sqs)
                o = ot[k]
                for kc in range(KC):
                    P.op("dve", lambda e: e.scalar_tensor_tensor(o[:, kc, 0:n], xs[:, kc, 0:n], fg[:, kc:kc + 1],
                                                                 rs[:, 0:n], ALU.mult, ALU.mult),
                         reads=[xs.t(), rs.t(), fg.t()], writes=[o.t(kc)])
                P.dma("sp", outT[:, t0:t0 + n].rearrange("(kc p) n -> p kc n", p=128), o[:, :, 0:n],
                      reads=[o.t(kc) for kc in range(KC)])
                k ^= 1

    def st_pool(i, src, dst, ntok, g, nxt=None):
        o = i // 2
        K4 = KC // 4
        TBP = 512
        icn = icnt_l if g == 0 else icnt_c
        with P.scope():
            hp = P.sb("hp", [128, KC, TBP + 16], F32)
            pl = P.sb("pl", [128, KC, TBP], BF16)
            ic = P.sb("ic", [128, 4, TBP], F32)
            WA = P.sb("WA", [128, KC, TBP + 16], F32)
            WB = P.sb("WB", [128, KC, TBP + 16], F32)
            nb = norm_bufs(128)
            mst = Stepper(modvec_gen(nxt, 256, 3), (ntok + TBP - 1) // TBP, 6 * D // 256) if nxt is not None else None
            wp = P.sb("wp", [128, 4, K4, c.PG], BF16)
            gp = P.sb("gp", [128, KC], F32)
            psc = P.sb("psc", [128, KC], F32)
            xo = [P.sb("xo", [128, 512], F32) for _ in range(2)]
            xn = [P.sb("xn", [128, 512], F32) for _ in range(2)]
            for gi in range(4):
                P.dma("pool", wp[:, gi, :, :], pool_w[o, gi].rearrange("(k p) n -> p k n", p=128), writes=[wp.t(gi)])
            P.dma("sp", psc[:], pool_scT[:, o * KC:(o + 1) * KC], writes=[psc.t()])
            b0c = mvcol(i, 2, 0, g)
            P.op("dve", lambda e: e.tensor_tensor(gp[:], modv[:, b0c:b0c + 2 * KC:2], psc[:], ALU.mult),
                 reads=[modv.t(), psc.t()], writes=[gp.t()])
            kk = 0
            for b0 in range(0, ntok, TBP):
                tb = min(TBP, ntok - b0)
                L = tb + 16
                lo, hi = b0 - 8, b0 + tb + 8
                a, bnd = max(lo, 0), min(hi, ntok)
                hkeys = []
                if lo < 0:
                    P.op("pool", lambda e: e.memset(hp[:, :, 0:8], 0.0), writes=[t_ for kc in range(KC) for t_ in hp.seg(kc, 0, 8)])
                if hi > ntok:
                    P.op("pool", lambda e: e.memset(hp[:, :, tb + 8:tb + 16], 0.0), writes=[t_ for kc in range(KC) for t_ in hp.seg(kc, tb + 8, tb + 16)])
                subs = [(t0, n, t0 - lo) for (t0, n) in _split(a, bnd, 128)]
                if mst:
                    mst.step()
                norm_run(src, subs, hp, (lambda kc, c0, n: hp.seg(kc, c0, c0 + n)), i, 1, 0, g, nb)
                for w in range(4):
                    P.dma("sp", ic[:, w, 0:tb], icn[w:w + 1, b0:b0 + tb].partition_broadcast(128), writes=[ic.t(w)])
                hall = [t_ for kc in range(KC) for t_ in hp.seg(kc, 0, L)]
                P.op("dve", lambda e: e.tensor_tensor(WA[:, :, 1:L], hp[:, :, 0:L - 1], hp[:, :, 1:L], ALU.add),
                     reads=hall, writes=[WA.t()])
                P.op("dve", lambda e: e.tensor_tensor(WB[:, K4:KC, 2:L - 1], WA[:, K4:KC, 1:L - 2], WA[:, K4:KC, 3:L], ALU.add),
                     reads=[WA.t()], writes=[WB.t()])
                P.op("dve", lambda e: e.tensor_tensor(WA[:, 2 * K4:KC, 4:L - 3], WB[:, 2 * K4:KC, 2:L - 5], WB[:, 2 * K4:KC, 6:L - 1], ALU.add),
                     reads=[WB.t()], writes=[WA.t()])
                P.op("dve", lambda e: e.tensor_tensor(WB[:, 3 * K4:KC, 8:L - 7], WA[:, 3 * K4:KC, 4:L - 11], WA[:, 3 * K4:KC, 12:L - 3], ALU.add),
                     reads=[WA.t()], writes=[WB.t()])
                for gi in range(4):
                    cur = WA if gi % 2 == 0 else WB
                    en = "dve" if gi < 2 else "pool"
                    ka, kb = gi * K4, (gi + 1) * K4
                    icb = ic[:, gi, 0:tb].unsqueeze(1).to_broadcast([128, K4, tb])
                    P.op(en, lambda e: e.tensor_tensor(cur[:, ka:kb, 8:8 + tb], cur[:, ka:kb, 8:8 + tb], icb, ALU.mult),
                         reads=[WA.t(), WB.t(), ic.t(gi)], writes=[cur.t(("f", gi))])
                    P.op(en, lambda e: e.tensor_tensor(pl[:, ka:kb, 0:tb], cur[:, ka:kb, 8:8 + tb], hp[:, ka:kb, 8:8 + tb], ALU.subtract),
                         reads=[cur.t(("f", gi))] + hall, writes=[pl.t(gi)])
                for gi in range(4):
                    for oc in range(K4):
                        ocg = gi * K4 + oc
                        k = kk % 2
                        kk += 1
                        P.dma("sp", xo[k][:, 0:tb], src[ocg * 128:(ocg + 1) * 128, b0:b0 + tb], writes=[xo[k].t()])
                        ps = P.psum()
                        for k4 in range(K4):
                            P.op("pe", lambda e: e.matmul(ps[:, 0:tb], wp[:, gi, k4, oc * 128:(oc + 1) * 128],
                                                          pl[:, gi * K4 + k4, 0:tb], start=(k4 == 0), stop=(k4 == K4 - 1)),
                                 reads=[wp.t(gi), pl.t(gi)], writes=[ps.t()], sig=(k4 == K4 - 1))
                        P.op("dve", lambda e: e.scalar_tensor_tensor(xn[k][:, 0:tb], ps[:, 0:tb], gp[:, ocg:ocg + 1],
                                                                     xo[k][:, 0:tb], ALU.mult, ALU.add),
                             reads=[ps.t(), xo[k].t(), gp.t()], writes=[xn[k].t()])
                        P.dma("sp", dst[ocg * 128:(ocg + 1) * 128, b0:b0 + tb], xn[k][:, 0:tb], reads=[xn[k].t()])

            if mst:
                mst.finish()

    NT = CTX + SEQ
    AW, BQK, BV, NAH, GH, DIN = c.AW, c.BQK, c.BV, c.NAH, c.GH, c.DIN
    NCC = CTX // 128
    NCH = NT // 128
    ROWS = c.ROWS
    NB64 = c.NBLK * 64
    QKA = dscr("QKA", [2 * AW, NT], BF16)
    VA = dscr("VA", [NT, AW], BF16)
    QKB = dscr("QKB", [2 * BQK, NT], BF16)
    VBm = dscr("VBm", [NT, BV], BF16)
    GBs = dscr("GBs", [BV, NT], BF16)
    ABT = dscr("ABT", [32, NT], F32)
    OT = dscr("OT", [D, NT], BF16)
    segs = [("qa", 0, AW), ("ka", AW, 2 * AW), ("va", 2 * AW, 3 * AW), ("qb", 3 * AW, 3 * AW + BQK),
            ("kb", 3 * AW + BQK, 3 * AW + 2 * BQK), ("vb", 3 * AW + 2 * BQK, 3 * AW + 2 * BQK + BV),
            ("gb", 3 * AW + 2 * BQK + BV, 3 * AW + 2 * BQK + 2 * BV), ("ab", DIN - 32, DIN)]

    def ctype(col):
        for (nm, a, b) in segs:
            if a <= col < b:
                return nm, col - a
        raise ValueError

    def e_proj(i, e_, src, tok0, ntok, g):
        TB = min(c.TBE, ntok)
        with P.scope():
            hTs = [P.sb("hT", [128, KC, TB], BF16) for _ in range(2)]
            nb = norm_bufs(256)
            wt = [P.sb("wi", [128, KC, 512], BF16) for _ in range(2)]
            st = [P.sb("st", [128, 512], BF16) for _ in range(3)]
            stf = [P.sb("stf", [32, 512], F32) for _ in range(2)]
            nst = 0
            blocks = list(range(0, ntok, TB))

            def mk_norm(bi):
                b0_ = blocks[bi]
                tb_ = min(TB, ntok - b0_)
                hb = hTs[bi % 2]
                subs_ = [(t0, min(256, tb_ - t0)) for t0 in range(0, tb_, 256)]
                return norm_gen(src, [(b0_ + t0, n, t0) for (t0, n) in subs_], hb, (lambda kc, c0, n: [hb.t((c0, kc))]), i, 1, 0, g, nb)

            for _ in mk_norm(0):
                pass
            for bi, b0 in enumerate(blocks):
                tb = min(TB, ntok - b0)
                hT = hTs[bi % 2]
                subs = [(t0, min(256, tb - t0)) for t0 in range(0, tb, 256)]
                nxt_norm = mk_norm(bi + 1) if bi + 1 < len(blocks) else None

                def hk(kc, ca, cb):
                    return [hT.t((t0, kc)) for (t0, n) in subs if t0 < cb and t0 + n > ca]

                ng_ = (DIN + 511) // 512
                for cg in range(ng_):
                    if nxt_norm is not None and cg % 3 == 2:
                        try:
                            next(nxt_norm)
                        except StopIteration:
                            nxt_norm = None
                    w = wt[cg % 2]
                    ncol = min(512, DIN - cg * 512)
                    P.dma("pool", w[:, :, 0:ncol], w_in[e_, :, cg * 512:cg * 512 + ncol].rearrange("(kc p) n -> p kc n", p=128),
                          writes=[w.t()])
                    j = 0
                    while j * 128 < ncol:
                        col = cg * 512 + j * 128
                        nm, off = ctype(col)
                        if nm in ("va", "vb"):
                            j2 = j
                            while (j2 + 1) * 128 < ncol and ctype(cg * 512 + (j2 + 1) * 128)[0] == nm:
                                j2 += 1
                            nr = (j2 - j + 1) * 128
                            dstT = VA if nm == "va" else VBm
                            for tc in range(0, tb, 128):
                                ps = P.psum()
                                for kc in range(KC):
                                    P.op("pe", lambda e: e.matmul(ps[:, 0:nr], hT[:, kc, tc:tc + 128], w[:, kc, j * 128:j * 128 + nr],
                                                                  start=(kc == 0), stop=(kc == KC - 1)),
                                         reads=[w.t()] + hk(kc, tc, tc + 128), writes=[ps.t()], sig=(kc == KC - 1))
                                s_ = st[nst % 3]
                                nst += 1
                                P.op("act", lambda e: e.copy(s_[:, 0:nr], ps[:, 0:nr]), reads=[ps.t()], writes=[s_.t()])
                                P.dma("sp", dstT[tok0 + b0 + tc:tok0 + b0 + tc + 128, off:off + nr], s_[:, 0:nr], reads=[s_.t()])
                            j = j2 + 1
                            continue
                        m = 32 if nm == "ab" else 128
                        for s0 in range(0, tb, 512):
                            n = min(512, tb - s0)
                            ps = P.psum()
                            for kc in range(KC):
                                P.op("pe", lambda e: e.matmul(ps[0:m, 0:n], w[:, kc, j * 128:j * 128 + m], hT[:, kc, s0:s0 + n],
                                                              start=(kc == 0), stop=(kc == KC - 1)),
                                     reads=[w.t()] + hk(kc, s0, s0 + n), writes=[ps.t()], sig=(kc == KC - 1))
                            tcol = tok0 + b0 + s0
                            if nm == "ab":
                                s_ = stf[nst % 2]
                                nst += 1
                                P.op("act", lambda e: e.copy(s_[:, 0:n], ps[0:32, 0:n]), reads=[ps.t()], writes=[s_.t()])
                                P.dma("sp", ABT[:, tcol:tcol + n], s_[:, 0:n], reads=[s_.t()])
                            else:
                                s_ = st[nst % 3]
                                nst += 1
                                if nm == "qb":
                                    P.op("act", lambda e: e.mul(s_[:, 0:n], ps[:, 0:n], 0.125), reads=[ps.t()], writes=[s_.t()])
                                elif nm == "gb":
                                    P.op("act", lambda e: e.activation(out=s_[:, 0:n], in_=ps[:, 0:n], func=AF.Silu),
                                         reads=[ps.t()], writes=[s_.t()])
                                else:
                                    P.op("act", lambda e: e.copy(s_[:, 0:n], ps[:, 0:n]), reads=[ps.t()], writes=[s_.t()])
                                if nm in ("qa", "ka"):
                                    r0 = off + (AW if nm == "ka" else 0)
                                    dd = QKA[r0:r0 + 128, tcol:tcol + n]
                                elif nm in ("qb", "kb"):
                                    r0 = off + (BQK if nm == "kb" else 0)
                                    dd = QKB[r0:r0 + 128, tcol:tcol + n]
                                else:
                                    dd = GBs[off:off + 128, tcol:tcol + n]
                                P.dma("sp", dd, s_[:, 0:n], reads=[s_.t()])
                        j += 1

                if nxt_norm is not None:
                    for _ in nxt_norm:
                        pass

    def e_na(e_, need_ctx, nxt=None):
        scale = 128 ** -0.5
        with P.scope():
            P.nrot = 3
            kT = [P.sb("kT", [128, NT], BF16) for _ in range(2)]
            qT = [P.sb("qT", [128, NT], BF16) for _ in range(2)]
            V = [P.sb("V", [128, NCH, 128], BF16) for _ in range(2)]
            TF = [P.sb("TF", [128, NB64], F32) for _ in range(2)]
            TZ = [P.sb("TZ", [128, NB64], F32) for _ in range(2)]
            mF = P.sb("mF", [128, NB64], F32)
            mZ = P.sb("mZ", [128, NB64], F32)
            ones_bf = P.sb("ones_bf", [128, 128], BF16)
            ex = [P.sb("ex", [128, 512], F32) for _ in range(2)]
            pT = [P.sb("pT", [128, 512], BF16) for _ in range(4)]
            rd = [P.sb("rd", [128, 512], F32) for _ in range(2)]
            oT = [P.sb("oT", [128, 512], BF16) for _ in range(2)]
            P.dma("sp", mF[:], nmask[0], writes=[mF.t()])
            P.dma("sp", mZ[:], nmask[1], writes=[mZ.t()])
            P.dma("pool", ones_bf[:], cmat[2], writes=[ones_bf.t()])
            cnt = 0
            ntile = 0
            mst = Stepper(modvec_gen(nxt, 512, 3), NAH, 6 * D // 512) if nxt is not None else None
            for h in range(NAH):
                if mst:
                    mst.step()
                k_, q_, v_, tf, tz = kT[h % 2], qT[h % 2], V[h % 2], TF[h % 2], TZ[h % 2]
                P.dma("sp", k_[:], QKA[AW + h * 128:AW + (h + 1) * 128, :], writes=[k_.t()])
                P.dma("sp", q_[:], QKA[h * 128:(h + 1) * 128, :], writes=[q_.t()])
                P.dma("sp", v_[:], VA[:, h * 128:(h + 1) * 128].rearrange("(ch p) d -> p ch d", p=128), writes=[v_.t()])
                P.dma("sp", tf[:], rpbx[e_, h], writes=[tf.t()])
                P.op("act", lambda e: e.activation(out=tf[:], in_=tf[:], func=AF.Exp), reads=[tf.t()], writes=[tf.t()])
                P.op("dve", lambda e: e.tensor_tensor(tz[:], tf[:], mZ[:], ALU.mult), reads=[tf.t(), mZ.t()], writes=[tz.t()])
                P.op("dve", lambda e: e.tensor_tensor(tf[:], tf[:], mF[:], ALU.mult), reads=[tf.t(), mF.t()], writes=[tf.t()])
                tiles = [("lat", qt) for qt in range(ROWS // 8)] + ([("ctx", 0)] if need_ctx else [])
                items = []
                for (kind, qt) in tiles:
                    if kind == "lat":
                        r0 = qt * 8
                        qc0, nq = CTX + qt * 512, 512
                        c_lo, c_hi = max(0, (r0 - 4) // 2), min(ROWS // 2 - 1, (r0 + 10) // 2)
                        chunks = [("c", cc) for cc in range(NCC)] + [("l", cc) for cc in range(c_lo, c_hi + 1)]
                    else:
                        r0 = 0
                        qc0, nq = 0, CTX
                        chunks = [("c", cc) for cc in range(NCC)]
                    acc = (P.ps[3 + 2 * (ntile % 2)], P.ps[4 + 2 * (ntile % 2)], ntile)
                    ntile += 1
                    for idx, (ck, cc) in enumerate(chunks):
                        items.append(dict(r0=r0, qc0=qc0, nq=nq, ck=ck, cc=cc, first=(idx == 0), last=(idx == len(chunks) - 1),
                                          acc=acc))

                def emit_s(it):
                    ck, cc, nq, qc0, r0 = it["ck"], it["cc"], it["nq"], it["qc0"], it["r0"]
                    kc0 = cc * 128 if ck == "c" else CTX + cc * 128
                    ps_s = P.psum()
                    P.op("pe", lambda e: e.matmul(ps_s[:, 0:nq], k_[:, kc0:kc0 + 128], q_[:, qc0:qc0 + nq], start=True, stop=True),
                         reads=[k_.t(), q_.t()], writes=[ps_s.t()])
                    p_ = pT[it["n"] % 4]
                    it["p"] = p_
                    if ck == "c":
                        P.op("act", lambda e: e.activation(out=p_[:, 0:nq], in_=ps_s[:, 0:nq], func=AF.Exp, scale=scale),
                             reads=[ps_s.t()], writes=[p_.t()])
                    else:
                        x_ = ex[it["n"] % 2]
                        P.op("act", lambda e: e.activation(out=x_[:, 0:nq], in_=ps_s[:, 0:nq], func=AF.Exp, scale=scale),
                             reads=[ps_s.t()], writes=[x_.t()])
                        bb = r0 - 2 * cc + 11
                        fr = None
                        if r0 == 0 and cc <= 3:
                            fr = (0, 256)
                        if r0 == ROWS - 8 and cc >= ROWS // 2 - 4:
                            fr = (256, 512)
                        rngs = [(0, 512, tz)] if fr is None else [(fr[0], fr[1], tf), (256 - fr[0], 512 - fr[0], tz)]
                        for (ca, cb, tab) in rngs:
                            P.op("dve", lambda e: e.tensor_tensor(p_[:, ca:cb], x_[:, ca:cb], tab[:, bb * 64 + ca:bb * 64 + cb], ALU.mult),
                                 reads=[x_.t(), tab.t()], writes=[p_.t()])

                def emit_pv(it):
                    ps_o, ps_d, nt = it["acc"]
                    nq, qc0, p_ = it["nq"], it["qc0"], it["p"]
                    chi = it["cc"] if it["ck"] == "c" else NCC + it["cc"]
                    P.op("pe", lambda e: e.matmul(ps_o[:, 0:nq], v_[:, chi, :], p_[:, 0:nq], start=it["first"], stop=it["last"]),
                         reads=[v_.t(), p_.t()], writes=[ps_o.t()], sig=True)
                    P.op("pe", lambda e: e.matmul(ps_d[:, 0:nq], ones_bf[:], p_[:, 0:nq], start=it["first"], stop=it["last"]),
                         reads=[ones_bf.t(), p_.t()], writes=[ps_d.t()], sig=True)
                    if it["last"]:
                        r_, o_ = rd[nt % 2], oT[nt % 2]
                        P.op("dve", lambda e: e.reciprocal(r_[:, 0:nq], ps_d[:, 0:nq]), reads=[ps_d.t()], writes=[r_.t()])
                        P.op("dve", lambda e: e.tensor_tensor(o_[:, 0:nq], ps_o[:, 0:nq], r_[:, 0:nq], ALU.mult),
                             reads=[ps_o.t(), r_.t()], writes=[o_.t()])
                        P.dma("sp", OT[h * 128:(h + 1) * 128, qc0:qc0 + nq], o_[:, 0:nq], reads=[o_.t()])

                LA = 2
                for n_, it in enumerate(items):
                    it["n"] = cnt + n_
                for n_ in range(len(items) + LA):
                    if n_ < len(items):
                        emit_s(items[n_])
                    if n_ >= LA:
                        emit_pv(items[n_ - LA])
                cnt += len(items)
            if mst:
                mst.finish()
            P.nrot = 7

    def e_gla(e_, need_ctx):
        NFC = BQK // 128
        with P.scope():
            Cs = [P.sb("ropeC", [128, 512], F32) for _ in range(2)]
            Ss = [P.sb("ropeS", [128, 512], F32) for _ in range(2)]
            permb = P.sb("permb", [128, 128], BF16)
            identb = P.sb("identb", [128, 128], BF16)
            ones128 = P.sb("ones128", [128, 128], F32)
            onesr = P.sb("onesr", [1, 128], F32)
            tri = [P.sb("tri", [128, 128], F32) for _ in range(2)]
            wg2s = P.sb("wg2s", [32, 2 * BQK], F32)
            bgs = P.sb("bgs", [1, 2 * BQK], F32)
            abT = P.sb("abT", [32, NT], F32)
            gg = P.sb("gg", [128, GH], F32)
            qr = P.sb("qr", [128, NT], BF16)
            kr = P.sb("kr", [128, NT], BF16)
            qtil = P.sb("qtil", [128, NT], BF16)
            ktil = P.sb("ktil", [128, NT], BF16)
            khat = P.sb("khat", [128, NCH, 128], BF16)
            vv = P.sb("vv", [128, NCH, 256], BF16)
            dec = P.sb("dec", [128, NCH], F32)
            ob = [P.sb("ob", [128, NT], F32) for _ in range(2)]
            St = P.sb("St", [128, 128], F32)
            t1 = [P.sb("t1", [128, 512], F32) for _ in range(2)]
            t2 = [P.sb("t2", [128, 512], F32) for _ in range(2)]
            kh = [P.sb("kh", [128, 128], BF16) for _ in range(4)]
            xk = [P.sb("xk", [128, 128], F32) for _ in range(4)]
            sg = [P.sb("sg", [128, 512], BF16) for _ in range(2)]
            yo = [P.sb("yo", [128, 512], BF16) for _ in range(2)]
            P.dma("pool", permb[:], cmat[4], writes=[permb.t()])
            P.dma("pool", identb[:], cmat[3], writes=[identb.t()])
            P.dma("sp", ones128[:], cmat[1], writes=[ones128.t()])
            P.dma("sp", onesr[:], cmat[2][0:1, :], writes=[onesr.t()])
            P.dma("sp", tri[0][:], cmat[5], writes=[tri[0].t()])
            P.dma("sp", tri[1][:], cmat[6], writes=[tri[1].t()])
            P.dma("sp", wg2s[:], wg2[e_], writes=[wg2s.t()])
            P.dma("sp", bgs[:], bg[e_], writes=[bgs.t()])
            P.dma("sp", abT[:], ABT, writes=[abT.t()])
            P.dma("sp", gg[:], gla_gT[:, e_ * GH:(e_ + 1) * GH], writes=[gg.t()])
            k2 = 0
            for fc in range(NFC):
                P.dma("sp", qr[:], QKB[fc * 128:(fc + 1) * 128, :], writes=[qr.t()])
                P.dma("sp", kr[:], QKB[BQK + fc * 128:BQK + (fc + 1) * 128, :], writes=[kr.t()])
                P.dma("sp", vv[:], VBm[:, fc * 256:(fc + 1) * 256].rearrange("(ch p) d -> p ch d", p=128), writes=[vv.t()])
                for buf in (qr, kr):
                    for s0 in range(0, SEQ, 512):
                        a, b = CTX + s0, CTX + s0 + 512
                        ps = P.psum()
                        P.op("pe", lambda e: e.matmul(ps[:, 0:512], permb[:], buf[:, a:b], start=True, stop=True),
                             reads=[permb.t(), buf.t()], writes=[ps.t()])
                        x1, x2 = t1[k2 % 2], t2[k2 % 2]
                        C_, S_ = Cs[k2 % 2], Ss[k2 % 2]
                        k2 += 1
                        P.dma("sp", C_[:], rope[0][:, s0:s0 + 512], writes=[C_.t()])
                        P.dma("sp", S_[:], rope[1][:, s0:s0 + 512], writes=[S_.t()])
                        P.op("dve", lambda e: e.tensor_tensor(x1[:], ps[:, 0:512], S_[:], ALU.mult),
                             reads=[ps.t(), S_.t()], writes=[x1.t()])
                        P.op("pool", lambda e: e.tensor_tensor(x2[:], buf[:, a:b], C_[:], ALU.mult),
                             reads=[buf.t(), C_.t()], writes=[x2.t()])
                        P.op("dve", lambda e: e.tensor_tensor(buf[:, a:b], x1[:], x2[:], ALU.add),
                             reads=[x1.t(), x2.t(), ps.t()], writes=[buf.t()])
                for d in range(2):
                  with P.scope():
                    sp_ = P.sb("sp", [128, NCH, 128], F32)
                    cum = P.sb("cum", [128, NT], F32)
                    zc = d * BQK + fc * 128
                    for ch in range(NCH):
                        ps = P.psum()
                        P.op("pe", lambda e: e.matmul(ps[:, 0:128], abT[:, ch * 128:(ch + 1) * 128], wg2s[:, zc:zc + 128], start=True, stop=False),
                             reads=[abT.t(), wg2s.t()], writes=[ps.t()], sig=False)
                        P.op("pe", lambda e: e.matmul(ps[:, 0:128], onesr[:], bgs[:, zc:zc + 128], start=False, stop=True),
                             reads=[onesr.t(), bgs.t()], writes=[ps.t()])
                        P.op("act", lambda e: e.activation(out=sp_[:, ch, :], in_=ps[:, 0:128], func=AF.Exp, scale=-1.0),
                             reads=[ps.t()], writes=[sp_.t(ch)])
                    for ch in range(NCH):
                        P.op("act", lambda e: e.activation(out=sp_[:, ch, :], in_=sp_[:, ch, :], func=AF.Ln, bias=1.0),
                             reads=[sp_.t(ch)], writes=[sp_.t(ch)])
                    for c4 in range(0, NCH, 4):
                        nn = min(4, NCH - c4)
                        ps = P.psum()
                        for u in range(nn):
                            P.op("pe", lambda e: e.matmul(ps[:, u * 128:(u + 1) * 128], sp_[:, c4 + u, :], tri[d][:], start=True, stop=True),
                                 reads=[sp_.t(c4 + u), tri[d].t()], writes=[ps.t()])
                        P.op("dve", lambda e: e.tensor_scalar(cum[:, c4 * 128:(c4 + nn) * 128], ps[:, 0:nn * 128], -1.0 / 16, None, ALU.mult),
                             reads=[ps.t()], writes=[cum.t(c4)])
                    allcum = [cum.t(c4) for c4 in range(0, NCH, 4)]
                    lastcol = 127 if d == 0 else 0
                    P.op("act", lambda e: e.activation(out=dec[:], in_=cum[:, lastcol:NT:128], func=AF.Exp),
                         reads=allcum, writes=[dec.t()])
                    for s0 in range(0, NT, 512):
                        n = min(512, NT - s0)
                        x1, x2 = t1[k2 % 2], t2[k2 % 2]
                        k2 += 1
                        P.op("act", lambda e: e.activation(out=x1[:, 0:n], in_=cum[:, s0:s0 + n], func=AF.Exp),
                             reads=allcum, writes=[x1.t()])
                        P.op("dve", lambda e: e.tensor_tensor(qtil[:, s0:s0 + n], qr[:, s0:s0 + n], x1[:, 0:n], ALU.mult),
                             reads=[qr.t(), x1.t()], writes=[qtil.t()])
                        P.op("act", lambda e: e.activation(out=x2[:, 0:n], in_=cum[:, s0:s0 + n], func=AF.Exp, scale=-1.0),
                             reads=allcum, writes=[x2.t()])
                        P.op("pool", lambda e: e.tensor_tensor(ktil[:, s0:s0 + n], kr[:, s0:s0 + n], x2[:, 0:n], ALU.mult),
                             reads=[kr.t(), x2.t()], writes=[ktil.t()])
                    for ch in range(NCH):
                        x1 = t1[k2 % 2]
                        khb = kh[k2 % 2]
                        k2 += 1
                        lc = ch * 128 + lastcol
                        P.op("act", lambda e: e.activation(out=x1[:, 0:128], in_=cum[:, ch * 128:(ch + 1) * 128], func=AF.Exp,
                                                           scale=-1.0, bias=cum[:, lc:lc + 1]),
                             reads=allcum, writes=[x1.t()])
                        P.op("dve", lambda e: e.tensor_tensor(khb[:], kr[:, ch * 128:(ch + 1) * 128], x1[:, 0:128], ALU.mult),
                             reads=[kr.t(), x1.t()], writes=[khb.t()])
                        P.op("pe", lambda e: e.transpose(P.psb[:, (ch % 8) * 128:(ch % 8 + 1) * 128], khb[:], identb[:]),
                             reads=[khb.t(), identb.t()], writes=[P.psb.t(ch % 8)])
                        P.op("act", lambda e: e.copy(khat[:, ch, :], P.psb[:, (ch % 8) * 128:(ch % 8 + 1) * 128]),
                             reads=[P.psb.t(ch % 8)], writes=[khat.t(ch)])
                  with P.scope():
                    aTa = P.sb("aTa", [128, NCH, 2, 128], BF16)
                    Sba = P.sb("Sba", [128, NCH, 128], BF16)
                    P.op("dve", lambda e: e.memset(St[:], 0.0), writes=[St.t()])
                    if d == 0:
                        order = list(range(NCH))
                    else:
                        order = list(range(NCC - 1, -1, -1)) + list(range(NCH - 1, NCC - 1, -1))
                    for ch in order:
                        a, b = ch * 128, (ch + 1) * 128
                        P.op("dve", lambda e: e.tensor_copy(Sba[:, ch, :], St[:]), reads=[St.t()], writes=[Sba.t(ch)])
                        ps_kv = P.psum()
                        P.op("pe", lambda e: e.matmul(ps_kv[:, 0:256], khat[:, ch, :], vv[:, ch, :], start=True, stop=True),
                             reads=[khat.t(ch), vv.t()], writes=[ps_kv.t()])
                        for hh in range(2):
                            pa, pb = hh * 64, (hh + 1) * 64
                            ps_a = P.psum()
                            P.op("pe", lambda e: e.matmul(ps_a[:, 0:128], ktil[pa:pb, a:b], qtil[pa:pb, a:b], start=True, stop=True),
                                 reads=[ktil.t(), qtil.t()], writes=[ps_a.t()])
                            P.op("dve", lambda e: e.tensor_tensor(aTa[:, ch, hh, :], ps_a[:, 0:128], tri[d][:], ALU.mult),
                                 reads=[ps_a.t(), tri[d].t()], writes=[aTa.t((ch, hh))])
                        for hh in range(2):
                            pa, pb = hh * 64, (hh + 1) * 64
                            P.op("dve", lambda e: e.scalar_tensor_tensor(St[pa:pb, :], St[pa:pb, :], dec[pa:pb, ch:ch + 1],
                                                                         ps_kv[pa:pb, hh * 128:(hh + 1) * 128], ALU.mult, ALU.add),
                                 reads=[St.t(), dec.t(), ps_kv.t()], writes=[St.t()])
                    for ch in order:
                        a, b = ch * 128, (ch + 1) * 128
                        for hh in range(2):
                            pa, pb = hh * 64, (hh + 1) * 64
                            ps_o = P.psum()
                            P.op("pe", lambda e: e.matmul(ps_o[:, 0:128], Sba[pa:pb, ch, :], qtil[pa:pb, a:b], start=True, stop=False),
                                 reads=[Sba.t(ch), qtil.t()], writes=[ps_o.t()], sig=False)
                            P.op("pe", lambda e: e.matmul(ps_o[:, 0:128], vv[:, ch, hh * 128:(hh + 1) * 128], aTa[:, ch, hh, :], start=False, stop=True),
                                 reads=[vv.t(), aTa.t((ch, hh))], writes=[ps_o.t()])
                            if d == 0:
                                P.op("act", lambda e: e.copy(ob[hh][:, a:b], ps_o[:, 0:128]), reads=[ps_o.t()], writes=[ob[hh].t(ch)])
                            else:
                                P.op("dve", lambda e: e.tensor_tensor(ob[hh][:, a:b], ob[hh][:, a:b], ps_o[:, 0:128], ALU.add),
                                     reads=[ps_o.t(), ob[hh].t(ch)], writes=[ob[hh].t(ch)])
                tiles_ = []
                for hh in range(2):
                    t_lo = 0 if need_ctx else CTX
                    for s0 in range(t_lo, NT, 512):
                        tiles_.append((hh, s0, min(512, NT - s0)))

                def post_a(k):
                    hh, s0, n = tiles_[k]
                    hg = fc * 2 + hh
                    chs = [ob[hh].t(ch) for ch in range(s0 // 128, (s0 + n) // 128)]
                    x1, x2, s_ = t1[k % 2], t2[k % 2], sg[k % 2]
                    P.dma("sp", s_[:, 0:n], GBs[hg * 128:(hg + 1) * 128, s0:s0 + n], writes=[s_.t()])
                    P.op("dve", lambda e: e.tensor_tensor(x1[:, 0:n], ob[hh][:, s0:s0 + n], ob[hh][:, s0:s0 + n], ALU.mult),
                         reads=chs, writes=[x1.t()])
                    ps = P.psum()
                    P.op("pe", lambda e: e.matmul(ps[:, 0:n], ones128[:], x1[:, 0:n], start=True, stop=True),
                         reads=[ones128.t(), x1.t()], writes=[ps.t()])
                    P.op("act", lambda e: e.activation(out=x2[:, 0:n], in_=ps[:, 0:n], func=AF.Ln, bias=EPS),
                         reads=[ps.t()], writes=[x2.t()])
                    P.op("act", lambda e: e.activation(out=x2[:, 0:n], in_=x2[:, 0:n], func=AF.Exp, scale=-0.5),
                         reads=[x2.t()], writes=[x2.t()])

                def post_b(k):
                    hh, s0, n = tiles_[k]
                    hg = fc * 2 + hh
                    chs = [ob[hh].t(ch) for ch in range(s0 // 128, (s0 + n) // 128)]
                    x1, x2, s_, y_ = t1[k % 2], t2[k % 2], sg[k % 2], yo[k % 2]
                    P.op("dve", lambda e: e.tensor_tensor(x1[:, 0:n], ob[hh][:, s0:s0 + n], x2[:, 0:n], ALU.mult),
                         reads=chs + [x2.t(), x1.t()], writes=[x1.t()])
                    P.op("dve", lambda e: e.scalar_tensor_tensor(y_[:, 0:n], x1[:, 0:n], gg[:, hg:hg + 1], s_[:, 0:n], ALU.mult, ALU.mult),
                         reads=[x1.t(), gg.t(), s_.t()], writes=[y_.t()])
                    P.dma("sp", OT[AW + hg * 128:AW + (hg + 1) * 128, s0:s0 + n], y_[:, 0:n], reads=[y_.t()])

                post_a(0)
                for k in range(len(tiles_)):
                    if k + 1 < len(tiles_):
                        post_a(k + 1)
                    post_b(k)

    def e_out(i, e_, src, dst, tok0, ntok, g):
        TB = min(1024, ntok)
        with P.scope():
            obs = [P.sb("otb", [128, KC, TB], BF16) for _ in range(2)]
            wt = [P.sb("wo", [128, KC, 512], BF16) for _ in range(2)]
            xo = [P.sb("xo", [128, 512], F32) for _ in range(2)]
            xn = [P.sb("xn", [128, 512], F32) for _ in range(2)]
            kk = 0
            blks = list(range(0, ntok, TB))

            def ld(bi):
                b0_ = blks[bi]
                tb_ = min(TB, ntok - b0_)
                P.dma("sp", obs[bi % 2][:, :, 0:tb_], OT[:, tok0 + b0_:tok0 + b0_ + tb_].rearrange("(kc p) n -> p kc n", p=128),
                      writes=[obs[bi % 2].t()])
            ld(0)
            for bi, b0 in enumerate(blks):
                tb = min(TB, ntok - b0)
                ob_ = obs[bi % 2]
                if bi + 1 < len(blks):
                    ld(bi + 1)
                for cg in range(D // 512):
                    w = wt[cg % 2]
                    P.dma("pool", w[:], w_out[e_, :, cg * 512:(cg + 1) * 512].rearrange("(kc p) n -> p kc n", p=128), writes=[w.t()])
                    for j in range(4):
                        oc = cg * 4 + j
                        for s0 in range(0, tb, 512):
                            n = min(512, tb - s0)
                            k = kk % 2
                            kk += 1
                            P.dma("sp", xo[k][:, 0:n], src[oc * 128:(oc + 1) * 128, b0 + s0:b0 + s0 + n], writes=[xo[k].t()])
                            ps = P.psum()
                            for kc in range(KC):
                                P.op("pe", lambda e: e.matmul(ps[:, 0:n], w[:, kc, j * 128:(j + 1) * 128], ob_[:, kc, s0:s0 + n],
                                                              start=(kc == 0), stop=(kc == KC - 1)),
                                     reads=[w.t(), ob_.t()], writes=[ps.t()], sig=(kc == KC - 1))
                            P.op("dve", lambda e: e.scalar_tensor_tensor(xn[k][:, 0:n], ps[:, 0:n], mv(i, 2, oc, g), xo[k][:, 0:n], ALU.mult, ALU.add),
                                 reads=[ps.t(), xo[k].t(), modv.t()], writes=[xn[k].t()])
                            P.dma("sp", dst[oc * 128:(oc + 1) * 128, b0 + s0:b0 + s0 + n], xn[k][:, 0:n], reads=[xn[k].t()])

    def st_even(i, xsrc, csrc, xdst, cdst, need_ctx, nxt):
        e_ = i // 2
        e_proj(i, e_, csrc, 0, CTX, 1)
        e_proj(i, e_, xsrc, CTX, SEQ, 0)
        e_na(e_, need_ctx, nxt)
        e_gla(e_, need_ctx)
        e_out(i, e_, xsrc, xdst, CTX, SEQ, 0)
        if need_ctx:
            e_out(i, e_, csrc, cdst, 0, CTX, 1)

    st_modvec(0)
    xsrc, csrc = xT, cT
    for i in range(DEPTH):
        is_even = i % 2 == 0
        need_ctx = any(j % 2 == 0 for j in range(i + 1, DEPTH))
        nxt = i + 1 if i + 1 < DEPTH else None
        used = False
        if is_even and fl["even"]:
            st_even(i, xsrc, csrc, X[1], C[1], need_ctx, nxt)
            used = True
        elif (not is_even) and fl["odd"]:
            st_pool(i, xsrc, X[1], SEQ, 0, None)
            if need_ctx:
                st_pool(i, csrc, C[1], CTX, 1, None)
        else:
            st_copy(xsrc, X[1], SEQ)
            if need_ctx:
                st_copy(csrc, C[1], CTX)
        if nxt is not None and not used:
            st_modvec(nxt)
        if fl["ffn"]:
            st_ffn(i, X[1], X[0], SEQ, 0)
            if need_ctx:
                st_ffn(i, C[1], C[0], CTX, 1)
        else:
            st_copy(X[1], X[0], SEQ)
            if need_ctx:
                st_copy(C[1], C[0], CTX)
        xsrc, csrc = X[0], C[0]
    st_final(xsrc)
    P.barrier()
    return nc


def _fm(v, KC):
    return np.ascontiguousarray(np.asarray(v, np.float32).reshape(KC, 128).T)


def prep_shared(cfg, inp):
    c = cfg
    KC, FC, DEPTH = c.KC, c.FC, c.DEPTH
    f = lambda a: np.ascontiguousarray(np.asarray(a, np.float32))
    sh = {}
    sh["w_mod"] = f(inp["w_mod"])
    sh["b_modT"] = f(np.stack([_fm(inp["b_mod"][i], 6 * KC) for i in range(DEPTH)], 1).reshape(128, -1))
    ngs = np.stack([np.stack([_fm(inp["norm1_g"][i], KC), _fm(inp["norm2_g"][i], KC)], 1) for i in range(DEPTH)], 1)
    sh["ngT"] = f(ngs.reshape(128, -1))
    sh["fgT"] = _fm(inp["final_g"], KC)
    sh["w_up"] = f(inp["w_up"])
    sh["w_down"] = f(inp["w_down"])
    cw = np.asarray(inp["conv_w"], np.float32)
    cb = np.asarray(inp["conv_b"], np.float32)
    cvt = np.zeros((128, DEPTH, FC, 4), np.float32)
    for i in range(DEPTH):
        for k in range(3):
            cvt[:, i, :, k] = _fm(cw[i, k], FC)
        cvt[:, i, :, 3] = _fm(cb[i], FC)
    sh["convT"] = f(cvt.reshape(128, -1))
    cm = np.zeros((8, 128, 128), np.float32)
    cm[0] = 1.0 / c.D
    cm[1] = 1.0 / 128
    cm[2] = 1.0
    cm[3] = np.eye(128)
    p = np.arange(128)
    perm = np.where((p % 32) < 16, p + 16, p - 16)
    cm[4][perm, p] = 1.0
    jj, ii = np.meshgrid(p, p, indexing="ij")
    cm[5] = np.where(jj <= ii, 1.0, 0.0)
    cm[6] = np.where(jj >= ii, 1.0, 0.0)
    sh["cmat"] = cm
    sh["pool_w"] = f(inp["pool_w"])
    NE, BQK, NAH, NBLK = c.NE, c.BQK, c.NAH, c.NBLK
    sh["w_in"] = f(inp["w_in"])
    sh["w_out"] = f(inp["w_out"])
    wg = np.zeros((NE, 32, 2 * BQK), np.float32)
    w2 = np.asarray(inp["w_gate2"], np.float32)
    for d in range(2):
        wg[:, d * 16:(d + 1) * 16, d * BQK:(d + 1) * BQK] = w2[:, d]
    sh["wg2"] = wg
    sh["bg"] = f(np.asarray(inp["b_gate"], np.float32).reshape(NE, 1, 2 * BQK))
    rp = np.asarray(inp["rpb"], np.float32)
    pp = np.arange(128)
    half = pp // 64
    kcol = pp % 64
    bb = np.arange(NBLK)
    qcol = np.arange(64)
    e_idx = bb[None, :] - 4 - half[:, None]
    ev = (e_idx >= 0) & (e_idx <= 14)
    dr = np.clip(14 - e_idx, 0, 14)
    dc = np.clip(kcol[:, None] - qcol[None, :] + 15, 0, 30)
    rx = rp[:, :, dr[:, :, None], dc[:, None, :]]
    rx = np.where(ev[None, None, :, :, None], rx, 0.0).astype(np.float32)
    sh["rpbx"] = f(rx.reshape(NE, NAH, 128, NBLK * 64))
    cstart = np.clip(qcol - 8, 0, 48)
    col_in = (kcol[:, None] >= cstart[None, :]) & (kcol[:, None] < cstart[None, :] + 16)
    mF = (ev[:, :, None] & col_in[:, None, :])
    mZ = mF & ((e_idx >= 4) & (e_idx <= 11))[:, :, None]
    sh["nmask"] = f(np.stack([mF, mZ], 0).astype(np.float32).reshape(2, 128, NBLK * 64))
    t = np.arange(c.SEQ)
    row = (t // 64).astype(np.float32)
    colp = (t % 64).astype(np.float32)
    dd = pp % 64
    ii = dd % 32
    inv = (np.float32(10000.0) ** (-(np.arange(16, dtype=np.float32)) / np.float32(16))).astype(np.float32)
    pos = np.where((dd // 32)[:, None] == 0, row[None, :], colp[None, :]).astype(np.float32)
    ang = (pos * inv[ii % 16][:, None]).astype(np.float32)
    sgn = np.where(ii < 16, -1.0, 1.0).astype(np.float32)
    sh["rope"] = f(np.stack([np.cos(ang), np.sin(ang) * sgn[:, None]], 0))
    gg = np.asarray(inp["gla_norm_g"], np.float32)
    sh["gla_gT"] = f(gg.transpose(2, 0, 1).reshape(128, -1))
    sh["pool_scT"] = f(np.stack([_fm(inp["pool_scale"][o], KC) for o in range(c.NO)], 1).reshape(128, -1))
    def icnt(L):
        t = np.arange(L)
        out = np.zeros((4, L), np.float32)
        for wi, w in enumerate((2, 4, 8, 16)):
            lo = np.clip(t - w // 2, 0, L); hi = np.clip(t + w // 2, 0, L)
            out[wi] = 1.0 / (hi - lo).astype(np.float32)
        return out
    sh["icnt_l"] = icnt(c.SEQ)
    sh["icnt_c"] = icnt(c.CTX)
    return sh


def prep_core(cfg, inp, b, zero=False):
    c = cfg
    d = {}
    x = np.asarray(inp["x"][b], np.float32)
    cx = np.asarray(inp["ctx"][b], np.float32)
    cvec = np.stack([_fm(inp["c"][b], c.KC), _fm(inp["c_ctx"], c.KC)], 2).reshape(128, -1)
    if zero:
        d["xT"] = np.zeros((c.D, c.SEQ), np.float32)
        d["cT"] = np.zeros((c.D, c.CTX), np.float32)
        d["cv"] = np.zeros_like(cvec)
    else:
        d["xT"] = np.ascontiguousarray(x.T)
        d["cT"] = np.ascontiguousarray(cx.T)
        d["cv"] = np.ascontiguousarray(cvec)
    return d


def run(cfg, inp, flags=None):
    nc = build(cfg, flags)
    sh = prep_shared(cfg, inp)
    names = set()
    in_maps = []
    ncore = 8
    for k in range(ncore):
        b = (k // 2) % cfg.B
        m = dict(sh)
        m.update(prep_core(cfg, inp, b, zero=(k % 2 == 1 or k // 2 >= cfg.B)))
        in_maps.append(m)
    res = run_bass_kernel_spmd(nc, in_maps, core_ids=list(range(ncore)))
    out = np.stack([np.ascontiguousarray(res.results[2 * b]["outT"].T) for b in range(cfg.B)], 0)
    return out.astype(np.float32)


def kernel(**inputs):
    return run(Cfg(), inputs)
```

```python
import contextlib
import numpy as np
import concourse.bass as bass
import concourse.mybir as mybir
from concourse.bass_utils import run_bass_kernel_spmd

F32, BF16 = mybir.dt.float32, mybir.dt.bfloat16
AF = mybir.ActivationFunctionType
ALU = mybir.AluOpType
ND = 6
EPS = 1e-6


class Cfg:
    def __init__(s, D=2048, SEQ=4096, CTX=256, DEPTH=4, DFF=5632, TBF=2048, B=4, TBE=1024):
        s.D, s.SEQ, s.CTX, s.DEPTH, s.DFF, s.TBF, s.B = D, SEQ, CTX, DEPTH, DFF, TBF, B
        s.KC = D // 128
        s.TBE = TBE
        s.NH = D // 128
        s.NAH = s.NH // 2
        s.GH = s.NH - s.NAH
        s.AW, s.BQK, s.BV = s.NAH * 128, s.GH * 64, s.GH * 128
        s.DIN = 3 * s.AW + 2 * s.BQK + 2 * s.BV + 32
        s.ROWS = SEQ // 64
        s.PG = D // 4
        s.FC = DFF // 128
        s.NE = (DEPTH + 1) // 2
        s.NO = DEPTH // 2
        s.NBLK = 23


def _split(a, b, mx=256):
    n = b - a
    k = (n + mx - 1) // mx
    out, t = [], a
    for i in range(k):
        m = n // k + (1 if i < n % k else 0)
        out.append((t, m))
        t += m
    return out


class Tk:
    __slots__ = ("w", "r")

    def __init__(s):
        s.w = None
        s.r = {}


class Buf:
    def __init__(s, h):
        s.h = h
        s.tks = {}

    def t(s, key=0):
        if key not in s.tks:
            s.tks[key] = Tk()
        return s.tks[key]

    def seg(s, kc, ca, cb, g=64):
        return [s.t((kc, q)) for q in range(ca // g, (cb - 1) // g + 1)]

    def __getitem__(s, k):
        return s.h[k]


class Prog:
    def __init__(s, nc):
        s.nc = nc
        s.eng = {"pe": nc.tensor, "act": nc.scalar, "dve": nc.vector, "pool": nc.gpsimd, "sp": nc.sync}
        s.sem = {k: nc.alloc_semaphore("s_" + k) for k in s.eng}
        s.cnt = {k: 0 for k in s.eng}
        s.pend = False
        s.waited = {k: {} for k in s.eng}
        s.dsem = {q: [nc.alloc_semaphore("d_%s%d" % (q, i)) for i in range(ND)] for q in ("sp", "pool", "act")}
        s.dcnt = {q: [0] * ND for q in s.dsem}
        s.dnext = {q: 0 for q in s.dsem}
        s.nps = 0
        s.nrot = 7
        s.ps = [Buf(nc.alloc_psum_tensor("ps%d" % i, [128, 512], F32)) for i in range(7)]
        s.psb = Buf(nc.alloc_psum_tensor("psb", [128, 1024], BF16))
        s.stack = None
        s.nalloc = 0

    @contextlib.contextmanager
    def scope(s):
        old = s.stack
        with contextlib.ExitStack() as st:
            s.stack = st
            yield
            s.barrier()
        s.stack = old

    def sb(s, name, shape, dt):
        s.nalloc += 1
        nm = "%s_%d" % (name, s.nalloc)
        if s.stack is None:
            return Buf(s.nc.alloc_sbuf_tensor(nm, list(shape), dt))
        return Buf(s.stack.enter_context(s.nc.sbuf_tensor(nm, list(shape), dt)))

    def psum(s):
        b = s.ps[s.nps % s.nrot]
        s.nps += 1
        return b

    def _wait(s, e, tok):
        if tok is None:
            return
        key, sem, val = tok
        if e == "pe" and key == "pe":
            return
        if s.waited[e].get(key, -1) >= val:
            return
        s.eng[e].wait_ge(sem, val)
        s.waited[e][key] = val

    def _deps(s, e, reads, writes):
        for t in reads:
            s._wait(e, t.w)
        for t in writes:
            s._wait(e, t.w)
            for r in t.r.values():
                s._wait(e, r)

    def _mark(s, tok, reads, writes):
        for t in reads:
            t.r[tok[0]] = tok
        for t in writes:
            t.w = tok
            t.r = {}

    def op(s, e, fn, reads=(), writes=(), sig=True):
        s._deps(e, reads, writes)
        ins = fn(s.eng[e])
        if e == "pe" and not sig:
            tok = ("pe", s.sem["pe"], s.cnt["pe"] + 1)
            s.pend = True
        else:
            s.cnt[e] += 1
            ins.then_inc(s.sem[e], 1)
            tok = (e, s.sem[e], s.cnt[e])
            if e == "pe":
                s.pend = False
        s._mark(tok, reads, writes)
        return tok

    def dma(s, q, out, in_, reads=(), writes=(), **kw):
        i = s.dnext[q]
        s.dnext[q] = (i + 1) % ND
        sem = s.dsem[q][i]
        key = "d_%s%d" % (q, i)
        if s.dcnt[q][i] > 0:
            s._wait(q, (key, sem, s.dcnt[q][i]))
        s._deps(q, reads, writes)
        ins = s.eng[q].dma_start(out=out, in_=in_, **kw)
        s.dcnt[q][i] += 16
        ins.then_inc(sem, 16)
        tok = (key, sem, s.dcnt[q][i])
        s._mark(tok, reads, writes)
        return tok

    def barrier(s):
        assert not s.pend
        toks = [(k, s.sem[k], s.cnt[k]) for k in s.eng if s.cnt[k] > 0]
        for q in s.dsem:
            for i in range(ND):
                if s.dcnt[q][i] > 0:
                    toks.append(("d_%s%d" % (q, i), s.dsem[q][i], s.dcnt[q][i]))
        for e in s.eng:
            for tok in toks:
                s._wait(e, tok)


def build(cfg, flags=None):
    fl = dict(even=True, odd=True, ffn=True)
    if flags:
        fl.update(flags)
    c = cfg
    D, SEQ, CTX, DEPTH, DFF, KC, FC = c.D, c.SEQ, c.CTX, c.DEPTH, c.DFF, c.KC, c.FC
    nc = bass.Bass("TRN2", target_bir_lowering=False)
    P = Prog(nc)

    def din(name, shape):
        return nc.dram_tensor(name, list(shape), F32, kind="ExternalInput").ap()

    def dscr(name, shape, dt=F32):
        return nc.dram_tensor(name, list(shape), dt).ap()

    xT = din("xT", [D, SEQ])
    cT = din("cT", [D, CTX])
    cv = din("cv", [128, KC * 2])
    w_mod = din("w_mod", [DEPTH, D, 6 * D])
    b_modT = din("b_modT", [128, DEPTH * 6 * KC])
    ngT = din("ngT", [128, DEPTH * 2 * KC])
    fgT = din("fgT", [128, KC])
    w_up = din("w_up", [DEPTH, D, 2 * DFF])
    w_down = din("w_down", [DEPTH, DFF, D])
    convT = din("convT", [128, DEPTH * FC * 4])
    pool_w = din("pool_w", [c.NO, 4, c.PG, c.PG])
    pool_scT = din("pool_scT", [128, c.NO * KC])
    icnt_l = din("icnt_l", [4, SEQ])
    icnt_c = din("icnt_c", [4, CTX])
    w_in = din("w_in", [c.NE, D, c.DIN])
    w_out = din("w_out", [c.NE, D, D])
    wg2 = din("wg2", [c.NE, 32, 2 * c.BQK])
    bg = din("bg", [c.NE, 1, 2 * c.BQK])
    rpbx = din("rpbx", [c.NE, c.NAH, 128, c.NBLK * 64])
    nmask = din("nmask", [2, 128, c.NBLK * 64])
    rope = din("rope", [2, 128, SEQ])
    gla_gT = din("gla_gT", [128, c.NE * c.GH])
    cmat = din("cmat", [8, 128, 128])
    outT = nc.dram_tensor("outT", [D, SEQ], F32, kind="ExternalOutput").ap()
    X = [dscr("X0", [D, SEQ]), dscr("X1", [D, SEQ])]
    C = [dscr("C0", [D, CTX]), dscr("C1", [D, CTX])]

    onesD = P.sb("onesD", [128, 128], BF16)
    silc = P.sb("silc", [128, KC * 2], BF16)
    cvs = P.sb("cvs", [128, KC * 2], F32)
    modv = P.sb("modv", [128, DEPTH * 6 * KC * 2], F32)
    bmod = P.sb("bmod", [128, DEPTH * 6 * KC], F32)
    ng = P.sb("ng", [128, DEPTH * 2 * KC], F32)
    fg = P.sb("fg", [128, KC], F32)
    conv = P.sb("conv", [128, DEPTH * FC * 4], F32)
    P.dma("pool", onesD[:], cmat[0], writes=[onesD.t()])
    P.dma("sp", cvs[:], cv, writes=[cvs.t()])
    P.dma("sp", bmod[:], b_modT, writes=[bmod.t()])
    P.dma("sp", ng[:], ngT, writes=[ng.t()])
    P.dma("sp", fg[:], fgT, writes=[fg.t()])
    P.dma("sp", conv[:], convT, writes=[conv.t()])
    P.op("act", lambda e: e.activation(out=silc[:], in_=cvs[:], func=AF.Silu), reads=[cvs.t()], writes=[silc.t()])

    def mvcol(i, v, kc, g):
        return ((i * 6 + v) * KC + kc) * 2 + g

    def mv(i, v, kc, g):
        cidx = mvcol(i, v, kc, g)
        return modv[:, cidx:cidx + 1]

    def modvec_gen(i, gw, nbuf):
        wt = [P.sb("mw", [128, KC, gw], BF16) for _ in range(nbuf)]
        mvt = [P.sb("mvt", [2, gw], F32) for _ in range(2)]
        i2 = P.sb("i2", [2, 2], F32)
        P.dma("sp", i2[:], cmat[3][0:2, 0:2], writes=[i2.t()])
        noc = 6 * KC
        base = i * noc * 2
        ngrp = 6 * D // gw
        no = gw // 128
        for g in range(ngrp):
            w = wt[g % nbuf]
            m_ = mvt[g % 2]
            P.dma("pool", w[:], w_mod[i, :, g * gw:(g + 1) * gw].rearrange("(kc p) n -> p kc n", p=128), writes=[w.t()])
            pg = P.psum()
            for kc in range(KC):
                P.op("pe", lambda e: e.matmul(pg[0:2, 0:gw], silc[:, kc * 2:kc * 2 + 2], w[:, kc, :],
                                              start=(kc == 0), stop=(kc == KC - 1)),
                     reads=[w.t(), silc.t()], writes=[pg.t()], sig=(kc == KC - 1))
            P.op("act", lambda e: e.copy(m_[:, 0:gw], pg[0:2, 0:gw]), reads=[pg.t()], writes=[m_.t()])
            pt = P.psum()
            for j in range(no):
                P.op("pe", lambda e: e.matmul(pt[:, j * 2:j * 2 + 2], m_[:, j * 128:(j + 1) * 128], i2[:], start=True, stop=True),
                     reads=[m_.t(), i2.t()], writes=[pt.t()], sig=(j == no - 1))
            oc0 = g * no
            P.op("dve", lambda e: e.tensor_tensor(modv[:, base + oc0 * 2:base + (oc0 + no) * 2].rearrange("p (o t) -> p o t", t=2),
                                                  pt[:, 0:no * 2].rearrange("p (o t) -> p o t", t=2),
                                                  bmod[:, i * noc + oc0:i * noc + oc0 + no].unsqueeze(2).to_broadcast([128, no, 2]),
                                                  ALU.add),
                 reads=[pt.t(), bmod.t()], writes=[modv.t()])
            yield
        for (v, which) in ((1, 0), (4, 1)):
            for g in range(2):
                b0 = mvcol(i, v, 0, g)
                sl = modv[:, b0:b0 + 2 * KC:2]
                P.op("dve", lambda e: e.scalar_tensor_tensor(sl, sl, 1.0, ng[:, (i * 2 + which) * KC:(i * 2 + which + 1) * KC],
                                                             ALU.add, ALU.mult),
                     reads=[modv.t(), ng.t()], writes=[modv.t()])

    def st_modvec(i):
        with P.scope():
            for _ in modvec_gen(i, 512, 2):
                pass

    class Stepper:
        def __init__(s_, gen, nsteps, total):
            s_.gen, s_.per = gen, (total + nsteps - 1) // nsteps

        def step(s_):
            if s_.gen is None:
                return
            for _ in range(s_.per):
                try:
                    next(s_.gen)
                except StopIteration:
                    s_.gen = None
                    return

        def finish(s_):
            while s_.gen is not None:
                s_.step()

    def rms_rstd(xs, n, rs, sqs):
        ps = P.psum()
        P.op("dve", lambda e: e.tensor_tensor(sqs[:, :, 0:n], xs[:, :, 0:n], xs[:, :, 0:n], ALU.mult),
             reads=[xs.t()], writes=[sqs.t()])
        for kc in range(KC):
            P.op("pe", lambda e: e.matmul(ps[:, 0:n], onesD[:], sqs[:, kc, 0:n], start=(kc == 0), stop=(kc == KC - 1)),
                 reads=[sqs.t(), onesD.t()], writes=[ps.t()], sig=(kc == KC - 1))
        P.op("act", lambda e: e.activation(out=rs[:, 0:n], in_=ps[:, 0:n], func=AF.Ln, bias=EPS),
             reads=[ps.t()], writes=[rs.t()])
        P.op("act", lambda e: e.activation(out=rs[:, 0:n], in_=rs[:, 0:n], func=AF.Exp, scale=-0.5),
             reads=[rs.t()], writes=[rs.t()])

    def norm_mod(src, t0, n, dstbuf, c0, wk, i, va, vs, g, xs, rs, sqs, tmps):
        P.dma("sp", xs[:, :, 0:n], src[:, t0:t0 + n].rearrange("(kc p) n -> p kc n", p=128), writes=[xs.t()])
        rms_rstd(xs, n, rs, sqs)
        for kc in range(KC):
            tmp = tmps[kc % 2]
            P.op("dve", lambda e: e.tensor_tensor(tmp[:, 0:n], xs[:, kc, 0:n], rs[:, 0:n], ALU.mult),
                 reads=[xs.t(), rs.t()], writes=[tmp.t()])
            P.op("act", lambda e: e.activation(out=dstbuf[:, kc, c0:c0 + n], in_=tmp[:, 0:n], func=AF.Identity,
                                               bias=mv(i, vs, kc, g), scale=mv(i, va, kc, g)),
                 reads=[tmp.t(), modv.t()], writes=wk(kc))

    def norm_gen(src, subs, dstbuf, wkf, i, va, vs, g, nb):
        def stage1(k):
            t0, n, c0 = subs[k]
            xs, sq, rs = nb["xs"][k % 2], nb["sq"][k % 2], nb["rs"][k % 2]
            P.dma("sp", xs[:, :, 0:n], src[:, t0:t0 + n].rearrange("(kc p) n -> p kc n", p=128), writes=[xs.t()])
            rms_rstd(xs, n, rs, sq)

        def stage2(k):
            t0, n, c0 = subs[k]
            xs, rs = nb["xs"][k % 2], nb["rs"][k % 2]
            ba, bs = mvcol(i, va, 0, g), mvcol(i, vs, 0, g)
            rb = rs[:, 0:n].unsqueeze(1).to_broadcast([128, KC, n])
            ab = modv[:, ba:ba + 2 * KC:2].unsqueeze(2).to_broadcast([128, KC, n])
            sb_ = modv[:, bs:bs + 2 * KC:2].unsqueeze(2).to_broadcast([128, KC, n])
            P.op("dve", lambda e: e.tensor_tensor(xs[:, :, 0:n], xs[:, :, 0:n], rb, ALU.mult),
                 reads=[xs.t(), rs.t()], writes=[xs.t()])
            P.op("dve", lambda e: e.tensor_tensor(xs[:, :, 0:n], xs[:, :, 0:n], ab, ALU.mult),
                 reads=[xs.t(), modv.t()], writes=[xs.t()])
            P.op("dve", lambda e: e.tensor_tensor(dstbuf[:, :, c0:c0 + n], xs[:, :, 0:n], sb_, ALU.add),
                 reads=[xs.t(), modv.t()], writes=[t_ for kc in range(KC) for t_ in wkf(kc, c0, n)])
        if not subs:
            return
        stage1(0)
        for k in range(len(subs)):
            if k + 1 < len(subs):
                stage1(k + 1)
            stage2(k)
            yield

    def norm_run(*a_, **k_):
        for _ in norm_gen(*a_, **k_):
            pass

    def norm_bufs(ns):
        return dict(xs=[P.sb("xs", [128, KC, ns], F32) for _ in range(2)],
                    sq=[P.sb("sq", [128, KC, ns], BF16) for _ in range(2)],
                    rs=[P.sb("rs", [128, ns], F32) for _ in range(2)],
                    tmp=[])

    ACTS = dscr("ACTS", [FC, 128, SEQ], BF16)
    NSUB = 128

    def st_ffn(i, src, dst, ntok, g):
        PT = min(c.TBF, ntok)
        with P.scope():
            h2 = P.sb("h2", [128, KC, PT + 2], BF16)
            gsb = P.sb("gsb", [128, PT + 2], F32)
            cvt = P.sb("cvt", [128, PT], F32)
            val = [P.sb("val", [128, PT], BF16) for _ in range(2)]
            aj = [P.sb("aj", [128, PT], BF16) for _ in range(2)]
            nb = norm_bufs(NSUB)
            wv = [P.sb("wv", [128, KC, 512], BF16) for _ in range(2)]
            wg = [P.sb("wg", [128, KC, 512], BF16) for _ in range(2)]
            for b0 in range(0, ntok, PT):
                tb = min(PT, ntok - b0)
                lo, hi = b0 - 1, b0 + tb + 1
                if lo < 0:
                    for kc in range(KC):
                        P.op("pool", lambda e: e.memset(h2[:, kc, 0:1], 0.0), writes=h2.seg(kc, 0, 1))
                if hi > ntok:
                    for kc in range(KC):
                        P.op("pool", lambda e: e.memset(h2[:, kc, tb + 1:tb + 2], 0.0), writes=h2.seg(kc, tb + 1, tb + 2))
                a, bnd = max(lo, 0), min(hi, ntok)
                norm_run(src, [(t0, n, t0 - lo) for (t0, n) in _split(a, bnd, NSUB)], h2,
                         (lambda kc, c0, n: h2.seg(kc, c0, c0 + n)), i, 4, 3, g, nb)
                nsub = (tb + 511) // 512
                for jg in range((FC + 3) // 4):
                    nj = min(4, FC - jg * 4)
                    wvj, wgj = wv[jg % 2], wg[jg % 2]
                    P.dma("pool", wvj[:, :, 0:nj * 128], w_up[i, :, jg * 512:jg * 512 + nj * 128].rearrange("(kc p) n -> p kc n", p=128),
                          writes=[wvj.t()])
                    P.dma("pool", wgj[:, :, 0:nj * 128], w_up[i, :, DFF + jg * 512:DFF + jg * 512 + nj * 128].rearrange("(kc p) n -> p kc n", p=128),
                          writes=[wgj.t()])
                    for jj in range(nj):
                        j = jg * 4 + jj
                        ws = slice(jj * 128, (jj + 1) * 128)
                        vj, ajj = val[j % 2], aj[j % 2]
                        for s in range(nsub):
                            n = min(512, tb - s * 512)
                            c0 = 1 + s * 512
                            psv = P.psum()
                            for kc in range(KC):
                                P.op("pe", lambda e: e.matmul(psv[:, 0:n], wvj[:, kc, ws], h2[:, kc, c0:c0 + n],
                                                              start=(kc == 0), stop=(kc == KC - 1)),
                                     reads=[wvj.t()] + h2.seg(kc, c0, c0 + n), writes=[psv.t()], sig=(kc == KC - 1))
                            P.op("act", lambda e: e.copy(vj[:, s * 512:s * 512 + n], psv[:, 0:n]), reads=[psv.t()],
                                 writes=[vj.t(s)])
                            psg = P.psum()
                            for kc in range(KC):
                                P.op("pe", lambda e: e.matmul(psg[:, 0:n], wgj[:, kc, ws], h2[:, kc, c0:c0 + n],
                                                              start=(kc == 0), stop=(kc == KC - 1)),
                                     reads=[wgj.t()] + h2.seg(kc, c0, c0 + n), writes=[psg.t()], sig=(kc == KC - 1))
                            P.op("act", lambda e: e.copy(gsb[:, c0:c0 + n], psg[:, 0:n]), reads=[psg.t()],
                                 writes=[gsb.t(s)])
                        psh = P.psum()
                        for kc in range(KC):
                            P.op("pe", lambda e: e.matmul(psh[:, 0:2], wgj[:, kc, ws], h2[:, kc, 0:tb + 2:tb + 1],
                                                          start=(kc == 0), stop=(kc == KC - 1)),
                                 reads=[wgj.t()] + h2.seg(kc, 0, 1) + h2.seg(kc, tb + 1, tb + 2), writes=[psh.t()],
                                 sig=(kc == KC - 1))
                        P.op("act", lambda e: e.copy(gsb[:, 0:tb + 2:tb + 1], psh[:, 0:2]), reads=[psh.t()],
                             writes=[gsb.t("h")])
                        cb = (i * FC + j) * 4
                        allg = [gsb.t(s) for s in range(nsub)] + [gsb.t("h")]
                        P.op("dve", lambda e: e.tensor_scalar(cvt[:, 0:tb], gsb[:, 0:tb], conv[:, cb:cb + 1], None, ALU.mult),
                             reads=allg + [conv.t()], writes=[cvt.t()])
                        P.op("dve", lambda e: e.scalar_tensor_tensor(cvt[:, 0:tb], gsb[:, 1:tb + 1], conv[:, cb + 1:cb + 2],
                                                                     cvt[:, 0:tb], ALU.mult, ALU.add),
                             reads=allg + [cvt.t()], writes=[cvt.t()])
                        P.op("dve", lambda e: e.scalar_tensor_tensor(cvt[:, 0:tb], gsb[:, 2:tb + 2], conv[:, cb + 2:cb + 3],
                                                                     cvt[:, 0:tb], ALU.mult, ALU.add),
                             reads=allg + [cvt.t()], writes=[cvt.t()])
                        P.op("act", lambda e: e.activation(out=cvt[:, 0:tb], in_=cvt[:, 0:tb], func=AF.Gelu,
                                                           bias=conv[:, cb + 3:cb + 4]),
                             reads=[cvt.t(), conv.t()], writes=[cvt.t()])
                        P.op("dve", lambda e: e.tensor_tensor(ajj[:, 0:tb], cvt[:, 0:tb], vj[:, 0:tb], ALU.mult),
                             reads=[cvt.t()] + [vj.t(s) for s in range(nsub)], writes=[ajj.t()])
                        P.dma("sp", ACTS[j, :, b0:b0 + tb], ajj[:, 0:tb], reads=[ajj.t()])
        TB = min(1024, ntok)
        with P.scope():
            act = P.sb("actT", [128, FC, TB], BF16)
            wd = [P.sb("wd", [128, FC, 512], BF16) for _ in range(2)]
            xo = [P.sb("xo", [128, 512], F32) for _ in range(2)]
            xn = [P.sb("xn", [128, 512], F32) for _ in range(2)]
            kk = 0
            nwd = 0
            for b0 in range(0, ntok, TB):
                tb = min(TB, ntok - b0)
                nsub = (tb + 511) // 512
                for jq in range(0, FC, 11):
                    je = min(jq + 11, FC)
                    P.dma("sp", act[:, jq:je, 0:tb], ACTS[jq:je, :, b0:b0 + tb].rearrange("j p n -> p j n"),
                          writes=[act.t(jq)])
                for ocg in range(D // 512):
                    wdo = wd[nwd % 2]
                    nwd += 1
                    P.dma("pool", wdo[:], w_down[i, :, ocg * 512:(ocg + 1) * 512].rearrange("(fc p) n -> p fc n", p=128),
                          writes=[wdo.t()])
                    for o4 in range(4):
                        oc = ocg * 4 + o4
                        for s in range(nsub):
                            n = min(512, tb - s * 512)
                            k = kk % 2
                            kk += 1
                            P.dma("sp", xo[k][:, 0:n], src[oc * 128:(oc + 1) * 128, b0 + s * 512:b0 + s * 512 + n],
                                  writes=[xo[k].t()])
                            ps = P.psum()
                            for j in range(FC):
                                P.op("pe", lambda e: e.matmul(ps[:, 0:n], wdo[:, j, o4 * 128:(o4 + 1) * 128], act[:, j, s * 512:s * 512 + n],
                                                              start=(j == 0), stop=(j == FC - 1)),
                                     reads=[wdo.t(), act.t((j // 11) * 11)], writes=[ps.t()], sig=(j == FC - 1))
                            P.op("dve", lambda e: e.scalar_tensor_tensor(xn[k][:, 0:n], ps[:, 0:n], mv(i, 5, oc, g),
                                                                         xo[k][:, 0:n], ALU.mult, ALU.add),
                                 reads=[ps.t(), xo[k].t(), modv.t()], writes=[xn[k].t()])
                            P.dma("sp", dst[oc * 128:(oc + 1) * 128, b0 + s * 512:b0 + s * 512 + n], xn[k][:, 0:n],
                                  reads=[xn[k].t()])

    def st_copy(src, dst, ntok):
        with P.scope():
            xs = [P.sb("cp", [128, KC, 512], F32) for _ in range(2)]
            k = 0
            for t0 in range(0, ntok, 512):
                n = min(512, ntok - t0)
                P.dma("sp", xs[k][:, :, 0:n], src[:, t0:t0 + n].rearrange("(kc p) n -> p kc n", p=128), writes=[xs[k].t()])
                P.dma("sp", dst[:, t0:t0 + n].rearrange("(kc p) n -> p kc n", p=128), xs[k][:, :, 0:n], reads=[xs[k].t()])
                k ^= 1

    def st_final(src):
        with P.scope():
            xs = P.sb("xs", [128, KC, 512], F32)
            rs = P.sb("rs", [128, 512], F32)
            sqs = P.sb("sq", [128, KC, 512], BF16)
            ot = [P.sb("ot", [128, KC, 512], F32) for _ in range(2)]
            k = 0
            for t0 in range(0, SEQ, 512):
                n = min(512, SEQ - t0)
                P.dma("sp", xs[:, :, 0:n], src[:, t0:t0 + n].rearrange("(kc p) n -> p kc n", p=128), writes=[xs.t()])
                rms_rstd(xs, n, rs, sqs)
                o = ot[k]
                for kc in range(KC):
                    P.op("dve", lambda e: e.scalar_tensor_tensor(o[:, kc, 0:n], xs[:, kc, 0:n], fg[:, kc:kc + 1],
                                                                 rs[:, 0:n], ALU.mult, ALU.mult),
                         reads=[xs.t(), rs.t(), fg.t()], writes=[o.t(kc)])
                P.dma("sp", outT[:, t0:t0 + n].rearrange("(kc p) n -> p kc n", p=128), o[:, :, 0:n],
                      reads=[o.t(kc) for kc in range(KC)])
                k ^= 1

    def st_pool(i, src, dst, ntok, g, nxt=None):
        o = i // 2
        K4 = KC // 4
        TBP = 256
        icn = icnt_l if g == 0 else icnt_c
        with P.scope():
            hps = [P.sb("hp", [128, KC, TBP + 16], F32) for _ in range(2)]
            ics = [P.sb("ic", [128, 4, TBP], F32) for _ in range(2)]
            pl = P.sb("pl", [128, KC, TBP], BF16)
            WA = P.sb("WA", [128, KC, TBP + 16], F32)
            WB = P.sb("WB", [128, KC, TBP + 16], F32)
            nb = norm_bufs(128)
            wp = P.sb("wp", [128, 4, K4, c.PG], BF16)
            gp = P.sb("gp", [128, KC], F32)
            psc = P.sb("psc", [128, KC], F32)
            xo = [P.sb("xo", [128, TBP], F32) for _ in range(2)]
            xn = [P.sb("xn", [128, TBP], F32) for _ in range(2)]
            for gi in range(4):
                P.dma("pool", wp[:, gi, :, :], pool_w[o, gi].rearrange("(k p) n -> p k n", p=128), writes=[wp.t(gi)])
            P.dma("sp", psc[:], pool_scT[:, o * KC:(o + 1) * KC], writes=[psc.t()])
            b0c = mvcol(i, 2, 0, g)
            P.op("dve", lambda e: e.tensor_tensor(gp[:], modv[:, b0c:b0c + 2 * KC:2], psc[:], ALU.mult),
                 reads=[modv.t(), psc.t()], writes=[gp.t()])
            blks = list(range(0, ntok, TBP))
            kkc = [0]

            def norm_for(bi):
                b0 = blks[bi]
                hp, ic = hps[bi % 2], ics[bi % 2]
                tb = min(TBP, ntok - b0)
                lo, hi = b0 - 8, b0 + tb + 8
                a_, bnd = max(lo, 0), min(hi, ntok)
                if lo < 0:
                    P.op("pool", lambda e: e.memset(hp[:, :, 0:8], 0.0), writes=[t_ for kc in range(KC) for t_ in hp.seg(kc, 0, 8)])
                if hi > ntok:
                    P.op("pool", lambda e: e.memset(hp[:, :, tb + 8:tb + 16], 0.0),
                         writes=[t_ for kc in range(KC) for t_ in hp.seg(kc, tb + 8, tb + 16)])
                subs = [(t0, n, t0 - lo) for (t0, n) in _split(a_, bnd, 128)]
                norm_run(src, subs, hp, (lambda kc, c0, n: hp.seg(kc, c0, c0 + n)), i, 1, 0, g, nb)
                for w in range(4):
                    P.dma("sp", ic[:, w, 0:tb], icn[w:w + 1, b0:b0 + tb].partition_broadcast(128), writes=[ic.t(w)])

            def proc(bi):
                b0 = blks[bi]
                hp, ic = hps[bi % 2], ics[bi % 2]
                tb = min(TBP, ntok - b0)
                L = tb + 16
                hall = [t_ for kc in range(KC) for t_ in hp.seg(kc, 0, L)]
                P.op("dve", lambda e: e.tensor_tensor(WA[:, :, 1:L], hp[:, :, 0:L - 1], hp[:, :, 1:L], ALU.add),
                     reads=hall, writes=[WA.t()])
                P.op("dve", lambda e: e.tensor_tensor(WB[:, K4:KC, 2:L - 1], WA[:, K4:KC, 1:L - 2], WA[:, K4:KC, 3:L], ALU.add),
                     reads=[WA.t()], writes=[WB.t()])
                P.op("dve", lambda e: e.tensor_tensor(WA[:, 2 * K4:KC, 4:L - 3], WB[:, 2 * K4:KC, 2:L - 5], WB[:, 2 * K4:KC, 6:L - 1], ALU.add),
                     reads=[WB.t()], writes=[WA.t()])
                P.op("dve", lambda e: e.tensor_tensor(WB[:, 3 * K4:KC, 8:L - 7], WA[:, 3 * K4:KC, 4:L - 11], WA[:, 3 * K4:KC, 12:L - 3], ALU.add),
                     reads=[WA.t()], writes=[WB.t()])
                for gi in range(4):
                    cur = WA if gi % 2 == 0 else WB
                    en = "dve" if gi < 2 else "pool"
                    ka, kb = gi * K4, (gi + 1) * K4
                    icb = ic[:, gi, 0:tb].unsqueeze(1).to_broadcast([128, K4, tb])
                    P.op(en, lambda e: e.tensor_tensor(cur[:, ka:kb, 8:8 + tb], cur[:, ka:kb, 8:8 + tb], icb, ALU.mult),
                         reads=[WA.t(), WB.t(), ic.t(gi)], writes=[cur.t(("f", gi))])
                    P.op(en, lambda e: e.tensor_tensor(pl[:, ka:kb, 0:tb], cur[:, ka:kb, 8:8 + tb], hp[:, ka:kb, 8:8 + tb], ALU.subtract),
                         reads=[cur.t(("f", gi))] + hall, writes=[pl.t(gi)])
                for gi in range(4):
                    for oc in range(K4):
                        ocg = gi * K4 + oc
                        k = kkc[0] % 2
                        kkc[0] += 1
                        P.dma("sp", xo[k][:, 0:tb], src[ocg * 128:(ocg + 1) * 128, b0:b0 + tb], writes=[xo[k].t()])
                        ps = P.psum()
                        for k4 in range(K4):
                            P.op("pe", lambda e: e.matmul(ps[:, 0:tb], wp[:, gi, k4, oc * 128:(oc + 1) * 128],
                                                          pl[:, gi * K4 + k4, 0:tb], start=(k4 == 0), stop=(k4 == K4 - 1)),
                                 reads=[wp.t(gi), pl.t(gi)], writes=[ps.t()], sig=(k4 == K4 - 1))
                        P.op("dve", lambda e: e.scalar_tensor_tensor(xn[k][:, 0:tb], ps[:, 0:tb], gp[:, ocg:ocg + 1],
                                                                     xo[k][:, 0:tb], ALU.mult, ALU.add),
                             reads=[ps.t(), xo[k].t(), gp.t()], writes=[xn[k].t()])
                        P.dma("act", dst[ocg * 128:(ocg + 1) * 128, b0:b0 + tb], xn[k][:, 0:tb], reads=[xn[k].t()])

            norm_for(0)
            for bi in range(len(blks)):
                if bi + 1 < len(blks):
                    norm_for(bi + 1)
                proc(bi)

    NT = CTX + SEQ
    AW, BQK, BV, NAH, GH, DIN = c.AW, c.BQK, c.BV, c.NAH, c.GH, c.DIN
    NCC = CTX // 128
    NCH = NT // 128
    ROWS = c.ROWS
    NB64 = c.NBLK * 64
    QKA = dscr("QKA", [2 * AW, NT], BF16)
    VA = dscr("VA", [NT, AW], BF16)
    QKB = dscr("QKB", [2 * BQK, NT], BF16)
    VBm = dscr("VBm", [NT, BV], BF16)
    GBs = dscr("GBs", [BV, NT], BF16)
    ABT = dscr("ABT", [32, NT], F32)
    OT = dscr("OT", [D, NT], BF16)
    segs = [("qa", 0, AW), ("ka", AW, 2 * AW), ("va", 2 * AW, 3 * AW), ("qb", 3 * AW, 3 * AW + BQK),
            ("kb", 3 * AW + BQK, 3 * AW + 2 * BQK), ("vb", 3 * AW + 2 * BQK, 3 * AW + 2 * BQK + BV),
            ("gb", 3 * AW + 2 * BQK + BV, 3 * AW + 2 * BQK + 2 * BV), ("ab", DIN - 32, DIN)]

    def ctype(col):
        for (nm, a, b) in segs:
            if a <= col < b:
                return nm, col - a
        raise ValueError

    def e_proj(i, e_, src, tok0, ntok, g):
        TB = min(c.TBE, ntok)
        with P.scope():
            hTs = [P.sb("hT", [128, KC, TB], BF16) for _ in range(2)]
            nb = norm_bufs(256)
            wt = [P.sb("wi", [128, KC, 512], BF16) for _ in range(2)]
            st = [P.sb("st", [128, 512], BF16) for _ in range(3)]
            stf = [P.sb("stf", [32, 512], F32) for _ in range(2)]
            nst = 0
            blocks = list(range(0, ntok, TB))

            def mk_norm(bi):
                b0_ = blocks[bi]
                tb_ = min(TB, ntok - b0_)
                hb = hTs[bi % 2]
                subs_ = [(t0, min(256, tb_ - t0)) for t0 in range(0, tb_, 256)]
                return norm_gen(src, [(b0_ + t0, n, t0) for (t0, n) in subs_], hb, (lambda kc, c0, n: [hb.t((c0, kc))]), i, 1, 0, g, nb)

            for _ in mk_norm(0):
                pass
            for bi, b0 in enumerate(blocks):
                tb = min(TB, ntok - b0)
                hT = hTs[bi % 2]
                subs = [(t0, min(256, tb - t0)) for t0 in range(0, tb, 256)]
                nxt_norm = mk_norm(bi + 1) if bi + 1 < len(blocks) else None

                def hk(kc, ca, cb):
                    return [hT.t((t0, kc)) for (t0, n) in subs if t0 < cb and t0 + n > ca]

                ng_ = (DIN + 511) // 512
                for cg in range(ng_):
                    if nxt_norm is not None and cg % 3 == 2:
                        try:
                            next(nxt_norm)
                        except StopIteration:
                            nxt_norm = None
                    w = wt[cg % 2]
                    ncol = min(512, DIN - cg * 512)
                    P.dma("pool", w[:, :, 0:ncol], w_in[e_, :, cg * 512:cg * 512 + ncol].rearrange("(kc p) n -> p kc n", p=128),
                          writes=[w.t()])
                    j = 0
                    while j * 128 < ncol:
                        col = cg * 512 + j * 128
                        nm, off = ctype(col)
                        if nm in ("va", "vb"):
                            j2 = j
                            while (j2 + 1) * 128 < ncol and ctype(cg * 512 + (j2 + 1) * 128)[0] == nm:
                                j2 += 1
                            nr = (j2 - j + 1) * 128
                            dstT = VA if nm == "va" else VBm
                            for tc in range(0, tb, 128):
                                ps = P.psum()
                                for kc in range(KC):
                                    P.op("pe", lambda e: e.matmul(ps[:, 0:nr], hT[:, kc, tc:tc + 128], w[:, kc, j * 128:j * 128 + nr],
                                                                  start=(kc == 0), stop=(kc == KC - 1)),
                                         reads=[w.t()] + hk(kc, tc, tc + 128), writes=[ps.t()], sig=(kc == KC - 1))
                                s_ = st[nst % 3]
                                nst += 1
                                P.op("act", lambda e: e.copy(s_[:, 0:nr], ps[:, 0:nr]), reads=[ps.t()], writes=[s_.t()])
                                P.dma("sp", dstT[tok0 + b0 + tc:tok0 + b0 + tc + 128, off:off + nr], s_[:, 0:nr], reads=[s_.t()])
                            j = j2 + 1
                            continue
                        m = 32 if nm == "ab" else 128
                        for s0 in range(0, tb, 512):
                            n = min(512, tb - s0)
                            ps = P.psum()
                            for kc in range(KC):
                                P.op("pe", lambda e: e.matmul(ps[0:m, 0:n], w[:, kc, j * 128:j * 128 + m], hT[:, kc, s0:s0 + n],
                                                              start=(kc == 0), stop=(kc == KC - 1)),
                                     reads=[w.t()] + hk(kc, s0, s0 + n), writes=[ps.t()], sig=(kc == KC - 1))
                            tcol = tok0 + b0 + s0
                            if nm == "ab":
                                s_ = stf[nst % 2]
                                nst += 1
                                P.op("act", lambda e: e.copy(s_[:, 0:n], ps[0:32, 0:n]), reads=[ps.t()], writes=[s_.t()])
                                P.dma("sp", ABT[:, tcol:tcol + n], s_[:, 0:n], reads=[s_.t()])
                            else:
                                s_ = st[nst % 3]
                                nst += 1
                                if nm == "qb":
                                    P.op("act", lambda e: e.mul(s_[:, 0:n], ps[:, 0:n], 0.125), reads=[ps.t()], writes=[s_.t()])
                                elif nm == "gb":
                                    P.op("act", lambda e: e.activation(out=s_[:, 0:n], in_=ps[:, 0:n], func=AF.Silu),
                                         reads=[ps.t()], writes=[s_.t()])
                                else:
                                    P.op("act", lambda e: e.copy(s_[:, 0:n], ps[:, 0:n]), reads=[ps.t()], writes=[s_.t()])
                                if nm in ("qa", "ka"):
                                    r0 = off + (AW if nm == "ka" else 0)
                                    dd = QKA[r0:r0 + 128, tcol:tcol + n]
                                elif nm in ("qb", "kb"):
                                    r0 = off + (BQK if nm == "kb" else 0)
                                    dd = QKB[r0:r0 + 128, tcol:tcol + n]
                                else:
                                    dd = GBs[off:off + 128, tcol:tcol + n]
                                P.dma("sp", dd, s_[:, 0:n], reads=[s_.t()])
                        j += 1

                if nxt_norm is not None:
                    for _ in nxt_norm:
                        pass

    def e_na(e_, need_ctx, nxt=None):
        scale = 128 ** -0.5
        with P.scope():
            P.nrot = 3
            kT = [P.sb("kT", [128, NT], BF16) for _ in range(2)]
            qT = [P.sb("qT", [128, NT], BF16) for _ in range(2)]
            V = [P.sb("V", [128, NCH, 128], BF16) for _ in range(2)]
            TF = [P.sb("TF", [128, NB64], F32) for _ in range(2)]
            TZ = [P.sb("TZ", [128, NB64], F32) for _ in range(2)]
            mF = P.sb("mF", [128, NB64], F32)
            mZ = P.sb("mZ", [128, NB64], F32)
            ones_bf = P.sb("ones_bf", [128, 128], BF16)
            ex = [P.sb("ex", [128, 512], F32) for _ in range(2)]
            pT = [P.sb("pT", [128, 512], BF16) for _ in range(4)]
            rd = [P.sb("rd", [128, 512], F32) for _ in range(2)]
            oT = [P.sb("oT", [128, 512], BF16) for _ in range(2)]
            P.dma("sp", mF[:], nmask[0], writes=[mF.t()])
            P.dma("sp", mZ[:], nmask[1], writes=[mZ.t()])
            P.dma("pool", ones_bf[:], cmat[2], writes=[ones_bf.t()])
            cnt = 0
            ntile = 0
            mst = Stepper(modvec_gen(nxt, 512, 3), NAH, 6 * D // 512) if nxt is not None else None
            for h in range(NAH):
                if mst:
                    mst.step()
                k_, q_, v_, tf, tz = kT[h % 2], qT[h % 2], V[h % 2], TF[h % 2], TZ[h % 2]
                P.dma("sp", k_[:], QKA[AW + h * 128:AW + (h + 1) * 128, :], writes=[k_.t()])
                P.dma("sp", q_[:], QKA[h * 128:(h + 1) * 128, :], writes=[q_.t()])
                P.dma("sp", v_[:], VA[:, h * 128:(h + 1) * 128].rearrange("(ch p) d -> p ch d", p=128), writes=[v_.t()])
                P.dma("sp", tf[:], rpbx[e_, h], writes=[tf.t()])
                P.op("act", lambda e: e.activation(out=tf[:], in_=tf[:], func=AF.Exp), reads=[tf.t()], writes=[tf.t()])
                P.op("dve", lambda e: e.tensor_tensor(tz[:], tf[:], mZ[:], ALU.mult), reads=[tf.t(), mZ.t()], writes=[tz.t()])
                P.op("dve", lambda e: e.tensor_tensor(tf[:], tf[:], mF[:], ALU.mult), reads=[tf.t(), mF.t()], writes=[tf.t()])
                tiles = [("lat", qt) for qt in range(ROWS // 8)] + ([("ctx", 0)] if need_ctx else [])
                items = []
                for (kind, qt) in tiles:
                    if kind == "lat":
                        r0 = qt * 8
                        qc0, nq = CTX + qt * 512, 512
                        c_lo, c_hi = max(0, (r0 - 4) // 2), min(ROWS // 2 - 1, (r0 + 10) // 2)
                        chunks = [("c", cc) for cc in range(NCC)] + [("l", cc) for cc in range(c_lo, c_hi + 1)]
                    else:
                        r0 = 0
                        qc0, nq = 0, CTX
                        chunks = [("c", cc) for cc in range(NCC)]
                    acc = (P.ps[3 + 2 * (ntile % 2)], P.ps[4 + 2 * (ntile % 2)], ntile)
                    ntile += 1
                    for idx, (ck, cc) in enumerate(chunks):
                        items.append(dict(r0=r0, qc0=qc0, nq=nq, ck=ck, cc=cc, first=(idx == 0), last=(idx == len(chunks) - 1),
                                          acc=acc))

                def emit_s(it):
                    ck, cc, nq, qc0, r0 = it["ck"], it["cc"], it["nq"], it["qc0"], it["r0"]
                    kc0 = cc * 128 if ck == "c" else CTX + cc * 128
                    ps_s = P.psum()
                    P.op("pe", lambda e: e.matmul(ps_s[:, 0:nq], k_[:, kc0:kc0 + 128], q_[:, qc0:qc0 + nq], start=True, stop=True),
                         reads=[k_.t(), q_.t()], writes=[ps_s.t()])
                    p_ = pT[it["n"] % 4]
                    it["p"] = p_
                    if ck == "c":
                        P.op("act", lambda e: e.activation(out=p_[:, 0:nq], in_=ps_s[:, 0:nq], func=AF.Exp, scale=scale),
                             reads=[ps_s.t()], writes=[p_.t()])
                    else:
                        x_ = ex[it["n"] % 2]
                        P.op("act", lambda e: e.activation(out=x_[:, 0:nq], in_=ps_s[:, 0:nq], func=AF.Exp, scale=scale),
                             reads=[ps_s.t()], writes=[x_.t()])
                        bb = r0 - 2 * cc + 11
                        fr = None
                        if r0 == 0 and cc <= 3:
                            fr = (0, 256)
                        if r0 == ROWS - 8 and cc >= ROWS // 2 - 4:
                            fr = (256, 512)
                        rngs = [(0, 512, tz)] if fr is None else [(fr[0], fr[1], tf), (256 - fr[0], 512 - fr[0], tz)]
                        for (ca, cb, tab) in rngs:
                            P.op("dve", lambda e: e.tensor_tensor(p_[:, ca:cb], x_[:, ca:cb], tab[:, bb * 64 + ca:bb * 64 + cb], ALU.mult),
                                 reads=[x_.t(), tab.t()], writes=[p_.t()])

                def emit_pv(it):
                    ps_o, ps_d, nt = it["acc"]
                    nq, qc0, p_ = it["nq"], it["qc0"], it["p"]
                    chi = it["cc"] if it["ck"] == "c" else NCC + it["cc"]
                    P.op("pe", lambda e: e.matmul(ps_o[:, 0:nq], v_[:, chi, :], p_[:, 0:nq], start=it["first"], stop=it["last"]),
                         reads=[v_.t(), p_.t()], writes=[ps_o.t()], sig=True)
                    P.op("pe", lambda e: e.matmul(ps_d[:, 0:nq], ones_bf[:], p_[:, 0:nq], start=it["first"], stop=it["last"]),
                         reads=[ones_bf.t(), p_.t()], writes=[ps_d.t()], sig=True)
                    if it["last"]:
                        r_, o_ = rd[nt % 2], oT[nt % 2]
                        P.op("dve", lambda e: e.reciprocal(r_[:, 0:nq], ps_d[:, 0:nq]), reads=[ps_d.t()], writes=[r_.t()])
                        P.op("dve", lambda e: e.tensor_tensor(o_[:, 0:nq], ps_o[:, 0:nq], r_[:, 0:nq], ALU.mult),
                             reads=[ps_o.t(), r_.t()], writes=[o_.t()])
                        P.dma("sp", OT[h * 128:(h + 1) * 128, qc0:qc0 + nq], o_[:, 0:nq], reads=[o_.t()])

                LA = 2
                for n_, it in enumerate(items):
                    it["n"] = cnt + n_
                for n_ in range(len(items) + LA):
                    if n_ < len(items):
                        emit_s(items[n_])
                    if n_ >= LA:
                        emit_pv(items[n_ - LA])
                cnt += len(items)
            if mst:
                mst.finish()
            P.nrot = 7

    def e_gla(e_, need_ctx):
        NFC = BQK // 128
        with P.scope():
            Cs = [P.sb("ropeC", [128, 512], F32) for _ in range(2)]
            Ss = [P.sb("ropeS", [128, 512], F32) for _ in range(2)]
            permb = P.sb("permb", [128, 128], BF16)
            identb = P.sb("identb", [128, 128], BF16)
            ones128 = P.sb("ones128", [128, 128], F32)
            onesr = P.sb("onesr", [1, 128], F32)
            tri = [P.sb("tri", [128, 128], F32) for _ in range(2)]
            wg2s = P.sb("wg2s", [32, 2 * BQK], F32)
            bgs = P.sb("bgs", [1, 2 * BQK], F32)
            abT = P.sb("abT", [32, NT], F32)
            gg = P.sb("gg", [128, GH], F32)
            qr = P.sb("qr", [128, NT], BF16)
            kr = P.sb("kr", [128, NT], BF16)
            qtil = P.sb("qtil", [128, NT], BF16)
            ktil = P.sb("ktil", [128, NT], BF16)
            khat = P.sb("khat", [128, NCH, 128], BF16)
            vv = P.sb("vv", [128, NCH, 256], BF16)
            dec = P.sb("dec", [128, NCH], F32)
            ob = [P.sb("ob", [128, NT], F32) for _ in range(2)]
            St = P.sb("St", [128, 128], F32)
            t1 = [P.sb("t1", [128, 512], F32) for _ in range(2)]
            t2 = [P.sb("t2", [128, 512], F32) for _ in range(2)]
            kh = [P.sb("kh", [128, 128], BF16) for _ in range(4)]
            xk = [P.sb("xk", [128, 128], F32) for _ in range(4)]
            sg = [P.sb("sg", [128, 512], BF16) for _ in range(2)]
            yo = [P.sb("yo", [128, 512], BF16) for _ in range(2)]
            P.dma("pool", permb[:], cmat[4], writes=[permb.t()])
            P.dma("pool", identb[:], cmat[3], writes=[identb.t()])
            P.dma("sp", ones128[:], cmat[1], writes=[ones128.t()])
            P.dma("sp", onesr[:], cmat[2][0:1, :], writes=[onesr.t()])
            P.dma("sp", tri[0][:], cmat[5], writes=[tri[0].t()])
            P.dma("sp", tri[1][:], cmat[6], writes=[tri[1].t()])
            P.dma("sp", wg2s[:], wg2[e_], writes=[wg2s.t()])
            P.dma("sp", bgs[:], bg[e_], writes=[bgs.t()])
            P.dma("sp", abT[:], ABT, writes=[abT.t()])
            P.dma("sp", gg[:], gla_gT[:, e_ * GH:(e_ + 1) * GH], writes=[gg.t()])
            k2 = 0
            for fc in range(NFC):
                P.dma("sp", qr[:], QKB[fc * 128:(fc + 1) * 128, :], writes=[qr.t()])
                P.dma("sp", kr[:], QKB[BQK + fc * 128:BQK + (fc + 1) * 128, :], writes=[kr.t()])
                P.dma("sp", vv[:], VBm[:, fc * 256:(fc + 1) * 256].rearrange("(ch p) d -> p ch d", p=128), writes=[vv.t()])
                for buf in (qr, kr):
                    for s0 in range(0, SEQ, 512):
                        a, b = CTX + s0, CTX + s0 + 512
                        ps = P.psum()
                        P.op("pe", lambda e: e.matmul(ps[:, 0:512], permb[:], buf[:, a:b], start=True, stop=True),
                             reads=[permb.t(), buf.t()], writes=[ps.t()])
                        x1, x2 = t1[k2 % 2], t2[k2 % 2]
                        C_, S_ = Cs[k2 % 2], Ss[k2 % 2]
                        k2 += 1
                        P.dma("sp", C_[:], rope[0][:, s0:s0 + 512], writes=[C_.t()])
                        P.dma("sp", S_[:], rope[1][:, s0:s0 + 512], writes=[S_.t()])
                        P.op("dve", lambda e: e.tensor_tensor(x1[:], ps[:, 0:512], S_[:], ALU.mult),
                             reads=[ps.t(), S_.t()], writes=[x1.t()])
                        P.op("pool", lambda e: e.tensor_tensor(x2[:], buf[:, a:b], C_[:], ALU.mult),
                             reads=[buf.t(), C_.t()], writes=[x2.t()])
                        P.op("dve", lambda e: e.tensor_tensor(buf[:, a:b], x1[:], x2[:], ALU.add),
                             reads=[x1.t(), x2.t(), ps.t()], writes=[buf.t()])
                for d in range(2):
                  with P.scope():
                    sp_ = P.sb("sp", [128, NCH, 128], F32)
                    cum = P.sb("cum", [128, NT], F32)
                    zc = d * BQK + fc * 128
                    for ch in range(NCH):
                        ps = P.psum()
                        P.op("pe", lambda e: e.matmul(ps[:, 0:128], abT[:, ch * 128:(ch + 1) * 128], wg2s[:, zc:zc + 128], start=True, stop=False),
                             reads=[abT.t(), wg2s.t()], writes=[ps.t()], sig=False)
                        P.op("pe", lambda e: e.matmul(ps[:, 0:128], onesr[:], bgs[:, zc:zc + 128], start=False, stop=True),
                             reads=[onesr.t(), bgs.t()], writes=[ps.t()])
                        P.op("act", lambda e: e.activation(out=sp_[:, ch, :], in_=ps[:, 0:128], func=AF.Exp, scale=-1.0),
                             reads=[ps.t()], writes=[sp_.t(ch)])
                    for ch in range(NCH):
                        P.op("act", lambda e: e.activation(out=sp_[:, ch, :], in_=sp_[:, ch, :], func=AF.Ln, bias=1.0),
                             reads=[sp_.t(ch)], writes=[sp_.t(ch)])
                    for c4 in range(0, NCH, 4):
                        nn = min(4, NCH - c4)
                        ps = P.psum()
                        for u in range(nn):
                            P.op("pe", lambda e: e.matmul(ps[:, u * 128:(u + 1) * 128], sp_[:, c4 + u, :], tri[d][:], start=True, stop=True),
                                 reads=[sp_.t(c4 + u), tri[d].t()], writes=[ps.t()])
                        P.op("dve", lambda e: e.tensor_scalar(cum[:, c4 * 128:(c4 + nn) * 128], ps[:, 0:nn * 128], -1.0 / 16, None, ALU.mult),
                             reads=[ps.t()], writes=[cum.t(c4)])
                    allcum = [cum.t(c4) for c4 in range(0, NCH, 4)]
                    lastcol = 127 if d == 0 else 0
                    P.op("act", lambda e: e.activation(out=dec[:], in_=cum[:, lastcol:NT:128], func=AF.Exp),
                         reads=allcum, writes=[dec.t()])
                    for s0 in range(0, NT, 512):
                        n = min(512, NT - s0)
                        x1, x2 = t1[k2 % 2], t2[k2 % 2]
                        k2 += 1
                        P.op("act", lambda e: e.activation(out=x1[:, 0:n], in_=cum[:, s0:s0 + n], func=AF.Exp),
                             reads=allcum, writes=[x1.t()])
                        P.op("dve", lambda e: e.tensor_tensor(qtil[:, s0:s0 + n], qr[:, s0:s0 + n], x1[:, 0:n], ALU.mult),
                             reads=[qr.t(), x1.t()], writes=[qtil.t()])
                        P.op("act", lambda e: e.activation(out=x2[:, 0:n], in_=cum[:, s0:s0 + n], func=AF.Exp, scale=-1.0),
                             reads=allcum, writes=[x2.t()])
                        P.op("pool", lambda e: e.tensor_tensor(ktil[:, s0:s0 + n], kr[:, s0:s0 + n], x2[:, 0:n], ALU.mult),
                             reads=[kr.t(), x2.t()], writes=[ktil.t()])
                    for ch in range(NCH):
                        x1 = t1[k2 % 2]
                        khb = kh[k2 % 2]
                        k2 += 1
                        lc = ch * 128 + lastcol
                        P.op("act", lambda e: e.activation(out=x1[:, 0:128], in_=cum[:, ch * 128:(ch + 1) * 128], func=AF.Exp,
                                                           scale=-1.0, bias=cum[:, lc:lc + 1]),
                             reads=allcum, writes=[x1.t()])
                        P.op("dve", lambda e: e.tensor_tensor(khb[:], kr[:, ch * 128:(ch + 1) * 128], x1[:, 0:128], ALU.mult),
                             reads=[kr.t(), x1.t()], writes=[khb.t()])
                        P.op("pe", lambda e: e.transpose(P.psb[:, (ch % 8) * 128:(ch % 8 + 1) * 128], khb[:], identb[:]),
                             reads=[khb.t(), identb.t()], writes=[P.psb.t(ch % 8)])
                        P.op("act", lambda e: e.copy(khat[:, ch, :], P.psb[:, (ch % 8) * 128:(ch % 8 + 1) * 128]),
                             reads=[P.psb.t(ch % 8)], writes=[khat.t(ch)])
                  with P.scope():
                    aTa = P.sb("aTa", [128, NCH, 2, 128], BF16)
                    Sba = P.sb("Sba", [128, NCH, 128], BF16)
                    P.op("dve", lambda e: e.memset(St[:], 0.0), writes=[St.t()])
                    if d == 0:
                        order = list(range(NCH))
                    else:
                        order = list(range(NCC - 1, -1, -1)) + list(range(NCH - 1, NCC - 1, -1))
                    for ch in order:
                        a, b = ch * 128, (ch + 1) * 128
                        P.op("dve", lambda e: e.tensor_copy(Sba[:, ch, :], St[:]), reads=[St.t()], writes=[Sba.t(ch)])
                        ps_kv = P.psum()
                        P.op("pe", lambda e: e.matmul(ps_kv[:, 0:256], khat[:, ch, :], vv[:, ch, :], start=True, stop=True),
                             reads=[khat.t(ch), vv.t()], writes=[ps_kv.t()])
                        for hh in range(2):
                            pa, pb = hh * 64, (hh + 1) * 64
                            ps_a = P.psum()
                            P.op("pe", lambda e: e.matmul(ps_a[:, 0:128], ktil[pa:pb, a:b], qtil[pa:pb, a:b], start=True, stop=True),
                                 reads=[ktil.t(), qtil.t()], writes=[ps_a.t()])
                            P.op("dve", lambda e: e.tensor_tensor(aTa[:, ch, hh, :], ps_a[:, 0:128], tri[d][:], ALU.mult),
                                 reads=[ps_a.t(), tri[d].t()], writes=[aTa.t((ch, hh))])
                        for hh in range(2):
                            pa, pb = hh * 64, (hh + 1) * 64
                            P.op("dve", lambda e: e.scalar_tensor_tensor(St[pa:pb, :], St[pa:pb, :], dec[pa:pb, ch:ch + 1],
                                                                         ps_kv[pa:pb, hh * 128:(hh + 1) * 128], ALU.mult, ALU.add),
                                 reads=[St.t(), dec.t(), ps_kv.t()], writes=[St.t()])
                    for ch in order:
                        a, b = ch * 128, (ch + 1) * 128
                        for hh in range(2):
                            pa, pb = hh * 64, (hh + 1) * 64
                            ps_o = P.psum()
                            P.op("pe", lambda e: e.matmul(ps_o[:, 0:128], Sba[pa:pb, ch, :], qtil[pa:pb, a:b], start=True, stop=False),
                                 reads=[Sba.t(ch), qtil.t()], writes=[ps_o.t()], sig=False)
                            P.op("pe", lambda e: e.matmul(ps_o[:, 0:128], vv[:, ch, hh * 128:(hh + 1) * 128], aTa[:, ch, hh, :], start=False, stop=True),
                                 reads=[vv.t(), aTa.t((ch, hh))], writes=[ps_o.t()])
                            if d == 0:
                                P.op("act", lambda e: e.copy(ob[hh][:, a:b], ps_o[:, 0:128]), reads=[ps_o.t()], writes=[ob[hh].t(ch)])
                            else:
                                P.op("dve", lambda e: e.tensor_tensor(ob[hh][:, a:b], ob[hh][:, a:b], ps_o[:, 0:128], ALU.add),
                                     reads=[ps_o.t(), ob[hh].t(ch)], writes=[ob[hh].t(ch)])
                for hh in range(2):
                    hg = fc * 2 + hh
                    t_lo = 0 if need_ctx else CTX
                    for s0 in range(t_lo, NT, 512):
                        n = min(512, NT - s0)
                        chs = [ob[hh].t(ch) for ch in range(s0 // 128, (s0 + n) // 128)]
                        x1, x2 = t1[k2 % 2], t2[k2 % 2]
                        s_, y_ = sg[k2 % 2], yo[k2 % 2]
                        k2 += 1
                        P.dma("sp", s_[:, 0:n], GBs[hg * 128:(hg + 1) * 128, s0:s0 + n], writes=[s_.t()])
                        P.op("dve", lambda e: e.tensor_tensor(x1[:, 0:n], ob[hh][:, s0:s0 + n], ob[hh][:, s0:s0 + n], ALU.mult),
                             reads=chs, writes=[x1.t()])
                        ps = P.psum()
                        P.op("pe", lambda e: e.matmul(ps[:, 0:n], ones128[:], x1[:, 0:n], start=True, stop=True),
                             reads=[ones128.t(), x1.t()], writes=[ps.t()])
                        P.op("act", lambda e: e.activation(out=x2[:, 0:n], in_=ps[:, 0:n], func=AF.Ln, bias=EPS),
                             reads=[ps.t()], writes=[x2.t()])
                        P.op("act", lambda e: e.activation(out=x2[:, 0:n], in_=x2[:, 0:n], func=AF.Exp, scale=-0.5),
                             reads=[x2.t()], writes=[x2.t()])
                        P.op("dve", lambda e: e.tensor_tensor(x1[:, 0:n], ob[hh][:, s0:s0 + n], x2[:, 0:n], ALU.mult),
                             reads=chs + [x2.t(), x1.t()], writes=[x1.t()])
                        P.op("dve", lambda e: e.scalar_tensor_tensor(y_[:, 0:n], x1[:, 0:n], gg[:, hg:hg + 1], s_[:, 0:n], ALU.mult, ALU.mult),
                             reads=[x1.t(), gg.t(), s_.t()], writes=[y_.t()])
                        P.dma("sp", OT[AW + hg * 128:AW + (hg + 1) * 128, s0:s0 + n], y_[:, 0:n], reads=[y_.t()])

    def e_out(i, e_, src, dst, tok0, ntok, g):
        TB = min(1024, ntok)
        with P.scope():
            ob_ = P.sb("otb", [128, KC, TB], BF16)
            wt = [P.sb("wo", [128, KC, 512], BF16) for _ in range(2)]
            xo = [P.sb("xo", [128, 512], F32) for _ in range(2)]
            xn = [P.sb("xn", [128, 512], F32) for _ in range(2)]
            kk = 0
            for b0 in range(0, ntok, TB):
                tb = min(TB, ntok - b0)
                P.dma("sp", ob_[:, :, 0:tb], OT[:, tok0 + b0:tok0 + b0 + tb].rearrange("(kc p) n -> p kc n", p=128), writes=[ob_.t()])
                for cg in range(D // 512):
                    w = wt[cg % 2]
                    P.dma("pool", w[:], w_out[e_, :, cg * 512:(cg + 1) * 512].rearrange("(kc p) n -> p kc n", p=128), writes=[w.t()])
                    for j in range(4):
                        oc = cg * 4 + j
                        for s0 in range(0, tb, 512):
                            n = min(512, tb - s0)
                            k = kk % 2
                            kk += 1
                            P.dma("sp", xo[k][:, 0:n], src[oc * 128:(oc + 1) * 128, b0 + s0:b0 + s0 + n], writes=[xo[k].t()])
                            ps = P.psum()
                            for kc in range(KC):
                                P.op("pe", lambda e: e.matmul(ps[:, 0:n], w[:, kc, j * 128:(j + 1) * 128], ob_[:, kc, s0:s0 + n],
                                                              start=(kc == 0), stop=(kc == KC - 1)),
                                     reads=[w.t(), ob_.t()], writes=[ps.t()], sig=(kc == KC - 1))
                            P.op("dve", lambda e: e.scalar_tensor_tensor(xn[k][:, 0:n], ps[:, 0:n], mv(i, 2, oc, g), xo[k][:, 0:n], ALU.mult, ALU.add),
                                 reads=[ps.t(), xo[k].t(), modv.t()], writes=[xn[k].t()])
                            P.dma("sp", dst[oc * 128:(oc + 1) * 128, b0 + s0:b0 + s0 + n], xn[k][:, 0:n], reads=[xn[k].t()])

    def st_even(i, xsrc, csrc, xdst, cdst, need_ctx, nxt):
        e_ = i // 2
        e_proj(i, e_, csrc, 0, CTX, 1)
        e_proj(i, e_, xsrc, CTX, SEQ, 0)
        e_na(e_, need_ctx, nxt)
        e_gla(e_, need_ctx)
        e_out(i, e_, xsrc, xdst, CTX, SEQ, 0)
        if need_ctx:
            e_out(i, e_, csrc, cdst, 0, CTX, 1)

    st_modvec(0)
    xsrc, csrc = xT, cT
    for i in range(DEPTH):
        is_even = i % 2 == 0
        need_ctx = any(j % 2 == 0 for j in range(i + 1, DEPTH))
        nxt = i + 1 if i + 1 < DEPTH else None
        used = False
        if is_even and fl["even"]:
            st_even(i, xsrc, csrc, X[1], C[1], need_ctx, nxt)
            used = True
        elif (not is_even) and fl["odd"]:
            st_pool(i, xsrc, X[1], SEQ, 0, None)
            if need_ctx:
                st_pool(i, csrc, C[1], CTX, 1, None)
        else:
            st_copy(xsrc, X[1], SEQ)
            if need_ctx:
                st_copy(csrc, C[1], CTX)
        if nxt is not None and not used:
            st_modvec(nxt)
        if fl["ffn"]:
            st_ffn(i, X[1], X[0], SEQ, 0)
            if need_ctx:
                st_ffn(i, C[1], C[0], CTX, 1)
        else:
            st_copy(X[1], X[0], SEQ)
            if need_ctx:
                st_copy(C[1], C[0], CTX)
        xsrc, csrc = X[0], C[0]
    st_final(xsrc)
    P.barrier()
    return nc


def _fm(v, KC):
    return np.ascontiguousarray(np.asarray(v, np.float32).reshape(KC, 128).T)


def prep_shared(cfg, inp):
    c = cfg
    KC, FC, DEPTH = c.KC, c.FC, c.DEPTH
    f = lambda a: np.ascontiguousarray(np.asarray(a, np.float32))
    sh = {}
    sh["w_mod"] = f(inp["w_mod"])
    sh["b_modT"] = f(np.stack([_fm(inp["b_mod"][i], 6 * KC) for i in range(DEPTH)], 1).reshape(128, -1))
    ngs = np.stack([np.stack([_fm(inp["norm1_g"][i], KC), _fm(inp["norm2_g"][i], KC)], 1) for i in range(DEPTH)], 1)
    sh["ngT"] = f(ngs.reshape(128, -1))
    sh["fgT"] = _fm(inp["final_g"], KC)
    sh["w_up"] = f(inp["w_up"])
    sh["w_down"] = f(inp["w_down"])
    cw = np.asarray(inp["conv_w"], np.float32)
    cb = np.asarray(inp["conv_b"], np.float32)
    cvt = np.zeros((128, DEPTH, FC, 4), np.float32)
    for i in range(DEPTH):
        for k in range(3):
            cvt[:, i, :, k] = _fm(cw[i, k], FC)
        cvt[:, i, :, 3] = _fm(cb[i], FC)
    sh["convT"] = f(cvt.reshape(128, -1))
    cm = np.zeros((8, 128, 128), np.float32)
    cm[0] = 1.0 / c.D
    cm[1] = 1.0 / 128
    cm[2] = 1.0
    cm[3] = np.eye(128)
    p = np.arange(128)
    perm = np.where((p % 32) < 16, p + 16, p - 16)
    cm[4][perm, p] = 1.0
    jj, ii = np.meshgrid(p, p, indexing="ij")
    cm[5] = np.where(jj <= ii, 1.0, 0.0)
    cm[6] = np.where(jj >= ii, 1.0, 0.0)
    sh["cmat"] = cm
    sh["pool_w"] = f(inp["pool_w"])
    NE, BQK, NAH, NBLK = c.NE, c.BQK, c.NAH, c.NBLK
    sh["w_in"] = f(inp["w_in"])
    sh["w_out"] = f(inp["w_out"])
    wg = np.zeros((NE, 32, 2 * BQK), np.float32)
    w2 = np.asarray(inp["w_gate2"], np.float32)
    for d in range(2):
        wg[:, d * 16:(d + 1) * 16, d * BQK:(d + 1) * BQK] = w2[:, d]
    sh["wg2"] = wg
    sh["bg"] = f(np.asarray(inp["b_gate"], np.float32).reshape(NE, 1, 2 * BQK))
    rp = np.asarray(inp["rpb"], np.float32)
    pp = np.arange(128)
    half = pp // 64
    kcol = pp % 64
    bb = np.arange(NBLK)
    qcol = np.arange(64)
    e_idx = bb[None, :] - 4 - half[:, None]
    ev = (e_idx >= 0) & (e_idx <= 14)
    dr = np.clip(14 - e_idx, 0, 14)
    dc = np.clip(kcol[:, None] - qcol[None, :] + 15, 0, 30)
    rx = rp[:, :, dr[:, :, None], dc[:, None, :]]
    rx = np.where(ev[None, None, :, :, None], rx, 0.0).astype(np.float32)
    sh["rpbx"] = f(rx.reshape(NE, NAH, 128, NBLK * 64))
    cstart = np.clip(qcol - 8, 0, 48)
    col_in = (kcol[:, None] >= cstart[None, :]) & (kcol[:, None] < cstart[None, :] + 16)
    mF = (ev[:, :, None] & col_in[:, None, :])
    mZ = mF & ((e_idx >= 4) & (e_idx <= 11))[:, :, None]
    sh["nmask"] = f(np.stack([mF, mZ], 0).astype(np.float32).reshape(2, 128, NBLK * 64))
    t = np.arange(c.SEQ)
    row = (t // 64).astype(np.float32)
    colp = (t % 64).astype(np.float32)
    dd = pp % 64
    ii = dd % 32
    inv = (np.float32(10000.0) ** (-(np.arange(16, dtype=np.float32)) / np.float32(16))).astype(np.float32)
    pos = np.where((dd // 32)[:, None] == 0, row[None, :], colp[None, :]).astype(np.float32)
    ang = (pos * inv[ii % 16][:, None]).astype(np.float32)
    sgn = np.where(ii < 16, -1.0, 1.0).astype(np.float32)
    sh["rope"] = f(np.stack([np.cos(ang), np.sin(ang) * sgn[:, None]], 0))
    gg = np.asarray(inp["gla_norm_g"], np.float32)
    sh["gla_gT"] = f(gg.transpose(2, 0, 1).reshape(128, -1))
    sh["pool_scT"] = f(np.stack([_fm(inp["pool_scale"][o], KC) for o in range(c.NO)], 1).reshape(128, -1))
    def icnt(L):
        t = np.arange(L)
        out = np.zeros((4, L), np.float32)
        for wi, w in enumerate((2, 4, 8, 16)):
            lo = np.clip(t - w // 2, 0, L); hi = np.clip(t + w // 2, 0, L)
            out[wi] = 1.0 / (hi - lo).astype(np.float32)
        return out
    sh["icnt_l"] = icnt(c.SEQ)
    sh["icnt_c"] = icnt(c.CTX)
    return sh


def prep_core(cfg, inp, b, zero=False):
    c = cfg
    d = {}
    x = np.asarray(inp["x"][b], np.float32)
    cx = np.asarray(inp["ctx"][b], np.float32)
    cvec = np.stack([_fm(inp["c"][b], c.KC), _fm(inp["c_ctx"], c.KC)], 2).reshape(128, -1)
    if zero:
        d["xT"] = np.zeros((c.D, c.SEQ), np.float32)
        d["cT"] = np.zeros((c.D, c.CTX), np.float32)
        d["cv"] = np.zeros_like(cvec)
    else:
        d["xT"] = np.ascontiguousarray(x.T)
        d["cT"] = np.ascontiguousarray(cx.T)
        d["cv"] = np.ascontiguousarray(cvec)
    return d


def run(cfg, inp, flags=None):
    nc = build(cfg, flags)
    sh = prep_shared(cfg, inp)
    names = set()
    in_maps = []
    ncore = 8
    for k in range(ncore):
        b = (k // 2) % cfg.B
        m = dict(sh)
        m.update(prep_core(cfg, inp, b, zero=(k % 2 == 1 or k // 2 >= cfg.B)))
        in_maps.append(m)
    res = run_bass_kernel_spmd(nc, in_maps, core_ids=list(range(ncore)))
    out = np.stack([np.ascontiguousarray(res.results[2 * b]["outT"].T) for b in range(cfg.B)], 0)
    return out.astype(np.float32)


def kernel(**inputs):
    return run(Cfg(), inputs)
```

```python
import contextlib
import numpy as np
import concourse.bass as bass
import concourse.mybir as mybir
from concourse.bass_utils import run_bass_kernel_spmd

F32, BF16 = mybir.dt.float32, mybir.dt.bfloat16
AF = mybir.ActivationFunctionType
ALU = mybir.AluOpType
ND = 6
EPS = 1e-6


class Cfg:
    def __init__(s, D=2048, SEQ=4096, CTX=256, DEPTH=4, DFF=5632, TBF=2048, B=4, TBE=1024):
        s.D, s.SEQ, s.CTX, s.DEPTH, s.DFF, s.TBF, s.B = D, SEQ, CTX, DEPTH, DFF, TBF, B
        s.KC = D // 128
        s.TBE = TBE
        s.NH = D // 128
        s.NAH = s.NH // 2
        s.GH = s.NH - s.NAH
        s.AW, s.BQK, s.BV = s.NAH * 128, s.GH * 64, s.GH * 128
        s.DIN = 3 * s.AW + 2 * s.BQK + 2 * s.BV + 32
        s.ROWS = SEQ // 64
        s.PG = D // 4
        s.FC = DFF // 128
        s.NE = (DEPTH + 1) // 2
        s.NO = DEPTH // 2
        s.NBLK = 23


def _split(a, b, mx=256):
    n = b - a
    k = (n + mx - 1) // mx
    out, t = [], a
    for i in range(k):
        m = n // k + (1 if i < n % k else 0)
        out.append((t, m))
        t += m
    return out


class Tk:
    __slots__ = ("w", "r")

    def __init__(s):
        s.w = None
        s.r = {}


class Buf:
    def __init__(s, h):
        s.h = h
        s.tks = {}

    def t(s, key=0):
        if key not in s.tks:
            s.tks[key] = Tk()
        return s.tks[key]

    def seg(s, kc, ca, cb, g=64):
        return [s.t((kc, q)) for q in range(ca // g, (cb - 1) // g + 1)]

    def __getitem__(s, k):
        return s.h[k]


class Prog:
    def __init__(s, nc):
        s.nc = nc
        s.eng = {"pe": nc.tensor, "act": nc.scalar, "dve": nc.vector, "pool": nc.gpsimd, "sp": nc.sync}
        s.sem = {k: nc.alloc_semaphore("s_" + k) for k in s.eng}
        s.cnt = {k: 0 for k in s.eng}
        s.pend = False
        s.waited = {k: {} for k in s.eng}
        s.dsem = {q: [nc.alloc_semaphore("d_%s%d" % (q, i)) for i in range(ND)] for q in ("sp", "pool", "act")}
        s.dcnt = {q: [0] * ND for q in s.dsem}
        s.dnext = {q: 0 for q in s.dsem}
        s.nps = 0
        s.nrot = 7
        s.ps = [Buf(nc.alloc_psum_tensor("ps%d" % i, [128, 512], F32)) for i in range(7)]
        s.psb = Buf(nc.alloc_psum_tensor("psb", [128, 1024], BF16))
        s.stack = None
        s.nalloc = 0

    @contextlib.contextmanager
    def scope(s):
        old = s.stack
        with contextlib.ExitStack() as st:
            s.stack = st
            yield
            s.barrier()
        s.stack = old

    def sb(s, name, shape, dt):
        s.nalloc += 1
        nm = "%s_%d" % (name, s.nalloc)
        if s.stack is None:
            return Buf(s.nc.alloc_sbuf_tensor(nm, list(shape), dt))
        return Buf(s.stack.enter_context(s.nc.sbuf_tensor(nm, list(shape), dt)))

    def psum(s):
        b = s.ps[s.nps % s.nrot]
        s.nps += 1
        return b

    def _wait(s, e, tok):
        if tok is None:
            return
        key, sem, val = tok
        if e == "pe" and key == "pe":
            return
        if s.waited[e].get(key, -1) >= val:
            return
        s.eng[e].wait_ge(sem, val)
        s.waited[e][key] = val

    def _deps(s, e, reads, writes):
        for t in reads:
            s._wait(e, t.w)
        for t in writes:
            s._wait(e, t.w)
            for r in t.r.values():
                s._wait(e, r)

    def _mark(s, tok, reads, writes):
        for t in reads:
            t.r[tok[0]] = tok
        for t in writes:
            t.w = tok
            t.r = {}

    def op(s, e, fn, reads=(), writes=(), sig=True):
        s._deps(e, reads, writes)
        ins = fn(s.eng[e])
        if e == "pe" and not sig:
            tok = ("pe", s.sem["pe"], s.cnt["pe"] + 1)
            s.pend = True
        else:
            s.cnt[e] += 1
            ins.then_inc(s.sem[e], 1)
            tok = (e, s.sem[e], s.cnt[e])
            if e == "pe":
                s.pend = False
        s._mark(tok, reads, writes)
        return tok

    def dma(s, q, out, in_, reads=(), writes=(), **kw):
        i = s.dnext[q]
        s.dnext[q] = (i + 1) % ND
        sem = s.dsem[q][i]
        key = "d_%s%d" % (q, i)
        if s.dcnt[q][i] > 0:
            s._wait(q, (key, sem, s.dcnt[q][i]))
        s._deps(q, reads, writes)
        ins = s.eng[q].dma_start(out=out, in_=in_, **kw)
        s.dcnt[q][i] += 16
        ins.then_inc(sem, 16)
        tok = (key, sem, s.dcnt[q][i])
        s._mark(tok, reads, writes)
        return tok

    def barrier(s):
        assert not s.pend
        toks = [(k, s.sem[k], s.cnt[k]) for k in s.eng if s.cnt[k] > 0]
        for q in s.dsem:
            for i in range(ND):
                if s.dcnt[q][i] > 0:
                    toks.append(("d_%s%d" % (q, i), s.dsem[q][i], s.dcnt[q][i]))
        for e in s.eng:
            for tok in toks:
                s._wait(e, tok)


def build(cfg, flags=None):
    fl = dict(even=True, odd=True, ffn=True)
    if flags:
        fl.update(flags)
    c = cfg
    D, SEQ, CTX, DEPTH, DFF, KC, FC = c.D, c.SEQ, c.CTX, c.DEPTH, c.DFF, c.KC, c.FC
    nc = bass.Bass("TRN2", target_bir_lowering=False)
    P = Prog(nc)

    def din(name, shape):
        return nc.dram_tensor(name, list(shape), F32, kind="ExternalInput").ap()

    def dscr(name, shape, dt=F32):
        return nc.dram_tensor(name, list(shape), dt).ap()

    xT = din("xT", [D, SEQ])
    cT = din("cT", [D, CTX])
    cv = din("cv", [128, KC * 2])
    w_mod = din("w_mod", [DEPTH, D, 6 * D])
    b_modT = din("b_modT", [128, DEPTH * 6 * KC])
    ngT = din("ngT", [128, DEPTH * 2 * KC])
    fgT = din("fgT", [128, KC])
    w_up = din("w_up", [DEPTH, D, 2 * DFF])
    w_down = din("w_down", [DEPTH, DFF, D])
    convT = din("convT", [128, DEPTH * FC * 4])
    pool_w = din("pool_w", [c.NO, 4, c.PG, c.PG])
    pool_scT = din("pool_scT", [128, c.NO * KC])
    icnt_l = din("icnt_l", [4, SEQ])
    icnt_c = din("icnt_c", [4, CTX])
    w_in = din("w_in", [c.NE, D, c.DIN])
    w_out = din("w_out", [c.NE, D, D])
    wg2 = din("wg2", [c.NE, 32, 2 * c.BQK])
    bg = din("bg", [c.NE, 1, 2 * c.BQK])
    rpbx = din("rpbx", [c.NE, c.NAH, 128, c.NBLK * 64])
    nmask = din("nmask", [2, 128, c.NBLK * 64])
    rope = din("rope", [2, 128, SEQ])
    gla_gT = din("gla_gT", [128, c.NE * c.GH])
    cmat = din("cmat", [8, 128, 128])
    outT = nc.dram_tensor("outT", [D, SEQ], F32, kind="ExternalOutput").ap()
    X = [dscr("X0", [D, SEQ]), dscr("X1", [D, SEQ])]
    C = [dscr("C0", [D, CTX]), dscr("C1", [D, CTX])]

    onesD = P.sb("onesD", [128, 128], BF16)
    silc = P.sb("silc", [128, KC * 2], BF16)
    cvs = P.sb("cvs", [128, KC * 2], F32)
    modv = P.sb("modv", [128, DEPTH * 6 * KC * 2], F32)
    bmod = P.sb("bmod", [128, DEPTH * 6 * KC], F32)
    ng = P.sb("ng", [128, DEPTH * 2 * KC], F32)
    fg = P.sb("fg", [128, KC], F32)
    conv = P.sb("conv", [128, DEPTH * FC * 4], F32)
    P.dma("pool", onesD[:], cmat[0], writes=[onesD.t()])
    P.dma("sp", cvs[:], cv, writes=[cvs.t()])
    P.dma("sp", bmod[:], b_modT, writes=[bmod.t()])
    P.dma("sp", ng[:], ngT, writes=[ng.t()])
    P.dma("sp", fg[:], fgT, writes=[fg.t()])
    P.dma("sp", conv[:], convT, writes=[conv.t()])
    P.op("act", lambda e: e.activation(out=silc[:], in_=cvs[:], func=AF.Silu), reads=[cvs.t()], writes=[silc.t()])

    def mvcol(i, v, kc, g):
        return ((i * 6 + v) * KC + kc) * 2 + g

    def mv(i, v, kc, g):
        cidx = mvcol(i, v, kc, g)
        return modv[:, cidx:cidx + 1]

    def modvec_gen(i, gw, nbuf):
        wt = [P.sb("mw", [128, KC, gw], BF16) for _ in range(nbuf)]
        mvt = [P.sb("mvt", [2, gw], F32) for _ in range(2)]
        i2 = P.sb("i2", [2, 2], F32)
        P.dma("sp", i2[:], cmat[3][0:2, 0:2], writes=[i2.t()])
        noc = 6 * KC
        base = i * noc * 2
        ngrp = 6 * D // gw
        no = gw // 128
        for g in range(ngrp):
            w = wt[g % nbuf]
            m_ = mvt[g % 2]
            P.dma("pool", w[:], w_mod[i, :, g * gw:(g + 1) * gw].rearrange("(kc p) n -> p kc n", p=128), writes=[w.t()])
            pg = P.psum()
            for kc in range(KC):
                P.op("pe", lambda e: e.matmul(pg[0:2, 0:gw], silc[:, kc * 2:kc * 2 + 2], w[:, kc, :],
                                              start=(kc == 0), stop=(kc == KC - 1)),
                     reads=[w.t(), silc.t()], writes=[pg.t()], sig=(kc == KC - 1))
            P.op("act", lambda e: e.copy(m_[:, 0:gw], pg[0:2, 0:gw]), reads=[pg.t()], writes=[m_.t()])
            pt = P.psum()
            for j in range(no):
                P.op("pe", lambda e: e.matmul(pt[:, j * 2:j * 2 + 2], m_[:, j * 128:(j + 1) * 128], i2[:], start=True, stop=True),
                     reads=[m_.t(), i2.t()], writes=[pt.t()], sig=(j == no - 1))
            oc0 = g * no
            P.op("dve", lambda e: e.tensor_tensor(modv[:, base + oc0 * 2:base + (oc0 + no) * 2].rearrange("p (o t) -> p o t", t=2),
                                                  pt[:, 0:no * 2].rearrange("p (o t) -> p o t", t=2),
                                                  bmod[:, i * noc + oc0:i * noc + oc0 + no].unsqueeze(2).to_broadcast([128, no, 2]),
                                                  ALU.add),
                 reads=[pt.t(), bmod.t()], writes=[modv.t()])
            yield
        for (v, which) in ((1, 0), (4, 1)):
            for g in range(2):
                b0 = mvcol(i, v, 0, g)
                sl = modv[:, b0:b0 + 2 * KC:2]
                P.op("dve", lambda e: e.scalar_tensor_tensor(sl, sl, 1.0, ng[:, (i * 2 + which) * KC:(i * 2 + which + 1) * KC],
                                                             ALU.add, ALU.mult),
                     reads=[modv.t(), ng.t()], writes=[modv.t()])

    def st_modvec(i):
        with P.scope():
            for _ in modvec_gen(i, 512, 2):
                pass

    class Stepper:
        def __init__(s_, gen, nsteps, total):
            s_.gen, s_.per = gen, (total + nsteps - 1) // nsteps

        def step(s_):
            if s_.gen is None:
                return
            for _ in range(s_.per):
                try:
                    next(s_.gen)
                except StopIteration:
                    s_.gen = None
                    return

        def finish(s_):
            while s_.gen is not None:
                s_.step()

    def rms_rstd(xs, n, rs, sqs):
        ps = P.psum()
        P.op("dve", lambda e: e.tensor_tensor(sqs[:, :, 0:n], xs[:, :, 0:n], xs[:, :, 0:n], ALU.mult),
             reads=[xs.t()], writes=[sqs.t()])
        for kc in range(KC):
            P.op("pe", lambda e: e.matmul(ps[:, 0:n], onesD[:], sqs[:, kc, 0:n], start=(kc == 0), stop=(kc == KC - 1)),
                 reads=[sqs.t(), onesD.t()], writes=[ps.t()], sig=(kc == KC - 1))
        P.op("act", lambda e: e.activation(out=rs[:, 0:n], in_=ps[:, 0:n], func=AF.Ln, bias=EPS),
             reads=[ps.t()], writes=[rs.t()])
        P.op("act", lambda e: e.activation(out=rs[:, 0:n], in_=rs[:, 0:n], func=AF.Exp, scale=-0.5),
             reads=[rs.t()], writes=[rs.t()])

    def norm_mod(src, t0, n, dstbuf, c0, wk, i, va, vs, g, xs, rs, sqs, tmps):
        P.dma("sp", xs[:, :, 0:n], src[:, t0:t0 + n].rearrange("(kc p) n -> p kc n", p=128), writes=[xs.t()])
        rms_rstd(xs, n, rs, sqs)
        for kc in range(KC):
            tmp = tmps[kc % 2]
            P.op("dve", lambda e: e.tensor_tensor(tmp[:, 0:n], xs[:, kc, 0:n], rs[:, 0:n], ALU.mult),
                 reads=[xs.t(), rs.t()], writes=[tmp.t()])
            P.op("act", lambda e: e.activation(out=dstbuf[:, kc, c0:c0 + n], in_=tmp[:, 0:n], func=AF.Identity,
                                               bias=mv(i, vs, kc, g), scale=mv(i, va, kc, g)),
                 reads=[tmp.t(), modv.t()], writes=wk(kc))

    def norm_gen(src, subs, dstbuf, wkf, i, va, vs, g, nb):
        def stage1(k):
            t0, n, c0 = subs[k]
            xs, sq, rs = nb["xs"][k % 2], nb["sq"][k % 2], nb["rs"][k % 2]
            P.dma("sp", xs[:, :, 0:n], src[:, t0:t0 + n].rearrange("(kc p) n -> p kc n", p=128), writes=[xs.t()])
            rms_rstd(xs, n, rs, sq)

        def stage2(k):
            t0, n, c0 = subs[k]
            xs, rs = nb["xs"][k % 2], nb["rs"][k % 2]
            ba, bs = mvcol(i, va, 0, g), mvcol(i, vs, 0, g)
            rb = rs[:, 0:n].unsqueeze(1).to_broadcast([128, KC, n])
            ab = modv[:, ba:ba + 2 * KC:2].unsqueeze(2).to_broadcast([128, KC, n])
            sb_ = modv[:, bs:bs + 2 * KC:2].unsqueeze(2).to_broadcast([128, KC, n])
            P.op("dve", lambda e: e.tensor_tensor(xs[:, :, 0:n], xs[:, :, 0:n], rb, ALU.mult),
                 reads=[xs.t(), rs.t()], writes=[xs.t()])
            P.op("dve", lambda e: e.tensor_tensor(xs[:, :, 0:n], xs[:, :, 0:n], ab, ALU.mult),
                 reads=[xs.t(), modv.t()], writes=[xs.t()])
            P.op("dve", lambda e: e.tensor_tensor(dstbuf[:, :, c0:c0 + n], xs[:, :, 0:n], sb_, ALU.add),
                 reads=[xs.t(), modv.t()], writes=[t_ for kc in range(KC) for t_ in wkf(kc, c0, n)])
        if not subs:
            return
        stage1(0)
        for k in range(len(subs)):
            if k + 1 < len(subs):
                stage1(k + 1)
            stage2(k)
            yield

    def norm_run(*a_, **k_):
        for _ in norm_gen(*a_, **k_):
            pass

    def norm_bufs(ns):
        return dict(xs=[P.sb("xs", [128, KC, ns], F32) for _ in range(2)],
                    sq=[P.sb("sq", [128, KC, ns], BF16) for _ in range(2)],
                    rs=[P.sb("rs", [128, ns], F32) for _ in range(2)],
                    tmp=[])

    ACTS = dscr("ACTS", [FC, 128, SEQ], BF16)
    NSUB = 128

    def st_ffn(i, src, dst, ntok, g):
        PT = min(c.TBF, ntok)
        with P.scope():
            h2 = P.sb("h2", [128, KC, PT + 2], BF16)
            gsb = P.sb("gsb", [128, PT + 2], F32)
            cvt = P.sb("cvt", [128, PT], F32)
            val = [P.sb("val", [128, PT], BF16) for _ in range(2)]
            aj = [P.sb("aj", [128, PT], BF16) for _ in range(2)]
            nb = norm_bufs(NSUB)
            wv = [P.sb("wv", [128, KC, 512], BF16) for _ in range(2)]
            wg = [P.sb("wg", [128, KC, 512], BF16) for _ in range(2)]
            for b0 in range(0, ntok, PT):
                tb = min(PT, ntok - b0)
                lo, hi = b0 - 1, b0 + tb + 1
                if lo < 0:
                    for kc in range(KC):
                        P.op("pool", lambda e: e.memset(h2[:, kc, 0:1], 0.0), writes=h2.seg(kc, 0, 1))
                if hi > ntok:
                    for kc in range(KC):
                        P.op("pool", lambda e: e.memset(h2[:, kc, tb + 1:tb + 2], 0.0), writes=h2.seg(kc, tb + 1, tb + 2))
                a, bnd = max(lo, 0), min(hi, ntok)
                norm_run(src, [(t0, n, t0 - lo) for (t0, n) in _split(a, bnd, NSUB)], h2,
                         (lambda kc, c0, n: h2.seg(kc, c0, c0 + n)), i, 4, 3, g, nb)
                nsub = (tb + 511) // 512
                for jg in range((FC + 3) // 4):
                    nj = min(4, FC - jg * 4)
                    wvj, wgj = wv[jg % 2], wg[jg % 2]
                    P.dma("pool", wvj[:, :, 0:nj * 128], w_up[i, :, jg * 512:jg * 512 + nj * 128].rearrange("(kc p) n -> p kc n", p=128),
                          writes=[wvj.t()])
                    P.dma("pool", wgj[:, :, 0:nj * 128], w_up[i, :, DFF + jg * 512:DFF + jg * 512 + nj * 128].rearrange("(kc p) n -> p kc n", p=128),
                          writes=[wgj.t()])
                    for jj in range(nj):
                        j = jg * 4 + jj
                        ws = slice(jj * 128, (jj + 1) * 128)
                        vj, ajj = val[j % 2], aj[j % 2]
                        for s in range(nsub):
                            n = min(512, tb - s * 512)
                            c0 = 1 + s * 512
                            psv = P.psum()
                            for kc in range(KC):
                                P.op("pe", lambda e: e.matmul(psv[:, 0:n], wvj[:, kc, ws], h2[:, kc, c0:c0 + n],
                                                              start=(kc == 0), stop=(kc == KC - 1)),
                                     reads=[wvj.t()] + h2.seg(kc, c0, c0 + n), writes=[psv.t()], sig=(kc == KC - 1))
                            P.op("act", lambda e: e.copy(vj[:, s * 512:s * 512 + n], psv[:, 0:n]), reads=[psv.t()],
                                 writes=[vj.t(s)])
                            psg = P.psum()
                            for kc in range(KC):
                                P.op("pe", lambda e: e.matmul(psg[:, 0:n], wgj[:, kc, ws], h2[:, kc, c0:c0 + n],
                                                              start=(kc == 0), stop=(kc == KC - 1)),
                                     reads=[wgj.t()] + h2.seg(kc, c0, c0 + n), writes=[psg.t()], sig=(kc == KC - 1))
                            P.op("act", lambda e: e.copy(gsb[:, c0:c0 + n], psg[:, 0:n]), reads=[psg.t()],
                                 writes=[gsb.t(s)])
                        psh = P.psum()
                        for kc in range(KC):
                            P.op("pe", lambda e: e.matmul(psh[:, 0:2], wgj[:, kc, ws], h2[:, kc, 0:tb + 2:tb + 1],
                                                          start=(kc == 0), stop=(kc == KC - 1)),
                                 reads=[wgj.t()] + h2.seg(kc, 0, 1) + h2.seg(kc, tb + 1, tb + 2), writes=[psh.t()],
                                 sig=(kc == KC - 1))
                        P.op("act", lambda e: e.copy(gsb[:, 0:tb + 2:tb + 1], psh[:, 0:2]), reads=[psh.t()],
                             writes=[gsb.t("h")])
                        cb = (i * FC + j) * 4
                        allg = [gsb.t(s) for s in range(nsub)] + [gsb.t("h")]
                        P.op("dve", lambda e: e.tensor_scalar(cvt[:, 0:tb], gsb[:, 0:tb], conv[:, cb:cb + 1], None, ALU.mult),
                             reads=allg + [conv.t()], writes=[cvt.t()])
                        P.op("dve", lambda e: e.scalar_tensor_tensor(cvt[:, 0:tb], gsb[:, 1:tb + 1], conv[:, cb + 1:cb + 2],
                                                                     cvt[:, 0:tb], ALU.mult, ALU.add),
                             reads=allg + [cvt.t()], writes=[cvt.t()])
                        P.op("dve", lambda e: e.scalar_tensor_tensor(cvt[:, 0:tb], gsb[:, 2:tb + 2], conv[:, cb + 2:cb + 3],
                                                                     cvt[:, 0:tb], ALU.mult, ALU.add),
                             reads=allg + [cvt.t()], writes=[cvt.t()])
                        P.op("act", lambda e: e.activation(out=cvt[:, 0:tb], in_=cvt[:, 0:tb], func=AF.Gelu,
                                                           bias=conv[:, cb + 3:cb + 4]),
                             reads=[cvt.t(), conv.t()], writes=[cvt.t()])
                        P.op("dve", lambda e: e.tensor_tensor(ajj[:, 0:tb], cvt[:, 0:tb], vj[:, 0:tb], ALU.mult),
                             reads=[cvt.t()] + [vj.t(s) for s in range(nsub)], writes=[ajj.t()])
                        P.dma("sp", ACTS[j, :, b0:b0 + tb], ajj[:, 0:tb], reads=[ajj.t()])
        TB = min(1024, ntok)
        with P.scope():
            act = P.sb("actT", [128, FC, TB], BF16)
            wd = [P.sb("wd", [128, FC, 512], BF16) for _ in range(2)]
            xo = [P.sb("xo", [128, 512], F32) for _ in range(2)]
            xn = [P.sb("xn", [128, 512], F32) for _ in range(2)]
            kk = 0
            nwd = 0
            for b0 in range(0, ntok, TB):
                tb = min(TB, ntok - b0)
                nsub = (tb + 511) // 512
                for jq in range(0, FC, 11):
                    je = min(jq + 11, FC)
                    P.dma("sp", act[:, jq:je, 0:tb], ACTS[jq:je, :, b0:b0 + tb].rearrange("j p n -> p j n"),
                          writes=[act.t(jq)])
                for ocg in range(D // 512):
                    wdo = wd[nwd % 2]
                    nwd += 1
                    P.dma("pool", wdo[:], w_down[i, :, ocg * 512:(ocg + 1) * 512].rearrange("(fc p) n -> p fc n", p=128),
                          writes=[wdo.t()])
                    for o4 in range(4):
                        oc = ocg * 4 + o4
                        for s in range(nsub):
                            n = min(512, tb - s * 512)
                            k = kk % 2
                            kk += 1
                            P.dma("sp", xo[k][:, 0:n], src[oc * 128:(oc + 1) * 128, b0 + s * 512:b0 + s * 512 + n],
                                  writes=[xo[k].t()])
                            ps = P.psum()
                            for j in range(FC):
                                P.op("pe", lambda e: e.matmul(ps[:, 0:n], wdo[:, j, o4 * 128:(o4 + 1) * 128], act[:, j, s * 512:s * 512 + n],
                                                              start=(j == 0), stop=(j == FC - 1)),
                                     reads=[wdo.t(), act.t((j // 11) * 11)], writes=[ps.t()], sig=(j == FC - 1))
                            P.op("dve", lambda e: e.scalar_tensor_tensor(xn[k][:, 0:n], ps[:, 0:n], mv(i, 5, oc, g),
                                                                         xo[k][:, 0:n], ALU.mult, ALU.add),
                                 reads=[ps.t(), xo[k].t(), modv.t()], writes=[xn[k].t()])
                            P.dma("act", dst[oc * 128:(oc + 1) * 128, b0 + s * 512:b0 + s * 512 + n], xn[k][:, 0:n],
                                  reads=[xn[k].t()])

    def st_copy(src, dst, ntok):
        with P.scope():
            xs = [P.sb("cp", [128, KC, 512], F32) for _ in range(2)]
            k = 0
            for t0 in range(0, ntok, 512):
                n = min(512, ntok - t0)
                P.dma("sp", xs[k][:, :, 0:n], src[:, t0:t0 + n].rearrange("(kc p) n -> p kc n", p=128), writes=[xs[k].t()])
                P.dma("sp", dst[:, t0:t0 + n].rearrange("(kc p) n -> p kc n", p=128), xs[k][:, :, 0:n], reads=[xs[k].t()])
                k ^= 1

    def st_final(src):
        with P.scope():
            xs = P.sb("xs", [128, KC, 512], F32)
            rs = P.sb("rs", [128, 512], F32)
            sqs = P.sb("sq", [128, KC, 512], BF16)
            ot = [P.sb("ot", [128, KC, 512], F32) for _ in range(2)]
            k = 0
            for t0 in range(0, SEQ, 512):
                n = min(512, SEQ - t0)
                P.dma("sp", xs[:, :, 0:n], src[:, t0:t0 + n].rearrange("(kc p) n -> p kc n", p=128), writes=[xs.t()])
                rms_rstd(xs, n, rs, sqs)
                o = ot[k]
                for kc in range(KC):
                    P.op("dve", lambda e: e.scalar_tensor_tensor(o[:, kc, 0:n], xs[:, kc, 0:n], fg[:, kc:kc + 1],
                                                                 rs[:, 0:n], ALU.mult, ALU.mult),
                         reads=[xs.t(), rs.t(), fg.t()], writes=[o.t(kc)])
                P.dma("sp", outT[:, t0:t0 + n].rearrange("(kc p) n -> p kc n", p=128), o[:, :, 0:n],
                      reads=[o.t(kc) for kc in range(KC)])
                k ^= 1

    def st_pool(i, src, dst, ntok, g, nxt=None):
        o = i // 2
        K4 = KC // 4
        TBP = 256
        icn = icnt_l if g == 0 else icnt_c
        with P.scope():
            hps = [P.sb("hp", [128, KC, TBP + 16], F32) for _ in range(2)]
            ics = [P.sb("ic", [128, 4, TBP], F32) for _ in range(2)]
            pl = P.sb("pl", [128, KC, TBP], BF16)
            WA = P.sb("WA", [128, KC, TBP + 16], F32)
            WB = P.sb("WB", [128, KC, TBP + 16], F32)
            nb = norm_bufs(128)
            wp = P.sb("wp", [128, 4, K4, c.PG], BF16)
            gp = P.sb("gp", [128, KC], F32)
            psc = P.sb("psc", [128, KC], F32)
            xo = [P.sb("xo", [128, TBP], F32) for _ in range(2)]
            xn = [P.sb("xn", [128, TBP], F32) for _ in range(2)]
            for gi in range(4):
                P.dma("pool", wp[:, gi, :, :], pool_w[o, gi].rearrange("(k p) n -> p k n", p=128), writes=[wp.t(gi)])
            P.dma("sp", psc[:], pool_scT[:, o * KC:(o + 1) * KC], writes=[psc.t()])
            b0c = mvcol(i, 2, 0, g)
            P.op("dve", lambda e: e.tensor_tensor(gp[:], modv[:, b0c:b0c + 2 * KC:2], psc[:], ALU.mult),
                 reads=[modv.t(), psc.t()], writes=[gp.t()])
            blks = list(range(0, ntok, TBP))
            kkc = [0]

            def norm_for(bi):
                b0 = blks[bi]
                hp, ic = hps[bi % 2], ics[bi % 2]
                tb = min(TBP, ntok - b0)
                lo, hi = b0 - 8, b0 + tb + 8
                a_, bnd = max(lo, 0), min(hi, ntok)
                if lo < 0:
                    P.op("pool", lambda e: e.memset(hp[:, :, 0:8], 0.0), writes=[t_ for kc in range(KC) for t_ in hp.seg(kc, 0, 8)])
                if hi > ntok:
                    P.op("pool", lambda e: e.memset(hp[:, :, tb + 8:tb + 16], 0.0),
                         writes=[t_ for kc in range(KC) for t_ in hp.seg(kc, tb + 8, tb + 16)])
                subs = [(t0, n, t0 - lo) for (t0, n) in _split(a_, bnd, 128)]
                norm_run(src, subs, hp, (lambda kc, c0, n: hp.seg(kc, c0, c0 + n)), i, 1, 0, g, nb)
                for w in range(4):
                    P.dma("sp", ic[:, w, 0:tb], icn[w:w + 1, b0:b0 + tb].partition_broadcast(128), writes=[ic.t(w)])

            def proc(bi):
                b0 = blks[bi]
                hp, ic = hps[bi % 2], ics[bi % 2]
                tb = min(TBP, ntok - b0)
                L = tb + 16
                hall = [t_ for kc in range(KC) for t_ in hp.seg(kc, 0, L)]
                P.op("dve", lambda e: e.tensor_tensor(WA[:, :, 1:L], hp[:, :, 0:L - 1], hp[:, :, 1:L], ALU.add),
                     reads=hall, writes=[WA.t()])
                P.op("dve", lambda e: e.tensor_tensor(WB[:, K4:KC, 2:L - 1], WA[:, K4:KC, 1:L - 2], WA[:, K4:KC, 3:L], ALU.add),
                     reads=[WA.t()], writes=[WB.t()])
                P.op("dve", lambda e: e.tensor_tensor(WA[:, 2 * K4:KC, 4:L - 3], WB[:, 2 * K4:KC, 2:L - 5], WB[:, 2 * K4:KC, 6:L - 1], ALU.add),
                     reads=[WB.t()], writes=[WA.t()])
                P.op("dve", lambda e: e.tensor_tensor(WB[:, 3 * K4:KC, 8:L - 7], WA[:, 3 * K4:KC, 4:L - 11], WA[:, 3 * K4:KC, 12:L - 3], ALU.add),
                     reads=[WA.t()], writes=[WB.t()])
                for gi in range(4):
                    cur = WA if gi % 2 == 0 else WB
                    en = "dve" if gi < 2 else "pool"
                    ka, kb = gi * K4, (gi + 1) * K4
                    icb = ic[:, gi, 0:tb].unsqueeze(1).to_broadcast([128, K4, tb])
                    P.op(en, lambda e: e.tensor_tensor(cur[:, ka:kb, 8:8 + tb], cur[:, ka:kb, 8:8 + tb], icb, ALU.mult),
                         reads=[WA.t(), WB.t(), ic.t(gi)], writes=[cur.t(("f", gi))])
                    P.op(en, lambda e: e.tensor_tensor(pl[:, ka:kb, 0:tb], cur[:, ka:kb, 8:8 + tb], hp[:, ka:kb, 8:8 + tb], ALU.subtract),
                         reads=[cur.t(("f", gi))] + hall, writes=[pl.t(gi)])
                for gi in range(4):
                    for oc in range(K4):
                        ocg = gi * K4 + oc
                        k = kkc[0] % 2
                        kkc[0] += 1
                        P.dma("sp", xo[k][:, 0:tb], src[ocg * 128:(ocg + 1) * 128, b0:b0 + tb], writes=[xo[k].t()])
                        ps = P.psum()
                        for k4 in range(K4):
                            P.op("pe", lambda e: e.matmul(ps[:, 0:tb], wp[:, gi, k4, oc * 128:(oc + 1) * 128],
                                                          pl[:, gi * K4 + k4, 0:tb], start=(k4 == 0), stop=(k4 == K4 - 1)),
                                 reads=[wp.t(gi), pl.t(gi)], writes=[ps.t()], sig=(k4 == K4 - 1))
                        P.op("dve", lambda e: e.scalar_tensor_tensor(xn[k][:, 0:tb], ps[:, 0:tb], gp[:, ocg:ocg + 1],
                                                                     xo[k][:, 0:tb], ALU.mult, ALU.add),
                             reads=[ps.t(), xo[k].t(), gp.t()], writes=[xn[k].t()])
                        P.dma("act", dst[ocg * 128:(ocg + 1) * 128, b0:b0 + tb], xn[k][:, 0:tb], reads=[xn[k].t()])

            norm_for(0)
            for bi in range(len(blks)):
                if bi + 1 < len(blks):
                    norm_for(bi + 1)
                proc(bi)

    NT = CTX + SEQ
    AW, BQK, BV, NAH, GH, DIN = c.AW, c.BQK, c.BV, c.NAH, c.GH, c.DIN
    NCC = CTX // 128
    NCH = NT // 128
    ROWS = c.ROWS
    NB64 = c.NBLK * 64
    QKA = dscr("QKA", [2 * AW, NT], BF16)
    VA = dscr("VA", [NT, AW], BF16)
    QKB = dscr("QKB", [2 * BQK, NT], BF16)
    VBm = dscr("VBm", [NT, BV], BF16)
    GBs = dscr("GBs", [BV, NT], BF16)
    ABT = dscr("ABT", [32, NT], F32)
    OT = dscr("OT", [D, NT], BF16)
    segs = [("qa", 0, AW), ("ka", AW, 2 * AW), ("va", 2 * AW, 3 * AW), ("qb", 3 * AW, 3 * AW + BQK),
            ("kb", 3 * AW + BQK, 3 * AW + 2 * BQK), ("vb", 3 * AW + 2 * BQK, 3 * AW + 2 * BQK + BV),
            ("gb", 3 * AW + 2 * BQK + BV, 3 * AW + 2 * BQK + 2 * BV), ("ab", DIN - 32, DIN)]

    def ctype(col):
        for (nm, a, b) in segs:
            if a <= col < b:
                return nm, col - a
        raise ValueError

    def e_proj(i, e_, src, tok0, ntok, g):
        TB = min(c.TBE, ntok)
        with P.scope():
            hTs = [P.sb("hT", [128, KC, TB], BF16) for _ in range(2)]
            nb = norm_bufs(256)
            wt = [P.sb("wi", [128, KC, 512], BF16) for _ in range(2)]
            st = [P.sb("st", [128, 512], BF16) for _ in range(3)]
            stf = [P.sb("stf", [32, 512], F32) for _ in range(2)]
            nst = 0
            blocks = list(range(0, ntok, TB))

            def mk_norm(bi):
                b0_ = blocks[bi]
                tb_ = min(TB, ntok - b0_)
                hb = hTs[bi % 2]
                subs_ = [(t0, min(256, tb_ - t0)) for t0 in range(0, tb_, 256)]
                return norm_gen(src, [(b0_ + t0, n, t0) for (t0, n) in subs_], hb, (lambda kc, c0, n: [hb.t((c0, kc))]), i, 1, 0, g, nb)

            for _ in mk_norm(0):
                pass
            for bi, b0 in enumerate(blocks):
                tb = min(TB, ntok - b0)
                hT = hTs[bi % 2]
                subs = [(t0, min(256, tb - t0)) for t0 in range(0, tb, 256)]
                nxt_norm = mk_norm(bi + 1) if bi + 1 < len(blocks) else None

                def hk(kc, ca, cb):
                    return [hT.t((t0, kc)) for (t0, n) in subs if t0 < cb and t0 + n > ca]

                ng_ = (DIN + 511) // 512
                for cg in range(ng_):
                    if nxt_norm is not None and cg % 3 == 2:
                        try:
                            next(nxt_norm)
                        except StopIteration:
                            nxt_norm = None
                    w = wt[cg % 2]
                    ncol = min(512, DIN - cg * 512)
                    P.dma("pool", w[:, :, 0:ncol], w_in[e_, :, cg * 512:cg * 512 + ncol].rearrange("(kc p) n -> p kc n", p=128),
                          writes=[w.t()])
                    j = 0
                    while j * 128 < ncol:
                        col = cg * 512 + j * 128
                        nm, off = ctype(col)
                        if nm in ("va", "vb"):
                            j2 = j
                            while (j2 + 1) * 128 < ncol and ctype(cg * 512 + (j2 + 1) * 128)[0] == nm:
                                j2 += 1
                            nr = (j2 - j + 1) * 128
                            dstT = VA if nm == "va" else VBm
                            for tc in range(0, tb, 128):
                                ps = P.psum()
                                for kc in range(KC):
                                    P.op("pe", lambda e: e.matmul(ps[:, 0:nr], hT[:, kc, tc:tc + 128], w[:, kc, j * 128:j * 128 + nr],
                                                                  start=(kc == 0), stop=(kc == KC - 1)),
                                         reads=[w.t()] + hk(kc, tc, tc + 128), writes=[ps.t()], sig=(kc == KC - 1))
                                s_ = st[nst % 3]
                                nst += 1
                                P.op("act", lambda e: e.copy(s_[:, 0:nr], ps[:, 0:nr]), reads=[ps.t()], writes=[s_.t()])
                                P.dma("sp", dstT[tok0 + b0 + tc:tok0 + b0 + tc + 128, off:off + nr], s_[:, 0:nr], reads=[s_.t()])
                            j = j2 + 1
                            continue
                        m = 32 if nm == "ab" else 128
                        for s0 in range(0, tb, 512):
                            n = min(512, tb - s0)
                            ps = P.psum()
                            for kc in range(KC):
                                P.op("pe", lambda e: e.matmul(ps[0:m, 0:n], w[:, kc, j * 128:j * 128 + m], hT[:, kc, s0:s0 + n],
                                                              start=(kc == 0), stop=(kc == KC - 1)),
                                     reads=[w.t()] + hk(kc, s0, s0 + n), writes=[ps.t()], sig=(kc == KC - 1))
                            tcol = tok0 + b0 + s0
                            if nm == "ab":
                                s_ = stf[nst % 2]
                                nst += 1
                                P.op("act", lambda e: e.copy(s_[:, 0:n], ps[0:32, 0:n]), reads=[ps.t()], writes=[s_.t()])
                                P.dma("sp", ABT[:, tcol:tcol + n], s_[:, 0:n], reads=[s_.t()])
                            else:
                                s_ = st[nst % 3]
                                nst += 1
                                if nm == "qb":
                                    P.op("act", lambda e: e.mul(s_[:, 0:n], ps[:, 0:n], 0.125), reads=[ps.t()], writes=[s_.t()])
                                elif nm == "gb":
                                    P.op("act", lambda e: e.activation(out=s_[:, 0:n], in_=ps[:, 0:n], func=AF.Silu),
                                         reads=[ps.t()], writes=[s_.t()])
                                else:
                                    P.op("act", lambda e: e.copy(s_[:, 0:n], ps[:, 0:n]), reads=[ps.t()], writes=[s_.t()])
                                if nm in ("qa", "ka"):
                                    r0 = off + (AW if nm == "ka" else 0)
                                    dd = QKA[r0:r0 + 128, tcol:tcol + n]
                                elif nm in ("qb", "kb"):
                                    r0 = off + (BQK if nm == "kb" else 0)
                                    dd = QKB[r0:r0 + 128, tcol:tcol + n]
                                else:
                                    dd = GBs[off:off + 128, tcol:tcol + n]
                                P.dma("sp", dd, s_[:, 0:n], reads=[s_.t()])
                        j += 1

                if nxt_norm is not None:
                    for _ in nxt_norm:
                        pass

    def e_na(e_, need_ctx, nxt=None):
        scale = 128 ** -0.5
        with P.scope():
            P.nrot = 3
            kT = [P.sb("kT", [128, NT], BF16) for _ in range(2)]
            qT = [P.sb("qT", [128, NT], BF16) for _ in range(2)]
            V = [P.sb("V", [128, NCH, 128], BF16) for _ in range(2)]
            TF = [P.sb("TF", [128, NB64], F32) for _ in range(2)]
            TZ = [P.sb("TZ", [128, NB64], F32) for _ in range(2)]
            mF = P.sb("mF", [128, NB64], F32)
            mZ = P.sb("mZ", [128, NB64], F32)
            ones_bf = P.sb("ones_bf", [128, 128], BF16)
            ex = [P.sb("ex", [128, 512], F32) for _ in range(2)]
            pT = [P.sb("pT", [128, 512], BF16) for _ in range(4)]
            rd = [P.sb("rd", [128, 512], F32) for _ in range(2)]
            oT = [P.sb("oT", [128, 512], BF16) for _ in range(2)]
            P.dma("sp", mF[:], nmask[0], writes=[mF.t()])
            P.dma("sp", mZ[:], nmask[1], writes=[mZ.t()])
            P.dma("pool", ones_bf[:], cmat[2], writes=[ones_bf.t()])
            cnt = 0
            ntile = 0
            mst = Stepper(modvec_gen(nxt, 512, 3), NAH, 6 * D // 512) if nxt is not None else None
            for h in range(NAH):
                if mst:
                    mst.step()
                k_, q_, v_, tf, tz = kT[h % 2], qT[h % 2], V[h % 2], TF[h % 2], TZ[h % 2]
                P.dma("sp", k_[:], QKA[AW + h * 128:AW + (h + 1) * 128, :], writes=[k_.t()])
                P.dma("sp", q_[:], QKA[h * 128:(h + 1) * 128, :], writes=[q_.t()])
                P.dma("sp", v_[:], VA[:, h * 128:(h + 1) * 128].rearrange("(ch p) d -> p ch d", p=128), writes=[v_.t()])
                P.dma("sp", tf[:], rpbx[e_, h], writes=[tf.t()])
                P.op("act", lambda e: e.activation(out=tf[:], in_=tf[:], func=AF.Exp), reads=[tf.t()], writes=[tf.t()])
                P.op("dve", lambda e: e.tensor_tensor(tz[:], tf[:], mZ[:], ALU.mult), reads=[tf.t(), mZ.t()], writes=[tz.t()])
                P.op("dve", lambda e: e.tensor_tensor(tf[:], tf[:], mF[:], ALU.mult), reads=[tf.t(), mF.t()], writes=[tf.t()])
                tiles = [("lat", qt) for qt in range(ROWS // 8)] + ([("ctx", 0)] if need_ctx else [])
                items = []
                for (kind, qt) in tiles:
                    if kind == "lat":
                        r0 = qt * 8
                        qc0, nq = CTX + qt * 512, 512
                        c_lo, c_hi = max(0, (r0 - 4) // 2), min(ROWS // 2 - 1, (r0 + 10) // 2)
                        chunks = [("c", cc) for cc in range(NCC)] + [("l", cc) for cc in range(c_lo, c_hi + 1)]
                    else:
                        r0 = 0
                        qc0, nq = 0, CTX
                        chunks = [("c", cc) for cc in range(NCC)]
                    acc = (P.ps[3 + 2 * (ntile % 2)], P.ps[4 + 2 * (ntile % 2)], ntile)
                    ntile += 1
                    for idx, (ck, cc) in enumerate(chunks):
                        items.append(dict(r0=r0, qc0=qc0, nq=nq, ck=ck, cc=cc, first=(idx == 0), last=(idx == len(chunks) - 1),
                                          acc=acc))

                def emit_s(it):
                    ck, cc, nq, qc0, r0 = it["ck"], it["cc"], it["nq"], it["qc0"], it["r0"]
                    kc0 = cc * 128 if ck == "c" else CTX + cc * 128
                    ps_s = P.psum()
                    P.op("pe", lambda e: e.matmul(ps_s[:, 0:nq], k_[:, kc0:kc0 + 128], q_[:, qc0:qc0 + nq], start=True, stop=True),
                         reads=[k_.t(), q_.t()], writes=[ps_s.t()])
                    p_ = pT[it["n"] % 4]
                    it["p"] = p_
                    if ck == "c":
                        P.op("act", lambda e: e.activation(out=p_[:, 0:nq], in_=ps_s[:, 0:nq], func=AF.Exp, scale=scale),
                             reads=[ps_s.t()], writes=[p_.t()])
                    else:
                        x_ = ex[it["n"] % 2]
                        P.op("act", lambda e: e.activation(out=x_[:, 0:nq], in_=ps_s[:, 0:nq], func=AF.Exp, scale=scale),
                             reads=[ps_s.t()], writes=[x_.t()])
                        bb = r0 - 2 * cc + 11
                        fr = None
                        if r0 == 0 and cc <= 3:
                            fr = (0, 256)
                        if r0 == ROWS - 8 and cc >= ROWS // 2 - 4:
                            fr = (256, 512)
                        rngs = [(0, 512, tz)] if fr is None else [(fr[0], fr[1], tf), (256 - fr[0], 512 - fr[0], tz)]
                        for (ca, cb, tab) in rngs:
                            P.op("dve", lambda e: e.tensor_tensor(p_[:, ca:cb], x_[:, ca:cb], tab[:, bb * 64 + ca:bb * 64 + cb], ALU.mult),
                                 reads=[x_.t(), tab.t()], writes=[p_.t()])

                def emit_pv(it):
                    ps_o, ps_d, nt = it["acc"]
                    nq, qc0, p_ = it["nq"], it["qc0"], it["p"]
                    chi = it["cc"] if it["ck"] == "c" else NCC + it["cc"]
                    P.op("pe", lambda e: e.matmul(ps_o[:, 0:nq], v_[:, chi, :], p_[:, 0:nq], start=it["first"], stop=it["last"]),
                         reads=[v_.t(), p_.t()], writes=[ps_o.t()], sig=True)
                    P.op("pe", lambda e: e.matmul(ps_d[:, 0:nq], ones_bf[:], p_[:, 0:nq], start=it["first"], stop=it["last"]),
                         reads=[ones_bf.t(), p_.t()], writes=[ps_d.t()], sig=True)
                    if it["last"]:
                        r_, o_ = rd[nt % 2], oT[nt % 2]
                        P.op("dve", lambda e: e.reciprocal(r_[:, 0:nq], ps_d[:, 0:nq]), reads=[ps_d.t()], writes=[r_.t()])
                        P.op("dve", lambda e: e.tensor_tensor(o_[:, 0:nq], ps_o[:, 0:nq], r_[:, 0:nq], ALU.mult),
                             reads=[ps_o.t(), r_.t()], writes=[o_.t()])
                        P.dma("sp", OT[h * 128:(h + 1) * 128, qc0:qc0 + nq], o_[:, 0:nq], reads=[o_.t()])

                LA = 2
                for n_, it in enumerate(items):
                    it["n"] = cnt + n_
                for n_ in range(len(items) + LA):
                    if n_ < len(items):
                        emit_s(items[n_])
                    if n_ >= LA:
                        emit_pv(items[n_ - LA])
                cnt += len(items)
            if mst:
                mst.finish()
            P.nrot = 7

    def e_gla(e_, need_ctx):
        NFC = BQK // 128
        with P.scope():
            Cs = [P.sb("ropeC", [128, 512], F32) for _ in range(2)]
            Ss = [P.sb("ropeS", [128, 512], F32) for _ in range(2)]
            permb = P.sb("permb", [128, 128], BF16)
            identb = P.sb("identb", [128, 128], BF16)
            ones128 = P.sb("ones128", [128, 128], F32)
            onesr = P.sb("onesr", [1, 128], F32)
            tri = [P.sb("tri", [128, 128], F32) for _ in range(2)]
            wg2s = P.sb("wg2s", [32, 2 * BQK], F32)
            bgs = P.sb("bgs", [1, 2 * BQK], F32)
            abT = P.sb("abT", [32, NT], F32)
            gg = P.sb("gg", [128, GH], F32)
            qr = P.sb("qr", [128, NT], BF16)
            kr = P.sb("kr", [128, NT], BF16)
            qtil = P.sb("qtil", [128, NT], BF16)
            ktil = P.sb("ktil", [128, NT], BF16)
            khat = P.sb("khat", [128, NCH, 128], BF16)
            vv = P.sb("vv", [128, NCH, 256], BF16)
            dec = P.sb("dec", [128, NCH], F32)
            ob = [P.sb("ob", [128, NT], F32) for _ in range(2)]
            St = P.sb("St", [128, 128], F32)
            t1 = [P.sb("t1", [128, 512], F32) for _ in range(2)]
            t2 = [P.sb("t2", [128, 512], F32) for _ in range(2)]
            kh = [P.sb("kh", [128, 128], BF16) for _ in range(4)]
            xk = [P.sb("xk", [128, 128], F32) for _ in range(4)]
            sg = [P.sb("sg", [128, 512], BF16) for _ in range(2)]
            yo = [P.sb("yo", [128, 512], BF16) for _ in range(2)]
            P.dma("pool", permb[:], cmat[4], writes=[permb.t()])
            P.dma("pool", identb[:], cmat[3], writes=[identb.t()])
            P.dma("sp", ones128[:], cmat[1], writes=[ones128.t()])
            P.dma("sp", onesr[:], cmat[2][0:1, :], writes=[onesr.t()])
            P.dma("sp", tri[0][:], cmat[5], writes=[tri[0].t()])
            P.dma("sp", tri[1][:], cmat[6], writes=[tri[1].t()])
            P.dma("sp", wg2s[:], wg2[e_], writes=[wg2s.t()])
            P.dma("sp", bgs[:], bg[e_], writes=[bgs.t()])
            P.dma("sp", abT[:], ABT, writes=[abT.t()])
            P.dma("sp", gg[:], gla_gT[:, e_ * GH:(e_ + 1) * GH], writes=[gg.t()])
            k2 = 0
            for fc in range(NFC):
                P.dma("sp", qr[:], QKB[fc * 128:(fc + 1) * 128, :], writes=[qr.t()])
                P.dma("sp", kr[:], QKB[BQK + fc * 128:BQK + (fc + 1) * 128, :], writes=[kr.t()])
                P.dma("sp", vv[:], VBm[:, fc * 256:(fc + 1) * 256].rearrange("(ch p) d -> p ch d", p=128), writes=[vv.t()])
                for buf in (qr, kr):
                    for s0 in range(0, SEQ, 512):
                        a, b = CTX + s0, CTX + s0 + 512
                        ps = P.psum()
                        P.op("pe", lambda e: e.matmul(ps[:, 0:512], permb[:], buf[:, a:b], start=True, stop=True),
                             reads=[permb.t(), buf.t()], writes=[ps.t()])
                        x1, x2 = t1[k2 % 2], t2[k2 % 2]
                        C_, S_ = Cs[k2 % 2], Ss[k2 % 2]
                        k2 += 1
                        P.dma("sp", C_[:], rope[0][:, s0:s0 + 512], writes=[C_.t()])
                        P.dma("sp", S_[:], rope[1][:, s0:s0 + 512], writes=[S_.t()])
                        P.op("dve", lambda e: e.tensor_tensor(x1[:], ps[:, 0:512], S_[:], ALU.mult),
                             reads=[ps.t(), S_.t()], writes=[x1.t()])
                        P.op("pool", lambda e: e.tensor_tensor(x2[:], buf[:, a:b], C_[:], ALU.mult),
                             reads=[buf.t(), C_.t()], writes=[x2.t()])
                        P.op("dve", lambda e: e.tensor_tensor(buf[:, a:b], x1[:], x2[:], ALU.add),
                             reads=[x1.t(), x2.t(), ps.t()], writes=[buf.t()])
                for d in range(2):
                  with P.scope():
                    sp_ = P.sb("sp", [128, NCH, 128], F32)
                    cum = P.sb("cum", [128, NT], F32)
                    zc = d * BQK + fc * 128
                    for ch in range(NCH):
                        ps = P.psum()
                        P.op("pe", lambda e: e.matmul(ps[:, 0:128], abT[:, ch * 128:(ch + 1) * 128], wg2s[:, zc:zc + 128], start=True, stop=False),
                             reads=[abT.t(), wg2s.t()], writes=[ps.t()], sig=False)
                        P.op("pe", lambda e: e.matmul(ps[:, 0:128], onesr[:], bgs[:, zc:zc + 128], start=False, stop=True),
                             reads=[onesr.t(), bgs.t()], writes=[ps.t()])
                        P.op("act", lambda e: e.activation(out=sp_[:, ch, :], in_=ps[:, 0:128], func=AF.Exp, scale=-1.0),
                             reads=[ps.t()], writes=[sp_.t(ch)])
                    for ch in range(NCH):
                        P.op("act", lambda e: e.activation(out=sp_[:, ch, :], in_=sp_[:, ch, :], func=AF.Ln, bias=1.0),
                             reads=[sp_.t(ch)], writes=[sp_.t(ch)])
                    for c4 in range(0, NCH, 4):
                        nn = min(4, NCH - c4)
                        ps = P.psum()
                        for u in range(nn):
                            P.op("pe", lambda e: e.matmul(ps[:, u * 128:(u + 1) * 128], sp_[:, c4 + u, :], tri[d][:], start=True, stop=True),
                                 reads=[sp_.t(c4 + u), tri[d].t()], writes=[ps.t()])
                        P.op("dve", lambda e: e.tensor_scalar(cum[:, c4 * 128:(c4 + nn) * 128], ps[:, 0:nn * 128], -1.0 / 16, None, ALU.mult),
                             reads=[ps.t()], writes=[cum.t(c4)])
                    allcum = [cum.t(c4) for c4 in range(0, NCH, 4)]
                    lastcol = 127 if d == 0 else 0
                    P.op("act", lambda e: e.activation(out=dec[:], in_=cum[:, lastcol:NT:128], func=AF.Exp),
                         reads=allcum, writes=[dec.t()])
                    for s0 in range(0, NT, 512):
                        n = min(512, NT - s0)
                        x1, x2 = t1[k2 % 2], t2[k2 % 2]
                        k2 += 1
                        P.op("act", lambda e: e.activation(out=x1[:, 0:n], in_=cum[:, s0:s0 + n], func=AF.Exp),
                             reads=allcum, writes=[x1.t()])
                        P.op("dve", lambda e: e.tensor_tensor(qtil[:, s0:s0 + n], qr[:, s0:s0 + n], x1[:, 0:n], ALU.mult),
                             reads=[qr.t(), x1.t()], writes=[qtil.t()])
                        P.op("act", lambda e: e.activation(out=x2[:, 0:n], in_=cum[:, s0:s0 + n], func=AF.Exp, scale=-1.0),
                             reads=allcum, writes=[x2.t()])
                        P.op("pool", lambda e: e.tensor_tensor(ktil[:, s0:s0 + n], kr[:, s0:s0 + n], x2[:, 0:n], ALU.mult),
                             reads=[kr.t(), x2.t()], writes=[ktil.t()])
                    for ch in range(NCH):
                        x1 = t1[k2 % 2]
                        khb = kh[k2 % 2]
                        k2 += 1
                        lc = ch * 128 + lastcol
                        P.op("act", lambda e: e.activation(out=x1[:, 0:128], in_=cum[:, ch * 128:(ch + 1) * 128], func=AF.Exp,
                                                           scale=-1.0, bias=cum[:, lc:lc + 1]),
                             reads=allcum, writes=[x1.t()])
                        P.op("dve", lambda e: e.tensor_tensor(khb[:], kr[:, ch * 128:(ch + 1) * 128], x1[:, 0:128], ALU.mult),
                             reads=[kr.t(), x1.t()], writes=[khb.t()])
                        P.op("pe", lambda e: e.transpose(P.psb[:, (ch % 8) * 128:(ch % 8 + 1) * 128], khb[:], identb[:]),
                             reads=[khb.t(), identb.t()], writes=[P.psb.t(ch % 8)])
                        P.op("act", lambda e: e.copy(khat[:, ch, :], P.psb[:, (ch % 8) * 128:(ch % 8 + 1) * 128]),
                             reads=[P.psb.t(ch % 8)], writes=[khat.t(ch)])
                  with P.scope():
                    aTa = P.sb("aTa", [128, NCH, 2, 128], BF16)
                    Sba = P.sb("Sba", [128, NCH, 128], BF16)
                    P.op("dve", lambda e: e.memset(St[:], 0.0), writes=[St.t()])
                    if d == 0:
                        order = list(range(NCH))
                    else:
                        order = list(range(NCC - 1, -1, -1)) + list(range(NCH - 1, NCC - 1, -1))
                    for ch in order:
                        a, b = ch * 128, (ch + 1) * 128
                        P.op("dve", lambda e: e.tensor_copy(Sba[:, ch, :], St[:]), reads=[St.t()], writes=[Sba.t(ch)])
                        ps_kv = P.psum()
                        P.op("pe", lambda e: e.matmul(ps_kv[:, 0:256], khat[:, ch, :], vv[:, ch, :], start=True, stop=True),
                             reads=[khat.t(ch), vv.t()], writes=[ps_kv.t()])
                        for hh in range(2):
                            pa, pb = hh * 64, (hh + 1) * 64
                            ps_a = P.psum()
                            P.op("pe", lambda e: e.matmul(ps_a[:, 0:128], ktil[pa:pb, a:b], qtil[pa:pb, a:b], start=True, stop=True),
                                 reads=[ktil.t(), qtil.t()], writes=[ps_a.t()])
                            P.op("dve", lambda e: e.tensor_tensor(aTa[:, ch, hh, :], ps_a[:, 0:128], tri[d][:], ALU.mult),
                                 reads=[ps_a.t(), tri[d].t()], writes=[aTa.t((ch, hh))])
                        for hh in range(2):
                            pa, pb = hh * 64, (hh + 1) * 64
                            P.op("dve", lambda e: e.scalar_tensor_tensor(St[pa:pb, :], St[pa:pb, :], dec[pa:pb, ch:ch + 1],
                                                                         ps_kv[pa:pb, hh * 128:(hh + 1) * 128], ALU.mult, ALU.add),
                                 reads=[St.t(), dec.t(), ps_kv.t()], writes=[St.t()])
                    for ch in order:
                        a, b = ch * 128, (ch + 1) * 128
                        for hh in range(2):
                            pa, pb = hh * 64, (hh + 1) * 64
                            ps_o = P.psum()
                            P.op("pe", lambda e: e.matmul(ps_o[:, 0:128], Sba[pa:pb, ch, :], qtil[pa:pb, a:b], start=True, stop=False),
                                 reads=[Sba.t(ch), qtil.t()], writes=[ps_o.t()], sig=False)
                            P.op("pe", lambda e: e.matmul(ps_o[:, 0:128], vv[:, ch, hh * 128:(hh + 1) * 128], aTa[:, ch, hh, :], start=False, stop=True),
                                 reads=[vv.t(), aTa.t((ch, hh))], writes=[ps_o.t()])
                            if d == 0:
                                P.op("act", lambda e: e.copy(ob[hh][:, a:b], ps_o[:, 0:128]), reads=[ps_o.t()], writes=[ob[hh].t(ch)])
                            else:
                                P.op("dve", lambda e: e.tensor_tensor(ob[hh][:, a:b], ob[hh][:, a:b], ps_o[:, 0:128], ALU.add),
                                     reads=[ps_o.t(), ob[hh].t(ch)], writes=[ob[hh].t(ch)])
                tiles_ = []
                for hh in range(2):
                    t_lo = 0 if need_ctx else CTX
                    for s0 in range(t_lo, NT, 512):
                        tiles_.append((hh, s0, min(512, NT - s0)))

                def post_a(k):
                    hh, s0, n = tiles_[k]
                    hg = fc * 2 + hh
                    chs = [ob[hh].t(ch) for ch in range(s0 // 128, (s0 + n) // 128)]
                    x1, x2, s_ = t1[k % 2], t2[k % 2], sg[k % 2]
                    P.dma("sp", s_[:, 0:n], GBs[hg * 128:(hg + 1) * 128, s0:s0 + n], writes=[s_.t()])
                    P.op("dve", lambda e: e.tensor_tensor(x1[:, 0:n], ob[hh][:, s0:s0 + n], ob[hh][:, s0:s0 + n], ALU.mult),
                         reads=chs, writes=[x1.t()])
                    ps = P.psum()
                    P.op("pe", lambda e: e.matmul(ps[:, 0:n], ones128[:], x1[:, 0:n], start=True, stop=True),
                         reads=[ones128.t(), x1.t()], writes=[ps.t()])
                    P.op("act", lambda e: e.activation(out=x2[:, 0:n], in_=ps[:, 0:n], func=AF.Ln, bias=EPS),
                         reads=[ps.t()], writes=[x2.t()])
                    P.op("act", lambda e: e.activation(out=x2[:, 0:n], in_=x2[:, 0:n], func=AF.Exp, scale=-0.5),
                         reads=[x2.t()], writes=[x2.t()])

                def post_b(k):
                    hh, s0, n = tiles_[k]
                    hg = fc * 2 + hh
                    chs = [ob[hh].t(ch) for ch in range(s0 // 128, (s0 + n) // 128)]
                    x1, x2, s_, y_ = t1[k % 2], t2[k % 2], sg[k % 2], yo[k % 2]
                    P.op("dve", lambda e: e.tensor_tensor(x1[:, 0:n], ob[hh][:, s0:s0 + n], x2[:, 0:n], ALU.mult),
                         reads=chs + [x2.t(), x1.t()], writes=[x1.t()])
                    P.op("dve", lambda e: e.scalar_tensor_tensor(y_[:, 0:n], x1[:, 0:n], gg[:, hg:hg + 1], s_[:, 0:n], ALU.mult, ALU.mult),
                         reads=[x1.t(), gg.t(), s_.t()], writes=[y_.t()])
                    P.dma("sp", OT[AW + hg * 128:AW + (hg + 1) * 128, s0:s0 + n], y_[:, 0:n], reads=[y_.t()])

                post_a(0)
                for k in range(len(tiles_)):
                    if k + 1 < len(tiles_):
                        post_a(k + 1)
                    post_b(k)

    def e_out(i, e_, src, dst, tok0, ntok, g):
        TB = min(1024, ntok)
        with P.scope():
            obs = [P.sb("otb", [128, KC, TB], BF16) for _ in range(2)]
            wt = [P.sb("wo", [128, KC, 512], BF16) for _ in range(2)]
            xo = [P.sb("xo", [128, 512], F32) for _ in range(2)]
            xn = [P.sb("xn", [128, 512], F32) for _ in range(2)]
            kk = 0
            blks = list(range(0, ntok, TB))

            def ld(bi):
                b0_ = blks[bi]
                tb_ = min(TB, ntok - b0_)
                P.dma("sp", obs[bi % 2][:, :, 0:tb_], OT[:, tok0 + b0_:tok0 + b0_ + tb_].rearrange("(kc p) n -> p kc n", p=128),
                      writes=[obs[bi % 2].t()])
            ld(0)
            for bi, b0 in enumerate(blks):
                tb = min(TB, ntok - b0)
                ob_ = obs[bi % 2]
                if bi + 1 < len(blks):
                    ld(bi + 1)
                for cg in range(D // 512):
                    w = wt[cg % 2]
                    P.dma("pool", w[:], w_out[e_, :, cg * 512:(cg + 1) * 512].rearrange("(kc p) n -> p kc n", p=128), writes=[w.t()])
                    for j in range(4):
                        oc = cg * 4 + j
                        for s0 in range(0, tb, 512):
                            n = min(512, tb - s0)
                            k = kk % 2
                            kk += 1
                            P.dma("sp", xo[k][:, 0:n], src[oc * 128:(oc + 1) * 128, b0 + s0:b0 + s0 + n], writes=[xo[k].t()])
                            ps = P.psum()
                            for kc in range(KC):
                                P.op("pe", lambda e: e.matmul(ps[:, 0:n], w[:, kc, j * 128:(j + 1) * 128], ob_[:, kc, s0:s0 + n],
                                                              start=(kc == 0), stop=(kc == KC - 1)),
                                     reads=[w.t(), ob_.t()], writes=[ps.t()], sig=(kc == KC - 1))
                            P.op("dve", lambda e: e.scalar_tensor_tensor(xn[k][:, 0:n], ps[:, 0:n], mv(i, 2, oc, g), xo[k][:, 0:n], ALU.mult, ALU.add),
                                 reads=[ps.t(), xo[k].t(), modv.t()], writes=[xn[k].t()])
                            P.dma("act", dst[oc * 128:(oc + 1) * 128, b0 + s0:b0 + s0 + n], xn[k][:, 0:n], reads=[xn[k].t()])

    def st_even(i, xsrc, csrc, xdst, cdst, need_ctx, nxt):
        e_ = i // 2
        e_proj(i, e_, csrc, 0, CTX, 1)
        e_proj(i, e_, xsrc, CTX, SEQ, 0)
        e_na(e_, need_ctx, nxt)
        e_gla(e_, need_ctx)
        e_out(i, e_, xsrc, xdst, CTX, SEQ, 0)
        if need_ctx:
            e_out(i, e_, csrc, cdst, 0, CTX, 1)

    st_modvec(0)
    xsrc, csrc = xT, cT
    for i in range(DEPTH):
        is_even = i % 2 == 0
        need_ctx = any(j % 2 == 0 for j in range(i + 1, DEPTH))
        nxt = i + 1 if i + 1 < DEPTH else None
        used = False
        if is_even and fl["even"]:
            st_even(i, xsrc, csrc, X[1], C[1], need_ctx, nxt)
            used = True
        elif (not is_even) and fl["odd"]:
            st_pool(i, xsrc, X[1], SEQ, 0, None)
            if need_ctx:
                st_pool(i, csrc, C[1], CTX, 1, None)
        else:
            st_copy(xsrc, X[1], SEQ)
            if need_ctx:
                st_copy(csrc, C[1], CTX)
        if nxt is not None and not used:
            st_modvec(nxt)
        if fl["ffn"]:
            st_ffn(i, X[1], X[0], SEQ, 0)
            if need_ctx:
                st_ffn(i, C[1], C[0], CTX, 1)
        else:
            st_copy(X[1], X[0], SEQ)
            if need_ctx:
                st_copy(C[1], C[0], CTX)
        xsrc, csrc = X[0], C[0]
    st_final(xsrc)
    P.barrier()
    return nc


def _fm(v, KC):
    return np.ascontiguousarray(np.asarray(v, np.float32).reshape(KC, 128).T)


def prep_shared(cfg, inp):
    c = cfg
    KC, FC, DEPTH = c.KC, c.FC, c.DEPTH
    f = lambda a: np.ascontiguousarray(np.asarray(a, np.float32))
    sh = {}
    sh["w_mod"] = f(inp["w_mod"])
    sh["b_modT"] = f(np.stack([_fm(inp["b_mod"][i], 6 * KC) for i in range(DEPTH)], 1).reshape(128, -1))
    ngs = np.stack([np.stack([_fm(inp["norm1_g"][i], KC), _fm(inp["norm2_g"][i], KC)], 1) for i in range(DEPTH)], 1)
    sh["ngT"] = f(ngs.reshape(128, -1))
    sh["fgT"] = _fm(inp["final_g"], KC)
    sh["w_up"] = f(inp["w_up"])
    sh["w_down"] = f(inp["w_down"])
    cw = np.asarray(inp["conv_w"], np.float32)
    cb = np.asarray(inp["conv_b"], np.float32)
    cvt = np.zeros((128, DEPTH, FC, 4), np.float32)
    for i in range(DEPTH):
        for k in range(3):
            cvt[:, i, :, k] = _fm(cw[i, k], FC)
        cvt[:, i, :, 3] = _fm(cb[i], FC)
    sh["convT"] = f(cvt.reshape(128, -1))
    cm = np.zeros((8, 128, 128), np.float32)
    cm[0] = 1.0 / c.D
    cm[1] = 1.0 / 128
    cm[2] = 1.0
    cm[3] = np.eye(128)
    p = np.arange(128)
    perm = np.where((p % 32) < 16, p + 16, p - 16)
    cm[4][perm, p] = 1.0
    jj, ii = np.meshgrid(p, p, indexing="ij")
    cm[5] = np.where(jj <= ii, 1.0, 0.0)
    cm[6] = np.where(jj >= ii, 1.0, 0.0)
    sh["cmat"] = cm
    sh["pool_w"] = f(inp["pool_w"])
    NE, BQK, NAH, NBLK = c.NE, c.BQK, c.NAH, c.NBLK
    sh["w_in"] = f(inp["w_in"])
    sh["w_out"] = f(inp["w_out"])
    wg = np.zeros((NE, 32, 2 * BQK), np.float32)
    w2 = np.asarray(inp["w_gate2"], np.float32)
    for d in range(2):
        wg[:, d * 16:(d + 1) * 16, d * BQK:(d + 1) * BQK] = w2[:, d]
    sh["wg2"] = wg
    sh["bg"] = f(np.asarray(inp["b_gate"], np.float32).reshape(NE, 1, 2 * BQK))
    rp = np.asarray(inp["rpb"], np.float32)
    pp = np.arange(128)
    half = pp // 64
    kcol = pp % 64
    bb = np.arange(NBLK)
    qcol = np.arange(64)
    e_idx = bb[None, :] - 4 - half[:, None]
    ev = (e_idx >= 0) & (e_idx <= 14)
    dr = np.clip(14 - e_idx, 0, 14)
    dc = np.clip(kcol[:, None] - qcol[None, :] + 15, 0, 30)
    rx = rp[:, :, dr[:, :, None], dc[:, None, :]]
    rx = np.where(ev[None, None, :, :, None], rx, 0.0).astype(np.float32)
    sh["rpbx"] = f(rx.reshape(NE, NAH, 128, NBLK * 64))
    cstart = np.clip(qcol - 8, 0, 48)
    col_in = (kcol[:, None] >= cstart[None, :]) & (kcol[:, None] < cstart[None, :] + 16)
    mF = (ev[:, :, None] & col_in[:, None, :])
    mZ = mF & ((e_idx >= 4) & (e_idx <= 11))[:, :, None]
    sh["nmask"] = f(np.stack([mF, mZ], 0).astype(np.float32).reshape(2, 128, NBLK * 64))
    t = np.arange(c.SEQ)
    row = (t // 64).astype(np.float32)
    colp = (t % 64).astype(np.float32)
    dd = pp % 64
    ii = dd % 32
    inv = (np.float32(10000.0) ** (-(np.arange(16, dtype=np.float32)) / np.float32(16))).astype(np.float32)
    pos = np.where((dd // 32)[:, None] == 0, row[None, :], colp[None, :]).astype(np.float32)
    ang = (pos * inv[ii % 16][:, None]).astype(np.float32)
    sgn = np.where(ii < 16, -1.0, 1.0).astype(np.float32)
    sh["rope"] = f(np.stack([np.cos(ang), np.sin(ang) * sgn[:, None]], 0))
    gg = np.asarray(inp["gla_norm_g"], np.float32)
    sh["gla_gT"] = f(gg.transpose(2, 0, 1).reshape(128, -1))
    sh["pool_scT"] = f(np.stack([_fm(inp["pool_scale"][o], KC) for o in range(c.NO)], 1).reshape(128, -1))
    def icnt(L):
        t = np.arange(L)
        out = np.zeros((4, L), np.float32)
        for wi, w in enumerate((2, 4, 8, 16)):
            lo = np.clip(t - w // 2, 0, L); hi = np.clip(t + w // 2, 0, L)
            out[wi] = 1.0 / (hi - lo).astype(np.float32)
        return out
    sh["icnt_l"] = icnt(c.SEQ)
    sh["icnt_c"] = icnt(c.CTX)
    return sh


def prep_core(cfg, inp, b, zero=False):
    c = cfg
    d = {}
    x = np.asarray(inp["x"][b], np.float32)
    cx = np.asarray(inp["ctx"][b], np.float32)
    cvec = np.stack([_fm(inp["c"][b], c.KC), _fm(inp["c_ctx"], c.KC)], 2).reshape(128, -1)
    if zero:
        d["xT"] = np.zeros((c.D, c.SEQ), np.float32)
        d["cT"] = np.zeros((c.D, c.CTX), np.float32)
        d["cv"] = np.zeros_like(cvec)
    else:
        d["xT"] = np.ascontiguousarray(x.T)
        d["cT"] = np.ascontiguousarray(cx.T)
        d["cv"] = np.ascontiguousarray(cvec)
    return d


def run(cfg, inp, flags=None):
    nc = build(cfg, flags)
    sh = prep_shared(cfg, inp)
    names = set()
    in_maps = []
    ncore = 8
    for k in range(ncore):
        b = (k // 2) % cfg.B
        m = dict(sh)
        m.update(prep_core(cfg, inp, b, zero=(k % 2 == 1 or k // 2 >= cfg.B)))
        in_maps.append(m)
    res = run_bass_kernel_spmd(nc, in_maps, core_ids=list(range(ncore)))
    out = np.stack([np.ascontiguousarray(res.results[2 * b]["outT"].T) for b in range(cfg.B)], 0)
    return out.astype(np.float32)


def kernel(**inputs):
    return run(Cfg(), inputs)
```
